# Optimizing a Trainium2 kernel written in Bass

```python
import math
import jax
import jax.numpy as jnp
from jax import lax
import numpy as np

D_MODEL = 2048
BATCH = 2
SEQ = 4096
DEPTH = 4

N_MIXERS = 2
GRID_W = 64
N_META = 16
CHUNK = 128
N_PAD = CHUNK - N_META
EPS = 1e-6
D_FF = 3 * D_MODEL
SSD_EXPAND = 2
D_INNER = SSD_EXPAND * D_MODEL
SSD_HEAD_DIM = 64
SSD_HEADS = D_INNER // SSD_HEAD_DIM
SSD_GROUPS = 8
SSD_HPG = SSD_HEADS // SSD_GROUPS
SSD_STATE = 128
SSD_CONV = 7
SSD_CONV_CH = D_INNER + 2 * SSD_GROUPS * SSD_STATE
SSD_PROJ = D_INNER + SSD_CONV_CH + 2 * SSD_HEADS
HEAD_DIM = 128
N_HEADS = D_MODEL // HEAD_DIM
N_KV_HEADS = N_HEADS // 2
GQ = N_HEADS // N_KV_HEADS
ROPE_AXIS_DIM = HEAD_DIM // 2
ROPE_HALF = ROPE_AXIS_DIM // 2
ROPE_THETA = 10000.0
Q_BLOCK = 128
QKV_W = (N_HEADS + 2 * N_KV_HEADS) * HEAD_DIM

kernel_name = "hybrid_ssd_gqa_macaron_encoder"


def rms_norm(x, w):
    xf = x.astype(jnp.float32)
    y = xf * lax.rsqrt(jnp.mean(xf * xf, axis=-1, keepdims=True) + EPS)
    return (y * w.astype(jnp.float32)).astype(x.dtype)


def swiglu_ffn(x, norm_w, w_in, w_out):
    h = rms_norm(x, norm_w)
    gate, up = jnp.split(h @ w_in, 2, axis=-1)
    return (jax.nn.silu(gate) * up) @ w_out


def depthwise_conv_centred(u, w, b):
    k = w.shape[0]
    out = lax.conv_general_dilated(
        u, w[:, None, :].astype(u.dtype), window_strides=(1,),
        padding=[((k - 1) // 2, (k - 1) // 2)],
        dimension_numbers=("NWC", "WIO", "NWC"),
        feature_group_count=u.shape[-1])
    return out + b


def ssd_chunked(xdt, a, bm, cm):
    b, L, G, HG, P = xdt.shape
    nc = L // CHUNK
    xdt = xdt.reshape(b, nc, CHUNK, G, HG, P)
    bm = bm.reshape(b, nc, CHUNK, G, SSD_STATE)
    cm = cm.reshape(b, nc, CHUNK, G, SSD_STATE)
    a = a.astype(jnp.float32).reshape(b, nc, CHUNK, G, HG).transpose(0, 3, 4, 1, 2)
    a_cs = jnp.cumsum(a, axis=-1)
    tri = jnp.tril(jnp.ones((CHUNK, CHUNK), dtype=bool))
    seg = a_cs[..., :, None] - a_cs[..., None, :]
    decay_ls = jnp.exp(jnp.where(tri, seg, -jnp.inf))
    cb = jnp.einsum("bclgn,bcsgn->bgcls", cm, bm)
    y_diag = jnp.einsum("bgcls,bghcls,bcsghp->bclghp", cb, decay_ls, xdt)
    decay_to_end = jnp.exp(a_cs[..., -1:] - a_cs)
    states = jnp.einsum("bclgn,bghcl,bclghp->bcghpn", bm, decay_to_end, xdt)
    chunk_decay = jnp.exp(a_cs[..., -1])

    def step(h, inp):
        s_c, d_c = inp
        return d_c[..., None, None] * h + s_c, h

    h0 = jnp.zeros((b, G, HG, P, SSD_STATE), states.dtype)
    _, prev = lax.scan(step, h0, (jnp.moveaxis(states, 1, 0), jnp.moveaxis(chunk_decay, -1, 0)))
    prev = jnp.moveaxis(prev, 0, 1)
    y_off = jnp.einsum("bclgn,bcghpn,bghcl->bclghp", cm, prev, jnp.exp(a_cs))
    return (y_diag + y_off).reshape(b, L, G, HG, P)


def ssd_mixer(h, valid, in_proj, conv_w, conv_b, dt_bias, a_log, d_skip, norm_w, out_proj):
    b, L, _ = h.shape
    proj = h @ in_proj
    z, xbc, dt = jnp.split(proj, [D_INNER, D_INNER + SSD_CONV_CH], axis=-1)
    vmask = valid.astype(h.dtype)[None, :, None]
    xbc = jax.nn.silu(depthwise_conv_centred(xbc * vmask, conv_w, conv_b)) * vmask
    xs, bm, cm = jnp.split(xbc, [D_INNER, D_INNER + SSD_GROUPS * SSD_STATE], axis=-1)
    xs = xs.reshape(b, L, SSD_GROUPS, SSD_HPG, SSD_HEAD_DIM)
    bm = bm.reshape(b, L, SSD_GROUPS, SSD_STATE)
    cm = cm.reshape(b, L, SSD_GROUPS, SSD_STATE)
    dt = jax.nn.softplus(dt.astype(jnp.float32).reshape(b, L, 2, SSD_HEADS)
                         + dt_bias.astype(jnp.float32)) * valid.astype(jnp.float32)[None, :, None, None]
    a_neg = -jnp.exp(a_log.astype(jnp.float32))
    dt_f = dt[:, :, 0].reshape(b, L, SSD_GROUPS, SSD_HPG)
    dt_b = dt[:, :, 1].reshape(b, L, SSD_GROUPS, SSD_HPG)
    a_f = dt_f * a_neg[0].reshape(SSD_GROUPS, SSD_HPG)
    a_b = dt_b * a_neg[1].reshape(SSD_GROUPS, SSD_HPG)
    y_fwd = ssd_chunked(xs * dt_f[..., None], a_f, bm, cm)
    y_bwd = jnp.flip(ssd_chunked(jnp.flip(xs * dt_b[..., None], 1), jnp.flip(a_b, 1),
                                 jnp.flip(bm, 1), jnp.flip(cm, 1)), 1)
    y = y_fwd + y_bwd + d_skip.reshape(SSD_GROUPS, SSD_HPG)[..., None] * xs
    y = y.reshape(b, L, D_INNER).astype(h.dtype)
    g = (y * jax.nn.silu(z)).reshape(b, L, SSD_GROUPS, D_INNER // SSD_GROUPS)
    g = rms_norm(g, norm_w.reshape(SSD_GROUPS, D_INNER // SSD_GROUPS)).reshape(b, L, D_INNER)
    return g @ out_proj


def axial_rope_tables(row, col):
    inv_freq = ROPE_THETA ** (-jnp.arange(0, ROPE_AXIS_DIM, 2, dtype=jnp.float32) / ROPE_AXIS_DIM)
    ang = jnp.stack([row, col], axis=-1).astype(jnp.float32)[..., None] * inv_freq
    return jnp.cos(ang), jnp.sin(ang)


def apply_axial_rope(x, cos, sin):
    xs = x.astype(jnp.float32).reshape(*x.shape[:-1], 2, 2, ROPE_HALF)
    x1, x2 = xs[..., 0, :], xs[..., 1, :]
    c, s = cos[:, None], sin[:, None]
    out = jnp.stack([x1 * c - x2 * s, x2 * c + x1 * s], axis=-2)
    return out.reshape(x.shape).astype(x.dtype)


def attention_mixer(h, cos, sin, valid, w_qkv, q_norm, k_norm, w_o):
    b, L, _ = h.shape
    qkv = h @ w_qkv
    q, k, v = jnp.split(qkv, [N_HEADS * HEAD_DIM, (N_HEADS + N_KV_HEADS) * HEAD_DIM], axis=-1)
    q = apply_axial_rope(rms_norm(q.reshape(b, L, N_HEADS, HEAD_DIM), q_norm), cos, sin)
    k = apply_axial_rope(rms_norm(k.reshape(b, L, N_KV_HEADS, HEAD_DIM), k_norm), cos, sin)
    v = v.reshape(b, L, N_KV_HEADS, HEAD_DIM)
    nb = L // Q_BLOCK
    qb = q.reshape(b, nb, Q_BLOCK, N_KV_HEADS, GQ, HEAD_DIM).transpose(1, 0, 2, 3, 4, 5)
    key_bias = jnp.where(valid, 0.0, -jnp.inf).astype(jnp.float32)
    scale = HEAD_DIM ** -0.5

    def block(q_blk):
        s = jnp.einsum("bqkgd,bskd->bkgqs", q_blk, k).astype(jnp.float32) * scale + key_bias
        p = jax.nn.softmax(s, axis=-1).astype(v.dtype)
        return jnp.einsum("bkgqs,bskd->bqkgd", p, v)

    o = lax.map(block, qb)
    o = o.transpose(1, 0, 2, 3, 4, 5).reshape(b, L, N_HEADS * HEAD_DIM)
    return o @ w_o


def setup_inputs(seed: int = 0) -> dict:
    key = jax.random.key(seed)
    ks = jax.random.split(key, 20)
    f32 = jnp.float32
    n_ssd = (DEPTH + N_MIXERS - 1) // N_MIXERS
    n_attn = DEPTH // N_MIXERS

    def nrm(k, shape, scale):
        return jax.random.normal(k, shape, f32) * scale

    def gain(k, shape):
        return 1.0 + 0.02 * jax.random.normal(k, shape, f32)

    dt0 = jnp.exp(jax.random.uniform(ks[8], (n_ssd, 2, SSD_HEADS), f32,
                                     math.log(1e-3), math.log(1e-1)))
    return {
        "x": jax.random.normal(ks[0], (BATCH, SEQ, D_MODEL), f32),
        "meta_tokens": nrm(ks[1], (N_META, D_MODEL), 1.0),
        "ffn_norm": gain(ks[2], (DEPTH, 2, D_MODEL)),
        "ffn_w_in": nrm(ks[3], (DEPTH, 2, D_MODEL, 2 * D_FF), D_MODEL ** -0.5),
        "ffn_w_out": nrm(ks[4], (DEPTH, 2, D_FF, D_MODEL), D_FF ** -0.5),
        "mix_norm": gain(ks[5], (DEPTH, D_MODEL)),
        "ssd_in_proj": nrm(ks[6], (n_ssd, D_MODEL, SSD_PROJ), D_MODEL ** -0.5),
        "ssd_conv_w": nrm(ks[7], (n_ssd, SSD_CONV, SSD_CONV_CH), SSD_CONV ** -0.5),
        "ssd_conv_b": nrm(ks[9], (n_ssd, SSD_CONV_CH), 0.01),
        "ssd_dt_bias": dt0 + jnp.log(-jnp.expm1(-dt0)),
        "ssd_A_log": jnp.log(jax.random.uniform(ks[10], (n_ssd, 2, SSD_HEADS), f32, 1.0, 16.0)),
        "ssd_D": gain(ks[11], (n_ssd, SSD_HEADS)),
        "ssd_norm": gain(ks[12], (n_ssd, D_INNER)),
        "ssd_out_proj": nrm(ks[13], (n_ssd, D_INNER, D_MODEL), D_INNER ** -0.5),
        "attn_w_qkv": nrm(ks[14], (n_attn, D_MODEL, QKV_W), D_MODEL ** -0.5),
        "attn_q_norm": gain(ks[15], (n_attn, HEAD_DIM)),
        "attn_k_norm": gain(ks[16], (n_attn, HEAD_DIM)),
        "attn_w_o": nrm(ks[17], (n_attn, N_HEADS * HEAD_DIM, D_MODEL), (N_HEADS * HEAD_DIM) ** -0.5),
    }


def reference(x, meta_tokens, ffn_norm, ffn_w_in, ffn_w_out, mix_norm, ssd_in_proj, ssd_conv_w,
              ssd_conv_b, ssd_dt_bias, ssd_A_log, ssd_D, ssd_norm, ssd_out_proj, attn_w_qkv,
              attn_q_norm, attn_k_norm, attn_w_o):
    b, n_tok, _ = x.shape
    rows_n = n_tok // GRID_W
    L = N_PAD + N_META + n_tok
    valid = jnp.arange(L) >= N_PAD
    row = jnp.concatenate([jnp.zeros((N_PAD,), jnp.int32),
                           jnp.full((N_META,), -1, jnp.int32),
                           jnp.repeat(jnp.arange(rows_n, dtype=jnp.int32), GRID_W)])
    col = jnp.concatenate([jnp.zeros((N_PAD,), jnp.int32),
                           jnp.arange(N_META, dtype=jnp.int32),
                           jnp.tile(jnp.arange(GRID_W, dtype=jnp.int32), rows_n)])
    cos, sin = axial_rope_tables(row, col)
    h = jnp.concatenate([jnp.zeros((b, N_PAD, D_MODEL), x.dtype),
                         jnp.broadcast_to(meta_tokens.astype(x.dtype)[None], (b, N_META, D_MODEL)),
                         x], axis=1)
    for i in range(DEPTH):
        j = i // N_MIXERS
        h = h + 0.5 * swiglu_ffn(h, ffn_norm[i, 0], ffn_w_in[i, 0], ffn_w_out[i, 0])
        hn = rms_norm(h, mix_norm[i])
        if i % N_MIXERS == 0:
            h = h + ssd_mixer(hn, valid, ssd_in_proj[j], ssd_conv_w[j], ssd_conv_b[j], ssd_dt_bias[j],
                              ssd_A_log[j], ssd_D[j], ssd_norm[j], ssd_out_proj[j])
        else:
            h = h + attention_mixer(hn, cos, sin, valid, attn_w_qkv[j], attn_q_norm[j],
                                    attn_k_norm[j], attn_w_o[j])
        h = h + 0.5 * swiglu_ffn(h, ffn_norm[i, 1], ffn_w_in[i, 1], ffn_w_out[i, 1])
    return h[:, N_PAD + N_META:, :]
```

```python
import contextlib
import math
import numpy as np
import ml_dtypes
import concourse.bass as bass
import concourse.mybir as mybir
from concourse.bass_utils import run_bass_kernel_spmd

F32 = mybir.dt.float32
BF16 = mybir.dt.bfloat16
AF = mybir.ActivationFunctionType
ALU = mybir.AluOpType
AX = mybir.AxisListType
NPBF = ml_dtypes.bfloat16

D = 2048
KC = 16
TT = 1040
NREAL = 1024
NMETA = 16
TBS = [(0, 512), (512, 512), (1024, 16)]
TTILES = [(i * 128, 128) for i in range(8)] + [(1024, 16)]
DFF = 6144
EPS = 1e-6
SEQ = 4096
SV = SEQ + NMETA
SP_ = SEQ + 128
NCH = 33


class _Op:
    __slots__ = ("eng", "fn", "deps", "ms", "dkey", "sem", "val", "clock", "idx", "inc", "is_cc")


class Prog:
    ENGS = ("pe", "act", "dve", "pool", "sp")

    def __init__(self):
        self.nc = bass.Bass("TRN2", target_bir_lowering=False)
        self.es = contextlib.ExitStack()
        self.phase = None
        self.phase_id = 0
        nc = self.nc
        self.e = {"pe": nc.tensor, "act": nc.scalar, "dve": nc.vector, "pool": nc.gpsimd, "sp": nc.sync}
        self.ops = []
        self.nops = 0
        self.lastw = {}
        self.readers = {}
        self.out_dmas = []
        self.last_on = {}
        self.dma_since = []
        self.sem = {e: self.es.enter_context(nc.semaphore("s_" + e)) for e in ("pe", "act", "dve", "pool")}
        self.cnt = {e: 0 for e in self.sem}
        self.dsem, self.dcnt = {}, {}
        self.known = {e: {} for e in self.ENGS}
        self.bar = self.es.enter_context(nc.sbuf_tensor("bar", [128, 8], F32))

    def dram(self, name, shape, dt, kind):
        return self.nc.dram_tensor(name, list(shape), dt, kind=kind).ap()

    def idram(self, name, shape, dt):
        return self.nc.dram_tensor(name, list(shape), dt).ap()

    def _stack(self):
        return self.phase if self.phase is not None else self.es

    def sb(self, name, shape, dt):
        return self._stack().enter_context(self.nc.sbuf_tensor("%s_p%d" % (name, self.phase_id), list(shape), dt))

    def ps(self, name, shape, dt=F32):
        return self._stack().enter_context(self.nc.psum_tensor("%s_p%d" % (name, self.phase_id), list(shape), dt))

    def begin_phase(self):
        self.phase = contextlib.ExitStack()
        self.phase_id += 1

    def end_phase(self, wait_cc=False):
        self.barrier(wait_cc=wait_cc)
        self.flush()
        self.phase.close()
        self.phase = None

    def op(self, eng, fn, r=(), w=(), dkey=None, inc=16, extra=()):
        o = _Op()
        o.inc = inc
        o.is_cc = False
        o.eng, o.fn, o.dkey = eng, fn, dkey
        o.ms = dkey is not None
        o.idx = self.nops
        self.nops += 1
        o.sem = o.val = o.clock = None
        deps = {}
        for d in extra:
            deps[d.idx] = d
        for k in r:
            lw = self.lastw.get(k)
            if lw is not None:
                deps[lw.idx] = lw
        for k in w:
            lw = self.lastw.get(k)
            if lw is not None:
                deps[lw.idx] = lw
            rd = self.readers.get(k)
            if rd:
                for x in rd.values():
                    if isinstance(x, list):
                        for y in x:
                            deps[y.idx] = y
                    else:
                        deps[x.idx] = x
        dl = []
        for i in sorted(deps, reverse=True):
            d = deps[i]
            if d.eng == "pe" and eng == "pe" and d.dkey is None and dkey is None:
                continue
            d.ms = True
            dl.append(d)
        o.deps = dl
        for k in w:
            self.lastw[k] = o
            self.readers[k] = {}
        for k in r:
            rd = self.readers.setdefault(k, {})
            if dkey is not None:
                rd.setdefault("dma", []).append(o)
            else:
                rd[eng] = o
        self.ops.append(o)
        if fn is not None:
            if dkey is not None:
                self.dma_since.append(o)
            else:
                self.last_on[eng] = o
        return o

    def dma(self, q, out, in_, r=(), w=(), key=None, is_out=False):
        fn = (lambda E=self.e[q], out=out, in_=in_: E.dma_start(out=out, in_=in_))
        o = self.op(q, fn, r=r, w=w, dkey=key)
        if is_out:
            self.out_dmas.append(o)
        return o

    def allgather(self, src, dst, r, w, key):
        nc = self.nc
        fn = (lambda: nc.gpsimd.collective_compute("AllGather", ALU.bypass, replica_groups=[[0, 1, 2, 3], [4, 5, 6, 7]],
                                                   ins=[src.opt()], outs=[dst.opt()]))
        o = self.op("pool", fn, r=r, w=w, dkey=key, inc=1)
        o.is_cc = True
        return o

    def barrier(self, wait_cc=False):
        nc = self.nc
        pend_cc = [] if wait_cc else [d for d in self.dma_since if d.is_cc]
        deps = [self.last_on[e] for e in ("pe", "act", "dve", "pool") if e in self.last_on] + \
               [d for d in self.dma_since if wait_cc or not d.is_cc]
        b = self.op("dve", lambda: nc.vector.memset(self.bar[:], 0.0), extra=deps)
        b.ms = True
        for e in ("pe", "act", "pool", "sp"):
            self.op(e, None, extra=[b])
        keep = {k: o for k, o in self.lastw.items() if o.is_cc and o in pend_cc}
        self.lastw.clear()
        self.readers.clear()
        self.lastw.update(keep)
        self.dma_since = list(pend_cc)

    def _handle(self, s):
        return self.sem[s] if s in self.sem else self.dsem[s]

    def _wait(self, engname, d):
        kn = self.known[engname]
        if kn.get(d.sem, 0) >= d.val:
            return
        self.e[engname].wait_ge(self._handle(d.sem), d.val)
        kn = dict(kn)
        for ks, kv in d.clock.items():
            if kn.get(ks, 0) < kv:
                kn[ks] = kv
        self.known[engname] = kn

    def flush(self):
        nc = self.nc
        for o in self.ops:
            for d in o.deps:
                self._wait(o.eng, d)
            if o.fn is None:
                continue
            ins = o.fn()
            if o.ms:
                if o.dkey is not None:
                    k = ("d", o.dkey)
                    if k not in self.dsem:
                        self.dsem[k] = self.es.enter_context(nc.semaphore("sd%d" % len(self.dsem)))
                        self.dcnt[k] = 0
                    self.dcnt[k] += o.inc
                    ins.then_inc(self.dsem[k], o.inc)
                    o.sem, o.val = k, self.dcnt[k]
                else:
                    self.cnt[o.eng] += 1
                    ins.then_inc(self.sem[o.eng], 1)
                    o.sem, o.val = o.eng, self.cnt[o.eng]
                c = dict(self.known[o.eng])
                c[o.sem] = o.val
                o.clock = c
        self.ops = []

    def emit(self):
        self.flush()
        for o in self.out_dmas:
            self._wait("sp", o)
        return self.nc


def _xbc_order():
    o = []
    for par in range(2):
        for G_ in range(par, 8, 2):
            o += [G_ * 4 + i for i in range(4)]
        o += [32 + G_ for G_ in range(par, 8, 2)]
        o += [40 + G_ for G_ in range(par, 8, 2)]
    return o


XBC_ORDER = _xbc_order()


class FeatX:
    def __init__(self, P, name, nchunks, order=None):
        self.P, self.name = P, name
        n = nchunks // 3
        self.send = [P.idram("s_%s%d" % (name, i), [384, TT], BF16) for i in range(n)]
        self.recv = [P.idram("r_%s%d" % (name, i), [1536, TT], BF16) for i in range(n)]
        self.keys = {i: [] for i in range(n)}
        order = list(range(nchunks)) if order is None else order
        self.pos = {f: i for i, f in enumerate(order)}

    def write(self, f, src, r, dkey):
        P = self.P
        i, j = self.pos[f] // 3, self.pos[f] % 3
        sk = ("snd", self.name, f)
        self.keys[i].append(sk)
        P.dma("sp", self.send[i][j * 128:(j + 1) * 128, :], src, r=r, w=[sk], key=dkey)
        if len(self.keys[i]) == 3:
            P.allgather(self.send[i], self.recv[i], r=self.keys[i], w=[("rcv", self.name, i)], key="cc_" + self.name)
            self.keys[i] = []

    def rd(self, q, f):
        i, j = self.pos[f] // 3, self.pos[f] % 3
        return self.recv[i][q * 384 + j * 128:q * 384 + (j + 1) * 128, :], ("rcv", self.name, i)


class TokX:
    RP = (512, 528)

    def __init__(self, P, name, ncb, width, dt):
        self.P, self.name = P, name
        self.send = [[P.idram("s_%s%d_%d" % (name, cb, pt), [self.RP[pt], width], dt) for pt in range(2)] for cb in range(ncb)]
        self.recv = [[P.idram("r_%s%d_%d" % (name, cb, pt), [4 * self.RP[pt], width], dt) for pt in range(2)] for cb in range(ncb)]
        self.keys = {(cb, pt): [] for cb in range(ncb) for pt in range(2)}

    @staticmethod
    def loc(ti):
        if ti < 4:
            return 0, ti * 128
        if ti < 8:
            return 1, (ti - 4) * 128
        return 1, 512

    def write(self, ti, cb, tn, src, r, dkey):
        P = self.P
        pt, r0 = self.loc(ti)
        sk = ("snd", self.name, cb, ti)
        self.keys[(cb, pt)].append(sk)
        P.dma("sp", self.send[cb][pt][r0:r0 + tn, :], src, r=r, w=[sk], key=dkey)
        if len(self.keys[(cb, pt)]) == (4 if pt == 0 else 5):
            P.allgather(self.send[cb][pt], self.recv[cb][pt], r=self.keys[(cb, pt)], w=[("rcv", self.name, cb, pt)],
                        key="cc_" + self.name)
            self.keys[(cb, pt)] = []

    def rd(self, q, ti, cb, tn=128):
        pt, r0 = self.loc(ti)
        R = self.RP[pt]
        return self.recv[cb][pt][q * R + r0:q * R + r0 + tn, :], ("rcv", self.name, cb, pt)

    def rd4(self, q, pt, cb):
        R = self.RP[pt]
        return self.recv[cb][pt][q * R:q * R + 512, :], ("rcv", self.name, cb, pt)


class GX:
    def __init__(self, P):
        self.P = P
        w = [128] + [512] * 8
        self.send = [P.idram("s_g%d" % i, [1024, w[i]], BF16) for i in range(9)]
        self.recv = [P.idram("r_g%d" % i, [4096, w[i]], BF16) for i in range(9)]
        self.keys = {i: [] for i in range(9)}

    def write(self, g, c, src, r, dkey):
        P = self.P
        i = 0 if c == 0 else 1 + (c - 1) // 4
        c0 = 0 if c == 0 else ((c - 1) % 4) * 128
        sk = ("snd_g", g, c)
        self.keys[i].append(sk)
        P.dma("sp", self.send[i][g * 512:(g + 1) * 512, c0:c0 + 128].rearrange("(i p) t -> p i t", p=128), src,
              r=r, w=[sk], key=dkey)
        if len(self.keys[i]) == (2 if i == 0 else 8):
            P.allgather(self.send[i], self.recv[i], r=self.keys[i], w=[("rcv_g", i)], key="cc_g")
            self.keys[i] = []

    def rd(self, r_, jj, k, hf):
        i = 1 + 2 * k + hf
        return self.recv[i][r_ * 1024 + jj * 128:r_ * 1024 + (jj + 1) * 128, :], ("rcv_g", i)

    def rd_meta(self, r_, jj):
        return self.recv[0][r_ * 1024 + jj * 128:r_ * 1024 + (jj + 1) * 128, 112:128], ("rcv_g", 0)


class OX:
    def __init__(self, P):
        self.P = P
        w = [1024] * 4 + [16]
        self.send = [P.idram("s_o%d" % i, [512, w[i]], BF16) for i in range(5)]
        self.recv = [P.idram("r_o%d" % i, [2048, w[i]], BF16) for i in range(5)]
        self.keys = {i: [] for i in range(5)}

    def write(self, j, qb, qn, src, r, dkey):
        P = self.P
        i = qb // 2 if qb < 8 else 4
        c0 = (qb % 2) * 512 if qb < 8 else 0
        sk = ("snd_o", j, qb)
        self.keys[i].append(sk)
        P.dma("sp", self.send[i][j * 128:(j + 1) * 128, c0:c0 + qn], src, r=r, w=[sk], key=dkey)
        if len(self.keys[i]) == (8 if i < 4 else 4):
            P.allgather(self.send[i], self.recv[i], r=self.keys[i], w=[("rcv_o", i)], key="cc_o")
            self.keys[i] = []

    def rd(self, r_, jj, k):
        return self.recv[k][r_ * 512 + jj * 128:r_ * 512 + (jj + 1) * 128, :], ("rcv_o", k)

    def rd_meta(self, r_, jj):
        return self.recv[4][r_ * 512 + jj * 128:r_ * 512 + (jj + 1) * 128, :], ("rcv_o", 4)


class Sel:
    def __init__(self, P, G, stg, stg_keys, get_ps):
        nc = P.nc
        self.P, self.stg, self.stg_keys, self.get_ps = P, stg, stg_keys, get_ps
        self.sel = P.sb("sel_sb", [128, 4], F32)
        idf = P.sb("sel_idf", [128, 128], F32)
        self.selI = P.sb("selI", [128, 4, 128], BF16)
        P.dma("sp", self.sel[:], G.sel_in, w=["sel"], key="sel")
        P.dma("sp", idf[:], G.ident_in, w=["sel_idf"], key="sel_idf")
        for k in range(4):
            P.op("dve", lambda k=k: nc.vector.tensor_scalar(out=self.selI[:, k, :], in0=idf[:],
                                                            scalar1=self.sel[:, k:k + 1], scalar2=None, op0=ALU.mult),
                 r=["sel", "sel_idf"], w=["selI"])
        for sl in range(2):
            for k in range(4):
                P.op("dve", lambda sl=sl, k=k: nc.vector.memset(self.stg[sl][:, k, :], 0.0), w=self.stg_keys(sl, k))
        self.i = 0

    def load(self, dst, cands, ncols, r, w, view=None, prow=(0, 128)):
        P = self.P
        nc = P.nc
        slot = self.i % 2
        self.i += 1
        p0, p1 = prow
        for k in range(4):
            tgt = self.stg[slot][p0:p1, k, 0:ncols]
            if view is not None:
                tgt = view(tgt)
            P.dma("sp", tgt, cands[k], r=r, w=self.stg_keys(slot, k), key=("selstg", slot, k))
        for t0 in range(0, ncols, 512):
            tn = min(512, ncols - t0)
            ps, pkey = self.get_ps()

            def mm(ps=ps, slot=slot, t0=t0, tn=tn):
                ins = None
                for k in range(4):
                    ins = nc.tensor.matmul(ps[:, 0:tn], lhsT=self.selI[:, k, :], rhs=self.stg[slot][:, k, t0:t0 + tn],
                                           start=(k == 0), stop=(k == 3))
                return ins
            P.op("pe", mm, r=["selI"] + [kk for k in range(4) for kk in self.stg_keys(slot, k)], w=[pkey])
            P.op("act", lambda ps=ps, t0=t0, tn=tn: nc.scalar.copy(out=dst[:, t0:t0 + tn], in_=ps[:, 0:tn]),
                 r=[pkey], w=w)


class TPhase:
    def __init__(self, P, G, pre, ffns, post, h_src, h_dst, h_dst_is_out, mixj):
        self.P = P
        self.G = G
        self.pre, self.post = pre, post
        self.n_ffn = len(ffns)
        self.h_in = h_src
        self.h_out = h_dst
        self.h_dst_is_out = h_dst_is_out
        self.w_in = [G.ffn_w_in[l, sl] for (l, sl) in ffns]
        self.w_out = [G.ffn_w_out[l, sl] for (l, sl) in ffns]
        self.mixj = mixj
        self.h = P.sb("h", [128, KC, TT], F32)
        self.big = P.sb("big", [128, 32, TT], BF16)
        self.wb = [P.sb("wb%d" % i, [128, KC, 512], BF16) for i in range(3)]
        self.wslot = 0
        self.gam = P.sb("gam_sb", [128, 12, KC], F32)
        self.rstd = P.sb("rstd", [128, TT], F32)
        self.sq = [P.sb("sq%d" % i, [128, 512], BF16) for i in range(2)]
        self.sg = [P.sb("sg%d" % i, [128, 512], F32) for i in range(2)]
        self.qf = self.sg
        self.ones = P.sb("ones", [128, 128], BF16)
        self.pmm = [P.ps("pmm%d" % i, [128, 512]) for i in range(6)]
        self.pmisc = [P.ps("pmisc%d" % i, [128, 512]) for i in range(2)]
        self.pi = 0
        self.npmm = 6
        self.mi = 0
        self.si = 0
        nc = P.nc
        P.op("dve", lambda: nc.vector.memset(self.ones[:], 1.0), w=["ones"])
        P.dma("sp", self.gam[:], G.gam_in, w=["gam"], key="gam")
        P.dma("sp", self.h[:], self.h_in, w=[("h", kc, tb) for kc in range(KC) for tb in range(3)], key="h")

    def next_w(self):
        s = self.wslot
        self.wslot = (s + 1) % 3
        return s

    def next_p(self):
        i = self.pi % self.npmm
        self.pi = (i + 1) % self.npmm
        return i

    def next_m(self):
        i = self.mi
        self.mi = (i + 1) % 2
        return i

    def load_w(self, slot, W, r0, c0, ncols, col_off=0, nk=KC):
        P = self.P
        src = W[r0:r0 + nk * 128, c0:c0 + ncols].rearrange("(kc p) f -> p kc f", p=128)
        P.dma("pool", self.wb[slot][:, 0:nk, col_off:col_off + ncols], src,
              w=[("wb", slot, col_off // 256 + i) for i in range(max(1, ncols // 256))], key=("wb", slot, col_off // 256))

    def mm_group(self, ps_ap, pkey, terms, extra_r=()):
        P = self.P
        nc = P.nc
        n = len(terms)

        def fn():
            ins = None
            for i, (l, rr) in enumerate(terms):
                ins = nc.tensor.matmul(ps_ap, lhsT=l, rhs=rr, start=(i == 0), stop=(i == n - 1))
            return ins
        return P.op("pe", fn, r=list(extra_r), w=[pkey])

    def rmsnorm(self, gi, dst_kc0=0):
        P = self.P
        nc = P.nc
        for tb, (t0, tn) in enumerate(TBS):
            m = self.next_m()
            pm = self.pmisc[m]
            for kc in range(KC):
                s = self.si
                self.si = (s + 1) % 2
                P.op("act", lambda s=s, kc=kc, t0=t0, tn=tn: nc.scalar.activation(
                    out=self.sq[s][:, 0:tn], in_=self.h[:, kc, t0:t0 + tn], func=AF.Square),
                    r=[("h", kc, tb)], w=[("sq", s)])
                P.op("pe", lambda s=s, kc=kc, tn=tn, pm=pm: nc.tensor.matmul(
                    pm[:, 0:tn], lhsT=self.ones[:, :], rhs=self.sq[s][:, 0:tn], start=(kc == 0), stop=(kc == KC - 1)),
                    r=[("sq", s), "ones"], w=[("pmisc", m)])
            P.op("dve", lambda t0=t0, tn=tn, pm=pm: nc.vector.tensor_scalar(
                out=self.rstd[:, t0:t0 + tn], in0=pm[:, 0:tn], scalar1=1.0 / D, scalar2=EPS,
                op0=ALU.mult, op1=ALU.add), r=[("pmisc", m)], w=[("rstd", tb)])
            P.op("act", lambda t0=t0, tn=tn: nc.scalar.sqrt(
                out=self.rstd[:, t0:t0 + tn], in_=self.rstd[:, t0:t0 + tn]),
                r=[("rstd", tb)], w=[("rstd", tb)])
            P.op("dve", lambda t0=t0, tn=tn: nc.vector.reciprocal(
                out=self.rstd[:, t0:t0 + tn], in_=self.rstd[:, t0:t0 + tn]),
                r=[("rstd", tb)], w=[("rstd", tb)])
            for kc in range(KC):
                P.op("dve", lambda kc=kc, t0=t0, tn=tn: nc.vector.scalar_tensor_tensor(
                    out=self.big[:, dst_kc0 + kc, t0:t0 + tn], in0=self.h[:, kc, t0:t0 + tn],
                    scalar=self.gam[:, gi, kc:kc + 1], in1=self.rstd[:, t0:t0 + tn],
                    op0=ALU.mult, op1=ALU.mult),
                    r=[("h", kc, tb), ("rstd", tb), "gam"], w=[("big", dst_kc0 + kc, tb)])

    def ffn(self, fi, gi):
        P = self.P
        nc = P.nc
        W1, W2 = self.w_in[fi], self.w_out[fi]
        self.rmsnorm(gi)
        for j in range(3):
            for fg in range(8):
                slot = self.next_w()
                c0 = j * 2048 + fg * 256
                self.load_w(slot, W1, 0, c0, 256, 0)
                self.load_w(slot, W1, 0, DFF + c0, 256, 256)
                for f2 in range(2):
                    fc = fg * 2 + f2
                    for tb, (t0, tn) in enumerate(TBS):
                        pg, pu = self.next_p(), self.next_p()
                        self.mm_group(self.pmm[pg][:, 0:tn], ("pmm", pg),
                                      [(self.wb[slot][:, kc, f2 * 128:(f2 + 1) * 128], self.big[:, kc, t0:t0 + tn])
                                       for kc in range(KC)],
                                      extra_r=[("wb", slot, 0)] + [("big", kc, tb) for kc in range(KC)])
                        self.mm_group(self.pmm[pu][:, 0:tn], ("pmm", pu),
                                      [(self.wb[slot][:, kc, 256 + f2 * 128:256 + (f2 + 1) * 128],
                                        self.big[:, kc, t0:t0 + tn]) for kc in range(KC)],
                                      extra_r=[("wb", slot, 1)] + [("big", kc, tb) for kc in range(KC)])
                        s = self.si
                        self.si = (s + 1) % 2
                        P.op("act", lambda s=s, pg=pg, tn=tn: nc.scalar.activation(
                            out=self.sg[s][:, 0:tn], in_=self.pmm[pg][:, 0:tn], func=AF.Silu),
                            r=[("pmm", pg)], w=[("sg", s)])
                        P.op("dve", lambda s=s, pu=pu, fc=fc, t0=t0, tn=tn: nc.vector.tensor_tensor(
                            out=self.big[:, 16 + fc, t0:t0 + tn], in0=self.sg[s][:, 0:tn],
                            in1=self.pmm[pu][:, 0:tn], op=ALU.mult),
                            r=[("sg", s), ("pmm", pu)], w=[("big", 16 + fc, tb)])
            for dg in range(4):
                slot = self.next_w()
                self.load_w(slot, W2, j * 2048, dg * 512, 512, 0)
                for di in range(4):
                    dc = dg * 4 + di
                    for tb, (t0, tn) in enumerate(TBS):
                        p = self.next_p()
                        self.mm_group(self.pmm[p][:, 0:tn], ("pmm", p),
                                      [(self.wb[slot][:, kc, di * 128:(di + 1) * 128],
                                        self.big[:, 16 + kc, t0:t0 + tn]) for kc in range(KC)],
                                      extra_r=[("wb", slot, 0), ("wb", slot, 1)] +
                                      [("big", 16 + kc, tb) for kc in range(KC)])
                        P.op("dve", lambda p=p, dc=dc, t0=t0, tn=tn: nc.vector.scalar_tensor_tensor(
                            out=self.h[:, dc, t0:t0 + tn], in0=self.pmm[p][:, 0:tn], scalar=0.5,
                            in1=self.h[:, dc, t0:t0 + tn], op0=ALU.mult, op1=ALU.add),
                            r=[("pmm", p), ("h", dc, tb)], w=[("h", dc, tb)])

    def store_h(self):
        P = self.P
        P.dma("sp", self.h_out, self.h[:], r=[("h", kc, tb) for kc in range(KC) for tb in range(3)],
              w=["h_dram"], key="hout", is_out=self.h_dst_is_out)

    def tok_major_proj(self, W, c0, ncols, out_dram, out_c0, dt_out, stg_name):
        P = self.P
        nc = P.nc
        slot = self.next_w()
        if ncols >= 256:
            self.load_w(slot, W, 0, c0, ncols, 0)
            wres = [("wb", slot, i) for i in range(ncols // 256)]
        else:
            self.load_w(slot, W, 0, c0, ncols, 0)
            wres = [("wb", slot, 0)]
        stg = self.tstg[stg_name]
        skey = (lambda s_: ("big", 18 + s_, 0)) if stg_name == "tstg" else (lambda s_: ("dstg", s_))
        for ti, (t0, tn) in enumerate(TTILES):
            tb = min(ti // 4, 2)
            p = self.next_p()
            self.mm_group(self.pmm[p][0:tn, 0:ncols], ("pmm", p),
                          [(self.big[:, kc, t0:t0 + tn], self.wb[slot][:, kc, 0:ncols]) for kc in range(KC)],
                          extra_r=wres + [("big", kc, tb) for kc in range(KC)])
            s = self.tsi
            self.tsi = (s + 1) % 2
            P.op("act", lambda s=s, p=p, tn=tn, stg=stg: nc.scalar.copy(
                out=stg[s][0:tn, 0:ncols], in_=self.pmm[p][0:tn, 0:ncols]),
                r=[("pmm", p)], w=[skey(s)])
            out_dram.write(ti, out_c0 // 512, tn, stg[s][0:tn, 0:ncols], r=[skey(s)], dkey=(stg_name, s))

    def post_attn(self, gi):
        P = self.P
        nc = P.nc
        W = self.w_mix_in
        self.rmsnorm(gi)
        units = []
        for fg in range(6):
            for fi in range(4):
                f = fg * 4 + fi
                for tb, (t0, tn) in enumerate(TBS):
                    units.append(dict(fg=fg, fi=fi, f=f, gcol=0 if f < 16 else 1, st=f % 2, tb=tb, t0=t0, tn=tn))
        slots = {}
        qf3 = self.qf + [self.qf2]

        def stage_a(i):
            u = units[i]
            fg, fi, tb, t0, tn = u["fg"], u["fi"], u["tb"], u["t0"], u["tn"]
            for fg_ in (fg, fg + 1):
                if fg_ < 6 and fg_ not in slots:
                    slots[fg_] = self.next_w()
                    self.load_w(slots[fg_], W, 0, fg_ * 512, 512, 0)
            slot = slots[fg]
            p = self.next_p()
            self.mm_group(self.pmm[p][:, 0:tn], ("pmm", p),
                          [(self.wb[slot][:, kc, fi * 128:(fi + 1) * 128], self.big[:, kc, t0:t0 + tn])
                           for kc in range(KC)],
                          extra_r=[("wb", slot, 0), ("wb", slot, 1)] + [("big", kc, tb) for kc in range(KC)])
            s2, s3 = i % 2, i % 3
            P.op("act", lambda: nc.scalar.copy(out=qf3[s3][:, 0:tn], in_=self.pmm[p][:, 0:tn]),
                 r=[("pmm", p)], w=[("sg", s3)])
            P.op("act", lambda: nc.scalar.activation(out=self.sq[s2][:, 0:tn], in_=self.pmm[p][:, 0:tn], func=AF.Square),
                 r=[("pmm", p)], w=[("sq", s2)])

        def stage_b(i):
            u = units[i]
            tn, gcol = u["tn"], u["gcol"]
            s2, s3 = i % 2, i % 3
            m = 0
            pm = self.pmisc[m]
            P.op("pe", lambda: nc.tensor.matmul(pm[:, 0:tn], lhsT=self.ones[:, :], rhs=self.sq[s2][:, 0:tn],
                                                start=True, stop=True),
                 r=[("sq", s2), "ones"], w=[("pmisc", m)])
            P.op("dve", lambda: nc.vector.tensor_scalar(out=self.qr[s2][:, 0:tn], in0=pm[:, 0:tn], scalar1=1.0 / 128,
                                                        scalar2=EPS, op0=ALU.mult, op1=ALU.add),
                 r=[("pmisc", m)], w=[("rstd", s2)])
            P.op("act", lambda: nc.scalar.sqrt(out=self.qr[s2][:, 0:tn], in_=self.qr[s2][:, 0:tn]),
                 r=[("rstd", s2)], w=[("rstd", s2)])
            P.op("dve", lambda: nc.vector.reciprocal(out=self.qr[s2][:, 0:tn], in_=self.qr[s2][:, 0:tn]),
                 r=[("rstd", s2)], w=[("rstd", s2)])
            P.op("dve", lambda: nc.vector.scalar_tensor_tensor(
                out=self.qn[s2][:, 0:tn], in0=qf3[s3][:, 0:tn], scalar=self.qkg[:, gcol:gcol + 1],
                in1=self.qr[s2][:, 0:tn], op0=ALU.mult, op1=ALU.mult),
                r=[("sg", s3), ("rstd", s2), "qkg"], w=[("qn", s2)])

        def stage_c(i):
            u = units[i]
            f, st, tb, t0, tn = u["f"], u["st"], u["tb"], u["t0"], u["tn"]
            s2, s3 = i % 2, i % 3
            m2 = 1
            pm2 = self.pmisc[m2]
            P.op("pe", lambda: nc.tensor.matmul(pm2[:, 0:tn], lhsT=self.rot[:, :], rhs=self.qn[s2][:, 0:tn],
                                                start=True, stop=True),
                 r=[("qn", s2), "rot"], w=[("pmisc", m2)])
            P.op("dve", lambda: nc.vector.tensor_tensor(out=qf3[s3][:, 0:tn], in0=self.qn[s2][:, 0:tn],
                                                        in1=self.cos[:, t0:t0 + tn], op=ALU.mult),
                 r=[("qn", s2), "cos"], w=[("sg", s3)])
            P.op("dve", lambda: nc.vector.tensor_tensor(out=self.qn[s2][:, 0:tn], in0=pm2[:, 0:tn],
                                                        in1=self.sin[:, t0:t0 + tn], op=ALU.mult),
                 r=[("pmisc", m2), "sin"], w=[("qn", s2)])
            P.op("dve", lambda: nc.vector.tensor_tensor(out=self.fstg[st][:, t0:t0 + tn], in0=qf3[s3][:, 0:tn],
                                                        in1=self.qn[s2][:, 0:tn], op=ALU.add),
                 r=[("sg", s3), ("qn", s2)], w=[("big", 16 + st, tb)])
            if tb == 2:
                self.mix_out_f.write(f, self.fstg[st][:, :], r=[("big", 16 + st, tb_) for tb_ in range(3)],
                                     dkey=("fstg", st))

        n = len(units)
        for i in range(n + 2):
            if i < n:
                stage_a(i)
            if 1 <= i <= n:
                stage_b(i - 1)
            if i >= 2:
                stage_c(i - 2)
        for vg in range(2):
            self.tok_major_proj(W, 3072 + vg * 512, 512, self.mix_out_t, vg * 512, BF16, "tstg")

    def post_ssd(self, gi):
        P = self.P
        nc = P.nc
        W = self.w_mix_in
        self.rmsnorm(gi)
        self.tok_major_proj(W, 10240, 128, self.mix_out_dt, 0, F32, "dstg")
        segs = []
        for f in XBC_ORDER:
            if not segs or segs[-1][0] != f // 4:
                segs.append([f // 4, []])
            segs[-1][1].append(f)
        seg_slot = {}

        def want(si):
            if si < len(segs) and si not in seg_slot:
                seg_slot[si] = self.next_w()
                self.load_w(seg_slot[si], W, 0, 4096 + segs[si][0] * 512, 512, 0)
        seg_of = {}
        for si, (fg_, fl) in enumerate(segs):
            for f in fl:
                seg_of[f] = si
        want(0)
        for fpos, f in enumerate(XBC_ORDER):
            fg, fi = f // 4, f % 4
            si = seg_of[f]
            if f == segs[si][1][0]:
                want(si + 1)
            slot = seg_slot[si]
            st = fpos % 2
            for tb, (t0, tn) in enumerate(TBS):
                p = self.next_p()
                self.mm_group(self.pmm[p][:, 0:tn], ("pmm", p),
                              [(self.wb[slot][:, kc, fi * 128:(fi + 1) * 128], self.big[:, kc, t0:t0 + tn])
                               for kc in range(KC)],
                              extra_r=[("wb", slot, 0), ("wb", slot, 1)] + [("big", kc, tb) for kc in range(KC)])
                P.op("act", lambda p=p, st=st, t0=t0, tn=tn: nc.scalar.copy(
                    out=self.fstg[st][:, t0:t0 + tn], in_=self.pmm[p][:, 0:tn]),
                    r=[("pmm", p)], w=[("big", 16 + st, tb)])
            self.mix_out_f.write(f, self.fstg[st][:, :], r=[("big", 16 + st, tb) for tb in range(3)], dkey=("fstg", st))
        for fg in range(8):
            slot = self.next_w()
            self.load_w(slot, W, 0, fg * 512, 512, 0)
            for fi in range(4):
                f = fg * 4 + fi
                st = f % 2
                for tb, (t0, tn) in enumerate(TBS):
                    p = self.next_p()
                    self.mm_group(self.pmm[p][:, 0:tn], ("pmm", p),
                                  [(self.wb[slot][:, kc, fi * 128:(fi + 1) * 128], self.big[:, kc, t0:t0 + tn])
                                   for kc in range(KC)],
                                  extra_r=[("wb", slot, 0), ("wb", slot, 1)] + [("big", kc, tb) for kc in range(KC)])
                    P.op("act", lambda p=p, st=st, t0=t0, tn=tn: nc.scalar.activation(
                        out=self.fstg[st][:, t0:t0 + tn], in_=self.pmm[p][:, 0:tn], func=AF.Silu),
                        r=[("pmm", p)], w=[("big", 16 + st, tb)])
                P.dma("sp", self.G.zpark[f], self.fstg[st][:, :], r=[("big", 16 + st, tb) for tb in range(3)],
                      w=[("zpark", f)], key=("fstg", st))

    def pre_mix(self, nkc):
        P = self.P
        nc = P.nc
        G = self.G
        W = self.w_mix_out
        if self.pre == "ssd":
            self.nwT = P.sb("nwT", [128, 32], F32)
            P.dma("sp", self.nwT[:], G.nwT_in[self.mixj[0]], w=["nwT"], key="nwT")
            self.pgn = [self.pmisc[0], self.pmisc[1], self.pmm[5]]
            self.npmm = 5
        stg = [self.big[:, 16 + 4 * i:20 + 4 * i, 0:1024] for i in range(2)]
        sel = Sel(P, G, stg, lambda slot, k: [("big", 16 + 4 * slot + k, 0), ("big", 16 + 4 * slot + k, 1)],
                  lambda: (lambda p: (self.pmm[p], ("pmm", p)))(self.next_p()))
        pgk = [("pmisc", 0), ("pmisc", 1), ("pmm", 5)]
        for hh in range(nkc // KC):
            for kk in range(KC):
                kc = hh * KC + kk
                if self.pre == "attn":
                    r_, jj = kc // 4, kc % 4
                    cc = [G.x_o.rd(r_, jj, k) for k in range(4)]
                    sel.load(self.big[:, kk, 0:1024], [c_[0] for c_ in cc], 1024, r=[c_[1] for c_ in cc],
                             w=[("big", kk, 0), ("big", kk, 1)])
                    meta, mk = G.x_o.rd_meta(r_, jj)
                else:
                    r_, jj = kc // 8, kc % 8
                    for hf in range(2):
                        cc = [G.x_g.rd(r_, jj, k, hf) for k in range(4)]
                        sel.load(self.big[:, kk, hf * 512:(hf + 1) * 512], [c_[0] for c_ in cc], 512,
                                 r=[c_[1] for c_ in cc], w=[("big", kk, hf)])
                    meta, mk = G.x_g.rd_meta(r_, jj)
                P.dma("sp", self.big[:, kk, 1024:1040], meta, r=[mk], w=[("big", kk, 2)], key=("mixin_m", kk % 4))
                if self.pre == "ssd":
                    zs = kk % 2
                    zst = self.big[:, 24 + zs, :]
                    P.dma("sp", zst, G.zpark[kc], w=[("big", 24 + zs, tb) for tb in range(3)], key=("zst", zs))
                    for tb, (t0, tn) in enumerate(TBS):
                        P.op("dve", lambda kk=kk, zst=zst, t0=t0, tn=tn: nc.vector.tensor_tensor(
                            out=self.big[:, kk, t0:t0 + tn], in0=self.big[:, kk, t0:t0 + tn], in1=zst[:, t0:t0 + tn],
                            op=ALU.mult), r=[("big", kk, tb), ("big", 24 + zs, tb)], w=[("big", kk, tb)])
                        s_ = self.si
                        self.si = (s_ + 1) % 2
                        P.op("act", lambda kk=kk, s_=s_, t0=t0, tn=tn: nc.scalar.activation(
                            out=self.sq[s_][:, 0:tn], in_=self.big[:, kk, t0:t0 + tn], func=AF.Square),
                            r=[("big", kk, tb)], w=[("sq", s_)])
                        P.op("pe", lambda kk=kk, s_=s_, tb=tb, tn=tn: nc.tensor.matmul(
                            self.pgn[tb][:, 0:tn], lhsT=self.ones[:, :], rhs=self.sq[s_][:, 0:tn],
                            start=(kk % 4 == 0), stop=(kk % 4 == 3)), r=[("sq", s_), "ones"], w=[pgk[tb]])
                    if kk % 4 == 3:
                        for tb, (t0, tn) in enumerate(TBS):
                            P.op("dve", lambda tb=tb, t0=t0, tn=tn: nc.vector.tensor_scalar(
                                out=self.rstd[:, t0:t0 + tn], in0=self.pgn[tb][:, 0:tn], scalar1=1.0 / 512, scalar2=EPS,
                                op0=ALU.mult, op1=ALU.add), r=[pgk[tb]], w=[("rstd", tb)])
                            P.op("act", lambda t0=t0, tn=tn: nc.scalar.sqrt(
                                out=self.rstd[:, t0:t0 + tn], in_=self.rstd[:, t0:t0 + tn]),
                                r=[("rstd", tb)], w=[("rstd", tb)])
                            P.op("dve", lambda t0=t0, tn=tn: nc.vector.reciprocal(
                                out=self.rstd[:, t0:t0 + tn], in_=self.rstd[:, t0:t0 + tn]),
                                r=[("rstd", tb)], w=[("rstd", tb)])
                            for k4 in range(kk - 3, kk + 1):
                                kc4 = hh * KC + k4
                                P.op("dve", lambda k4=k4, kc4=kc4, t0=t0, tn=tn: nc.vector.scalar_tensor_tensor(
                                    out=self.big[:, k4, t0:t0 + tn], in0=self.big[:, k4, t0:t0 + tn],
                                    scalar=self.nwT[:, kc4:kc4 + 1], in1=self.rstd[:, t0:t0 + tn],
                                    op0=ALU.mult, op1=ALU.mult),
                                    r=[("big", k4, tb), ("rstd", tb), "nwT"], w=[("big", k4, tb)])
            for dg in range(4):
                slot = self.next_w()
                self.load_w(slot, W, hh * 2048, dg * 512, 512, 0)
                for di in range(4):
                    dc = dg * 4 + di
                    for tb, (t0, tn) in enumerate(TBS):
                        p = self.next_p()
                        self.mm_group(self.pmm[p][:, 0:tn], ("pmm", p),
                                      [(self.wb[slot][:, kc, di * 128:(di + 1) * 128],
                                        self.big[:, kc, t0:t0 + tn]) for kc in range(KC)],
                                      extra_r=[("wb", slot, 0), ("wb", slot, 1)] +
                                      [("big", kc, tb) for kc in range(KC)])
                        P.op("dve", lambda p=p, dc=dc, t0=t0, tn=tn: nc.vector.tensor_tensor(
                            out=self.h[:, dc, t0:t0 + tn], in0=self.pmm[p][:, 0:tn],
                            in1=self.h[:, dc, t0:t0 + tn], op=ALU.add),
                            r=[("pmm", p), ("h", dc, tb)], w=[("h", dc, tb)])
        self.npmm = 6

    def setup_mix(self):
        P = self.P
        nc = P.nc
        G = self.G
        j = self.mixj
        self.tsi = 0
        self.sent = []
        if self.pre == "attn":
            self.w_mix_out = G.attn_w_o[j[0]]
        elif self.pre == "ssd":
            self.w_mix_out = G.ssd_out_proj[j[0]]
        if self.post == "attn":
            self.w_mix_in = G.attn_w_qkv[j[1]]
            self.mix_out_f = G.x_qk
            self.mix_out_t = G.x_v
            self.cs = P.sb("cs", [128, 2, TT], F32)
            self.cos = self.cs[:, 0, :]
            self.sin = self.cs[:, 1, :]
            self.rot = P.sb("rot_sb", [128, 128], F32)
            self.qkg = P.sb("qkg_sb", [128, 2], F32)
            self.qn = [P.sb("qn%d" % i, [128, 512], F32) for i in range(2)]
            self.qf2 = P.sb("qf2", [128, 512], F32)
            self.qr = [self.rstd[:, i * 512:(i + 1) * 512] for i in range(2)]
            P.dma("sp", self.cs[:], G.cossin, w=["cos", "sin"], key="cs")
            P.dma("sp", self.rot[:], G.rot, w=["rot"], key="rot")
            P.dma("sp", self.qkg[:], G.qkg[:, j[1], :], w=["qkg"], key="qkg")
        elif self.post == "ssd":
            self.w_mix_in = G.ssd_in_proj[j[1]]
            self.mix_out_f = G.x_xbc
            self.mix_out_dt = G.x_dt
        if self.post is not None:
            self.fstg = [self.big[:, 16 + i, :] for i in range(2)]
            self.tstg = {"tstg": [self.big[:, 18 + i, 0:512] for i in range(2)],
                         "dstg": [P.sb("dstg%d" % i, [128, 128], F32) for i in range(2)]}

    def exchange(self):
        pass


def run_T(P, G, pre, ffns, post, h_src, h_dst, h_dst_is_out, mixj):
    P.begin_phase()
    T = TPhase(P, G, pre, ffns, post, h_src, h_dst, h_dst_is_out, mixj)
    T.setup_mix()
    if pre == "attn":
        T.pre_mix(16)
    elif pre == "ssd":
        T.pre_mix(32)
    for i, (l, sl) in enumerate(ffns):
        T.ffn(i, 2 * l + sl)
    if post == "attn":
        T.post_attn(8 + mixj[2])
    elif post == "ssd":
        T.post_ssd(8 + mixj[2])
    T.store_h()
    T.exchange()
    P.end_phase()


QBS = [(i * 512, 512) for i in range(8)] + [(4096, 16)]
KCS = [(i * 128, 128) for i in range(32)] + [(4096, 16)]


def run_HA(P, G):
    P.begin_phase()
    nc = P.nc
    qT = P.sb("qT", [128, 4, SV], BF16)
    kT = P.sb("kT", [128, 2, SV], BF16)
    v = P.sb("v", [128, 33, 256], BF16)
    ones = P.sb("ones", [128, 128], BF16)
    pt = [P.sb("pt%d" % i, [128, 512], BF16) for i in range(3)]
    rl = [P.sb("rl%d" % i, [128, 512], F32) for i in range(2)]
    ost = [P.sb("ost%d" % i, [128, 512], BF16) for i in range(2)]
    pss = [P.ps("pss%d" % i, [128, 512]) for i in range(3)]
    po = [P.ps("po%d" % i, [128, 512]) for i in range(2)]
    pl = [P.ps("pl%d" % i, [128, 512]) for i in range(2)]
    P.op("dve", lambda: nc.vector.memset(ones[:], 1.0), w=["ones"])
    P.op("dve", lambda: nc.vector.memset(v[:, 32, :], 0.0), w=["v"])
    ones32 = P.sb("ones32", [128, 128], F32)
    P.op("dve", lambda: nc.vector.memset(ones32[:], 1.0), w=["ones32"])
    acc = [P.sb("acc%d" % a_, [128, 512], F32) for a_ in range(2)]
    selstg = [P.sb("selstg%d" % i, [128, 4, 1024], BF16) for i in range(2)]
    psel = P.ps("psel", [128, 512])
    sel = Sel(P, G, selstg, lambda slot, k: [("selstg", slot, k)], lambda: (psel, "psel"))
    def ld_feat(dst3, idx, fsel, kname):
        for q in range(4):
            cc = [G.x_qk.rd(q, fsel(k)) for k in range(4)]
            sel.load(dst3[:, idx, q * 1024:(q + 1) * 1024], [c_[0][:, 0:1024] for c_ in cc], 1024,
                     r=[c_[1] for c_ in cc], w=[(kname, idx, q)])
        cc = [G.x_qk.rd(0, fsel(k)) for k in range(4)]
        sel.load(dst3[:, idx, 4096:4112], [c_[0][:, 1024:1040] for c_ in cc], 16, r=[c_[1] for c_ in cc],
                 w=[(kname, idx, 4)])
    for j in range(4):
        ld_feat(qT, j, lambda k, j=j: 4 * k + j, "q")
    for g in range(2):
        ld_feat(kT, g, lambda k, g=g: 16 + 2 * k + g, "k")
    v3d = lambda ap: ap.rearrange("p (c d) -> p c d", d=256)
    for q in range(4):
        for hf in range(2):
            c0 = 8 * q + 4 * hf
            cc = [G.x_v.rd4(q, hf, k // 2) for k in range(4)]
            sel.load(v[:, c0:c0 + 4, :].rearrange("p c d -> p (c d)"),
                     [cc[k][0][:, (k % 2) * 256:(k % 2 + 1) * 256].rearrange("(c p) d -> p c d", p=128) for k in range(4)],
                     1024, r=[c_[1] for c_ in cc] + ["v"], w=[("v", q)], view=v3d)
    cc = [G.x_v.rd(0, 8, k // 2, 16) for k in range(4)]
    sel.load(v[:, 32, :], [cc[k][0][:, (k % 2) * 256:(k % 2 + 1) * 256] for k in range(4)], 256,
             r=[c_[1] for c_ in cc] + ["v"], w=[("v", 4)], prow=(0, 16))
    QK = [("q", j, q) for j in range(4) for q in range(5)]
    sent = []
    scale = 128 ** -0.5
    its = []
    ob = 0
    for qb, (q0, qn) in enumerate(QBS):
        for g in range(2):
            for jj in range(2):
                j = 2 * g + jj
                a = ob % 2
                ob += 1
                for kc, (k0, kn) in enumerate(KCS):
                    its.append((qb, q0, qn, g, j, a, kc, k0, kn))

    def emit_s(i):
        qb, q0, qn, g, j, a, kc, k0, kn = its[i]
        s3 = i % 3
        P.op("pe", lambda: nc.tensor.matmul(
            pss[s3][0:kn, 0:qn], lhsT=kT[:, g, k0:k0 + kn], rhs=qT[:, j, q0:q0 + qn], start=True, stop=True),
            r=[("k", g, x_) for x_ in range(5)] + [("q", j, x_) for x_ in range(5)], w=[("pss", s3)])
        P.op("act", lambda: nc.scalar.activation(
            out=pt[s3][0:kn, 0:qn], in_=pss[s3][0:kn, 0:qn], func=AF.Exp, scale=scale),
            r=[("pss", s3)], w=[("pt", s3)])

    def emit_pv(i):
        qb, q0, qn, g, j, a, kc, k0, kn = its[i]
        s3 = i % 3
        P.op("pe", lambda: nc.tensor.matmul(
            po[a][:, 0:qn], lhsT=v[0:kn, kc, g * 128:(g + 1) * 128], rhs=pt[s3][0:kn, 0:qn],
            start=(kc == 0), stop=(kc == 32)), r=[("pt", s3), ("v", min(kc // 8, 4))], w=[("po", a)])
        if kc == 0:
            P.op("dve", lambda: nc.vector.tensor_copy(out=acc[a][:, 0:qn], in_=pt[s3][:, 0:qn]),
                 r=[("pt", s3)], w=[("acc", a)])
        else:
            P.op("dve", lambda: nc.vector.tensor_tensor(out=acc[a][0:kn, 0:qn], in0=acc[a][0:kn, 0:qn],
                                                        in1=pt[s3][0:kn, 0:qn], op=ALU.add),
                 r=[("pt", s3), ("acc", a)], w=[("acc", a)])
        if kc == 32:
            P.op("pe", lambda: nc.tensor.matmul(pl[a][:, 0:qn], lhsT=ones32[:, :], rhs=acc[a][:, 0:qn],
                                                start=True, stop=True),
                 r=[("acc", a), "ones32"], w=[("pl", a)])
            P.op("dve", lambda: nc.vector.reciprocal(out=rl[a][:, 0:qn], in_=pl[a][:, 0:qn]),
                 r=[("pl", a)], w=[("rl", a)])
            P.op("dve", lambda: nc.vector.tensor_tensor(
                out=ost[a][:, 0:qn], in0=po[a][:, 0:qn], in1=rl[a][:, 0:qn], op=ALU.mult),
                r=[("po", a), ("rl", a)], w=[("ost", a)])
            G.x_o.write(j, qb, qn, ost[a][:, 0:qn], r=[("ost", a)], dkey=("ost", a))

    for i in range(len(its) + 2):
        if i < len(its):
            emit_s(i)
        if i >= 2:
            emit_pv(i - 2)
    P.end_phase()


SW = SP_ + 6


def run_HS(P, G, jl):
    P.begin_phase()
    nc = P.nc
    cw_in, cb_in, cbrow_in = G.cw_in[jl], G.cb_in[jl], G.cbrow_in[jl]
    dsk_in, nw_in, cst_in = G.dsk_in[jl], G.nw_in[jl], G.cst_in
    dtb_in, alog_in = G.dtb_in[jl], G.alog_in[jl]
    sent = []
    selstg = [P.sb("selstg%d" % i, [128, 4, 512], BF16) for i in range(2)]
    psel = P.ps("psel", [128, 512])
    sel = Sel(P, G, selstg, lambda slot, k: [("selstg", slot, k)], lambda: (psel, "psel"))
    stg32 = [P.sb("stg32_%d" % i, [128, NCH, 16], F32) for i in range(1)]
    for i_ in range(1):
        P.op("dve", lambda i_=i_: nc.vector.memset(stg32[i_][:], 0.0), w=[("stg32", i_)])

    cst = P.sb("cst", [128, 4, 128], F32)
    Uf, Ub = cst[:, 0, :], cst[:, 1, :]
    neg4 = P.sb("neg4", [128, 2, 4, 128], BF16)
    ident = P.sb("ident", [128, 128], BF16)
    identf = P.sb("identf", [128, 128], F32)
    ones32 = P.sb("ones32", [128, 128], F32)
    onesrow = P.sb("onesrow", [1, 128], BF16)
    cw = P.sb("cw", [128, 2, 6, 7], F32)
    cb = P.sb("cb", [128, 2, 6], F32)
    cbrow32 = P.sb("cbrow32", [1, 2, 640], F32)
    cbrow = P.sb("cbrow", [1, 2, 640], BF16)
    dsk = P.sb("dsk", [128, 2, 8], F32)
    nw = P.sb("nw", [128, 2, 512], F32)
    diag = P.sb("diag", [128, 6, 7, 128], BF16)
    xbc_sb = P.sb("xbc_sb", [128, 6, SW], BF16)
    x_tok = P.sb("x_tok", [128, NCH, 512], BF16)
    B_tok = P.sb("B_tok", [128, NCH, 128], BF16)
    BT = P.sb("BT", [128, SP_], BF16)
    CT = P.sb("CT", [128, SP_], BF16)
    NS = NCH * 16
    dtt = P.sb("dtt", [128, NCH, 16], F32)
    av = P.sb("av", [128, NCH, 16], F32)
    aneg = P.sb("aneg", [128, NCH, 16], F32)
    ldt = aneg
    acs = P.sb("acs", [128, NCH, 16], F32)
    tot = P.sb("tot", [128, NCH, 16], F32)
    biasD = aneg
    dte = P.sb("dte", [128, NCH, 16], F32)
    eacs = P.sb("eacs", [128, NCH, 16], F32)
    cd = P.sb("cd", [128, NCH, 16], F32)
    tmps = stg32[0]
    prevb = xbc_sb[:].rearrange("p c t -> p (c t)")[:, 0:NCH * 512].rearrange("p (c f) -> p c f", c=NCH)
    Hs = [P.sb("H%d" % i, [128, 512], F32) for i in range(2)]
    Hf_bf = P.sb("Hf_bf", [128, 512], BF16)
    aU = [P.sb("aU%d" % i, [128, 8, 128], F32) for i in range(2)]
    Dm = [[P.sb("Dm%d_%d" % (pr_, i), [128, 8, 128], BF16) for i in range(2)] for pr_ in range(2)]
    Msum = P.sb("Msum", [128, 8, 128], BF16)
    Mm = [P.sb("Mm%d" % i, [128, 8, 128], BF16) for i in range(2)]
    CBT = [P.sb("CBT%d" % i, [128, 128], BF16) for i in range(2)]
    xdte = [P.sb("xdte%d" % i, [128, 512], BF16) for i in range(2)]
    yt = [P.sb("yt%d" % i, [128, 512], F32) for i in range(3)]
    ht = P.sb("ht", [128, 512], F32)
    sz = ht
    junk = yt[2]
    ss = P.sb("ss", [128, 1], F32)
    gn = P.sb("gn", [128, 512], BF16)
    gst = [P.sb("gst%d" % i, [128, 4, 128], BF16) for i in range(2)]
    pb = [P.ps("pb%d" % i, [128, 512]) for i in range(4)]
    pacs = P.ps("pacs", [128, 1024])
    pb.append(pacs[:, 0:512])
    pb.append(pacs[:, 512:1024])
    ptr = P.ps("ptr", [128, 512], BF16)

    P.dma("sp", cst[:], cst_in, w=["cst"], key="cst")
    P.dma("sp", cw[:], cw_in, w=["cw"], key="cw")
    P.dma("sp", cb[:], cb_in, w=["cb"], key="cb")
    P.dma("sp", cbrow32[:], cbrow_in, w=["cbrow32"], key="cbrow")
    P.dma("sp", dsk[:], dsk_in, w=["dsk"], key="dsk")
    P.dma("sp", nw[:], nw_in, w=["nw"], key="nw")
    P.op("dve", lambda: nc.vector.memset(ones32[:], 1.0), w=["ones32"])
    P.op("dve", lambda: nc.vector.memset(onesrow[:], 1.0), w=["onesrow"])
    P.op("dve", lambda: nc.vector.tensor_tensor(out=identf[:], in0=Uf, in1=Ub, op=ALU.mult), r=["cst"], w=["identf"])
    P.op("dve", lambda: nc.vector.tensor_copy(out=ident[:], in_=identf[:]), r=["identf"], w=["ident"])
    for d_ in range(2):
        for hh in range(4):
            P.op("dve", lambda d_=d_, hh=hh: nc.vector.tensor_copy(out=neg4[:, d_, hh, :], in_=cst[:, 2 + d_, :]),
                 r=["cst"], w=["neg4"])
    P.op("dve", lambda: nc.vector.tensor_copy(out=cbrow[:], in_=cbrow32[:]), r=["cbrow32"], w=["cbrow"])

    def bc(ap2, n):
        return ap2.unsqueeze(2).to_broadcast([128, ap2.shape[1], n])

    def v3(ap2, k):
        return ap2.rearrange("p (k n) -> p k n", k=k)

    for g in range(2):
        if g == 1:
            P.barrier()
        for ch in range(6):
            for k in range(7):
                P.op("dve", lambda ch=ch, k=k, g=g: nc.vector.tensor_scalar(
                    out=diag[:, ch, k, :], in0=ident[:], scalar1=cw[:, g, ch, k:k + 1], scalar2=None, op0=ALU.mult),
                    r=["ident", "cw"], w=[("diag", ch)])
        P.op("dve", lambda: nc.vector.memset(xbc_sb[:, :, 0:115], 0.0), w=[("xbc", ci, 0) for ci in range(6)])
        P.op("dve", lambda: nc.vector.memset(xbc_sb[:, :, SW - 3:SW], 0.0), w=[("xbc", ci, 5) for ci in range(6)])
        for ci in range(6):
            if ci < 4:
                fsel = lambda k, ci=ci: (2 * k + g) * 4 + ci
            elif ci == 4:
                fsel = lambda k: 32 + 2 * k + g
            else:
                fsel = lambda k: 40 + 2 * k + g
            for q in range(4):
                cc = [G.x_xbc.rd(q, fsel(k)) for k in range(4)]
                for hf in range(2):
                    sel.load(xbc_sb[:, ci, 131 + q * 1024 + hf * 512:131 + q * 1024 + (hf + 1) * 512],
                             [c_[0][:, hf * 512:(hf + 1) * 512] for c_ in cc], 512, r=[c_[1] for c_ in cc],
                             w=[("xbc", ci, 1 + q)])
            cc = [G.x_xbc.rd(0, fsel(k)) for k in range(4)]
            sel.load(xbc_sb[:, ci, 115:131], [c_[0][:, 1024:1040] for c_ in cc], 16,
                     r=[c_[1] for c_ in cc] + [("xbc", ci, 0)], w=[("xbc", ci, 0)])
        XBC = [("xbc", ci, x_) for ci in range(6) for x_ in range(6)]
        P.op("dve", lambda: nc.vector.memset(dtt[:], 0.0), w=["dtt"])
        for k in range(4):
            sl32 = 0
            for d_ in range(2):
                c0 = k * 16 + d_ * 64 + g * 8
                for q in range(4):
                    for pt in range(2):
                        src, rk = G.x_dt.rd4(q, pt, 0)
                        cb_ = 1 + 8 * q + 4 * pt
                        P.dma("sp", stg32[sl32][:, cb_:cb_ + 4, d_ * 8:d_ * 8 + 8],
                              src[:, c0:c0 + 8].rearrange("(c p) k -> p c k", p=128),
                              r=[rk], w=[("stg32", sl32)], key=("stg32", sl32, d_, q, pt))
                src, rk = G.x_dt.rd(0, 8, 0, 16)
                P.dma("sp", stg32[sl32][112:128, 0, d_ * 8:d_ * 8 + 8], src[:, c0:c0 + 8],
                      r=[rk], w=[("stg32", sl32)], key=("stg32", sl32, d_, 4, 0))
            P.op("dve", lambda k=k, sl32=sl32: nc.vector.scalar_tensor_tensor(
                out=dtt[:], in0=stg32[sl32][:], scalar=sel.sel[:, k:k + 1], in1=dtt[:], op0=ALU.mult, op1=ALU.add),
                r=[("stg32", sl32), "sel", "dtt"], w=["dtt"])
        P.dma("sp", tmps[:], dtb_in[g], w=[("stg32", 0)], key=("stg32", 0))
        P.dma("sp", aneg[:], alog_in[g], w=["aneg"], key="aneg")
        P.op("dve", lambda: nc.vector.tensor_tensor(out=dtt[:], in0=dtt[:], in1=tmps[:], op=ALU.add),
             r=["dtt", ("stg32", 0)], w=["dtt"])
        P.op("act", lambda: nc.scalar.activation(out=dtt[:], in_=dtt[:], func=AF.Exp), r=["dtt"], w=["dtt"])
        P.op("act", lambda: nc.scalar.activation(out=dtt[:], in_=dtt[:], func=AF.Ln, bias=1.0), r=["dtt"], w=["dtt"])
        P.op("dve", lambda: nc.vector.memset(dtt[0:112, 0, :], 0.0), r=["dtt"], w=["dtt"])
        P.op("act", lambda: nc.scalar.activation(out=aneg[:], in_=aneg[:], func=AF.Exp), r=["aneg"], w=["aneg"])
        P.op("dve", lambda: nc.vector.scalar_tensor_tensor(out=av[:], in0=aneg[:], scalar=-1.0, in1=dtt[:],
                                                           op0=ALU.mult, op1=ALU.mult),
             r=["aneg", "dtt"], w=["av"])
        P.op("dve", lambda: nc.vector.tensor_scalar_max(out=ldt[:], in0=dtt[:], scalar1=1e-30), r=["dtt"], w=["aneg"])
        P.op("act", lambda: nc.scalar.activation(out=ldt[:], in_=ldt[:], func=AF.Ln), r=["aneg"], w=["aneg"])
        for c in range(NCH):
            P.op("pe", lambda c=c: nc.tensor.matmul(pacs[:, c * 16:c * 16 + 8], lhsT=Uf, rhs=av[:, c, 0:8],
                                                   start=True, stop=True), r=["av", "cst"], w=[("pb", 4), ("pb", 5)])
            P.op("pe", lambda c=c: nc.tensor.matmul(pacs[:, c * 16 + 8:c * 16 + 16], lhsT=Ub, rhs=av[:, c, 8:16],
                                                   start=True, stop=True), r=["av", "cst"], w=[("pb", 4), ("pb", 5)])
        P.op("act", lambda: nc.scalar.copy(out=acs[:].rearrange("p c k -> p (c k)"), in_=pacs[:, 0:NS]),
             r=[("pb", 4), ("pb", 5)], w=["acs"])
        for c in range(NCH):
            P.op("pe", lambda c=c: nc.tensor.matmul(pacs[:, c * 16:c * 16 + 16], lhsT=ones32[:], rhs=av[:, c, :],
                                                   start=True, stop=True), r=["av", "ones32", "acs"], w=[("pb", 4), ("pb", 5)])
        P.op("act", lambda: nc.scalar.copy(out=tot[:].rearrange("p c k -> p (c k)"), in_=pacs[:, 0:NS]),
             r=[("pb", 4), ("pb", 5)], w=["tot"])
        P.op("dve", lambda: nc.vector.tensor_tensor(out=biasD[:], in0=ldt[:], in1=acs[:], op=ALU.subtract),
             r=["aneg", "acs"], w=["aneg"])
        P.op("act", lambda: nc.scalar.activation(out=eacs[:], in_=acs[:], func=AF.Exp), r=["acs"], w=["eacs"])
        P.op("act", lambda: nc.scalar.activation(out=cd[:], in_=tot[:], func=AF.Exp), r=["tot"], w=["cd"])
        P.op("dve", lambda: nc.vector.tensor_tensor(out=dte[:], in0=tot[:], in1=acs[:], op=ALU.subtract),
             r=["tot", "acs"], w=["dte"])
        P.op("act", lambda: nc.scalar.activation(out=dte[:], in_=dte[:], func=AF.Exp), r=["dte"], w=["dte"])
        P.op("dve", lambda: nc.vector.tensor_tensor(out=dte[:], in0=dte[:], in1=dtt[:], op=ALU.mult),
             r=["dte", "dtt"], w=["dte"])
        for c in range(NCH):
            w_ = 0
            cb0 = c * 128
            p0 = c % 2

            def conv_tok(c=c, cb0=cb0, p0=p0, g=g):
                ins = None
                for xc in range(4):
                    for k in range(7):
                        nc.tensor.matmul(pb[p0][:, xc * 128:(xc + 1) * 128], lhsT=xbc_sb[:, xc, cb0 + k:cb0 + k + 128],
                                         rhs=diag[:, xc, k, :], start=(k == 0), stop=False)
                    ins = nc.tensor.matmul(pb[p0][:, xc * 128:(xc + 1) * 128], lhsT=onesrow[0:1, :],
                                           rhs=cbrow[0:1, g, xc * 128:(xc + 1) * 128], start=False, stop=True)
                return ins
            P.op("pe", conv_tok, r=XBC + ["onesrow", "cbrow"] + [("diag", ch) for ch in range(4)], w=[("pb", p0)])
            P.op("act", lambda c=c, p0=p0: nc.scalar.activation(out=x_tok[:, c, :], in_=pb[p0][:, :], func=AF.Silu),
                 r=[("pb", p0)], w=[("x_tok", c)])
            p2 = 2 + c % 2

            def conv_b(c=c, cb0=cb0, p2=p2, g=g):
                for k in range(7):
                    nc.tensor.matmul(pb[p2][:, 0:128], lhsT=xbc_sb[:, 4, cb0 + k:cb0 + k + 128], rhs=diag[:, 4, k, :],
                                     start=(k == 0), stop=False)
                nc.tensor.matmul(pb[p2][:, 0:128], lhsT=onesrow[0:1, :], rhs=cbrow[0:1, g, 512:640],
                                 start=False, stop=True)
                for k in range(7):
                    nc.tensor.matmul(pb[p2][:, 128:256], lhsT=diag[:, 4, k, :], rhs=xbc_sb[:, 4, cb0 + k:cb0 + k + 128],
                                     start=(k == 0), stop=(k == 6))
                ins = None
                for k in range(7):
                    ins = nc.tensor.matmul(pb[p2][:, 256:384], lhsT=diag[:, 5, k, :], rhs=xbc_sb[:, 5, cb0 + k:cb0 + k + 128],
                                           start=(k == 0), stop=(k == 6))
                return ins
            P.op("pe", conv_b, r=XBC + ["onesrow", "cbrow", ("diag", 4), ("diag", 5)], w=[("pb", p2)])
            P.op("act", lambda c=c, p2=p2: nc.scalar.activation(out=B_tok[:, c, :], in_=pb[p2][:, 0:128], func=AF.Silu),
                 r=[("pb", p2)], w=[("B_tok", c)])
            P.op("act", lambda c=c, p2=p2, g=g: nc.scalar.activation(
                out=BT[:, c * 128:(c + 1) * 128], in_=pb[p2][:, 128:256], func=AF.Silu, bias=cb[:, g, 4:5]),
                r=[("pb", p2), "cb"], w=[("BT", c)])
            P.op("act", lambda c=c, p2=p2, g=g: nc.scalar.activation(
                out=CT[:, c * 128:(c + 1) * 128], in_=pb[p2][:, 256:384], func=AF.Silu, bias=cb[:, g, 5:6]),
                r=[("pb", p2), "cb"], w=[("CT", c)])
            if c == 0:
                P.op("dve", lambda: nc.vector.memset(x_tok[0:112, 0, :], 0.0), r=[("x_tok", 0)], w=[("x_tok", 0)])
                P.op("dve", lambda: nc.vector.memset(B_tok[0:112, 0, :], 0.0), r=[("B_tok", 0)], w=[("B_tok", 0)])
                P.op("dve", lambda: nc.vector.memset(BT[:, 0:112], 0.0), r=[("BT", 0)], w=[("BT", 0)])
                P.op("dve", lambda: nc.vector.memset(CT[:, 0:112], 0.0), r=[("CT", 0)], w=[("CT", 0)])
        P.op("dve", lambda: nc.vector.memset(Hs[1][:], 0.0), w=[("H", 1)])
        P.op("dve", lambda: nc.vector.memset(Hs[0][:], 0.0), w=[("H", 0)])
        P.op("dve", lambda: nc.vector.memset(Hf_bf[:], 0.0), w=["Hf_bf"])
        for c in range(NCH - 1, -1, -1):
            xs = c % 2
            P.op("act", lambda c=c: nc.scalar.copy(out=prevb[:, c, :], in_=Hs[1][:]), r=[("H", 1)],
                 w=[("prevb", c)] + (XBC if c == NCH - 1 else []))
            if c == 0:
                break
            P.op("dve", lambda c=c, xs=xs: nc.vector.tensor_tensor(
                out=v3(xdte[xs][:], 8), in0=v3(x_tok[:, c, :], 8), in1=bc(dte[:, c, 8:16], 64), op=ALU.mult),
                r=[("x_tok", c), "dte"], w=[("xdte", xs)])
            pS = 4 + c % 2
            P.op("pe", lambda c=c, xs=xs, pS=pS: nc.tensor.matmul(pb[pS][:, :], lhsT=B_tok[:, c, :], rhs=xdte[xs][:],
                                                                  start=True, stop=True),
                 r=[("B_tok", c), ("xdte", xs)], w=[("pb", pS)])
            P.op("dve", lambda c=c: nc.vector.tensor_tensor(
                out=v3(ht[:], 8), in0=v3(Hs[1][:], 8), in1=bc(cd[:, c, 8:16], 64), op=ALU.mult),
                r=[("H", 1), "cd"], w=["ht"])
            P.op("dve", lambda pS=pS: nc.vector.tensor_tensor(out=Hs[1][:], in0=ht[:], in1=pb[pS][:, :], op=ALU.add),
                 r=["ht", ("pb", pS)], w=[("H", 1)])
        def front(c, g=g):
            cs_ = slice(c * 128, (c + 1) * 128)
            pr = c % 2
            for d_ in range(2):
                U = Uf if d_ == 0 else Ub
                P.op("dve", lambda c=c, d_=d_, U=U: nc.vector.tensor_tensor(
                    out=aU[d_][:], in0=bc(av[:, c, d_ * 8:d_ * 8 + 8], 128),
                    in1=U.unsqueeze(1).to_broadcast([128, 8, 128]), op=ALU.mult),
                    r=["av", "cst"], w=[("aU", d_)])
                for hb in range(2):
                    pD = hb

                    def segmm(c=c, d_=d_, hb=hb, pD=pD):
                        nc.tensor.matmul(pb[pD][:, :], lhsT=ones32[:], rhs=aU[d_][:, hb * 4:hb * 4 + 4, :],
                                         start=True, stop=False)
                        return nc.tensor.matmul(pb[pD][:, :], lhsT=ident[:], rhs=neg4[:, d_, :, :],
                                                start=False, stop=True)
                    P.op("pe", segmm, r=[("aU", d_), "ones32", "ident", "neg4"], w=[("pb", pD)])
                    for h4 in range(4):
                        hh = hb * 4 + h4
                        P.op("act", lambda c=c, d_=d_, pD=pD, h4=h4, hh=hh, pr=pr: nc.scalar.activation(
                            out=Dm[pr][d_][:, hh, :], in_=pb[pD][:, h4 * 128:(h4 + 1) * 128], func=AF.Exp,
                            bias=biasD[:, c, d_ * 8 + hh:d_ * 8 + hh + 1]),
                            r=[("pb", pD), "aneg"], w=[("Dm", pr, d_, hh)])
            P.op("pe", lambda cs_=cs_: nc.tensor.matmul(pb[2][:, 0:128], lhsT=BT[:, cs_], rhs=CT[:, cs_],
                                                        start=True, stop=True),
                 r=[("BT", c), ("CT", c)], w=[("pb", 2)])
            P.op("act", lambda pr=pr: nc.scalar.copy(out=CBT[pr][:], in_=pb[2][:, 0:128]), r=[("pb", 2)], w=[("CBT", pr)])
            P.op("dve", lambda pr=pr: nc.vector.tensor_tensor(out=Msum[:], in0=Dm[pr][0][:], in1=Dm[pr][1][:], op=ALU.add),
                 r=[("Dm", pr, d_, hh) for d_ in range(2) for hh in range(8)], w=["Msum"])
            P.op("dve", lambda pr=pr: nc.vector.tensor_tensor(
                out=Mm[pr][:], in0=Msum[:], in1=CBT[pr][:].unsqueeze(1).to_broadcast([128, 8, 128]), op=ALU.mult),
                r=["Msum", ("CBT", pr)], w=[("Mm", pr)])

        def back(c, g=g):
            cs_ = slice(c * 128, (c + 1) * 128)
            pr = c % 2

            def ymm(c=c, pr=pr):
                ins = None
                for hh in range(8):
                    ins = nc.tensor.matmul(pb[3][:, hh * 64:(hh + 1) * 64], lhsT=Mm[pr][:, hh, :],
                                           rhs=x_tok[:, c, hh * 64:(hh + 1) * 64], start=True, stop=True)
                return ins
            P.op("pe", ymm, r=[("Mm", pr), ("x_tok", c)], w=[("pb", 3)])
            P.op("dve", lambda c=c, g=g: nc.vector.tensor_tensor(
                out=v3(yt[0][:], 8), in0=v3(x_tok[:, c, :], 8), in1=bc(dsk[:, g, :], 64), op=ALU.mult),
                r=[("x_tok", c), "dsk"], w=[("yt", 0)])
            P.op("dve", lambda: nc.vector.tensor_tensor(out=yt[0][:], in0=yt[0][:], in1=pb[3][:, :], op=ALU.add),
                 r=[("yt", 0), ("pb", 3)], w=[("yt", 0)])
            P.op("pe", lambda cs_=cs_: nc.tensor.matmul(pb[4][:, :], lhsT=CT[:, cs_], rhs=Hf_bf[:], start=True, stop=True),
                 r=[("CT", c), "Hf_bf"], w=[("pb", 4)])
            P.op("dve", lambda c=c: nc.vector.tensor_tensor(
                out=v3(yt[1][:], 8), in0=v3(pb[4][:, :], 8), in1=bc(eacs[:, c, 0:8], 64), op=ALU.mult),
                r=[("pb", 4), "eacs"], w=[("yt", 1)])
            P.op("pe", lambda c=c, cs_=cs_: nc.tensor.matmul(pb[5][:, :], lhsT=CT[:, cs_], rhs=prevb[:, c, :],
                                                             start=True, stop=True),
                 r=[("CT", c), ("prevb", c)], w=[("pb", 5)])
            P.op("dve", lambda c=c: nc.vector.tensor_tensor(
                out=v3(yt[2][:], 8), in0=v3(pb[5][:, :], 8), in1=bc(eacs[:, c, 8:16], 64), op=ALU.mult),
                r=[("pb", 5), "eacs"], w=[("yt", 2)])
            P.op("dve", lambda: nc.vector.tensor_tensor(out=yt[0][:], in0=yt[0][:], in1=yt[1][:], op=ALU.add),
                 r=[("yt", 0), ("yt", 1)], w=[("yt", 0)])
            P.op("dve", lambda: nc.vector.tensor_tensor(out=yt[0][:], in0=yt[0][:], in1=yt[2][:], op=ALU.add),
                 r=[("yt", 0), ("yt", 2)], w=[("yt", 0)])
            if c < NCH - 1:
                xs = c % 2
                P.op("dve", lambda c=c, xs=xs: nc.vector.tensor_tensor(
                    out=v3(xdte[xs][:], 8), in0=v3(x_tok[:, c, :], 8), in1=bc(dte[:, c, 0:8], 64), op=ALU.mult),
                    r=[("x_tok", c), "dte"], w=[("xdte", xs)])
                P.op("pe", lambda c=c, xs=xs: nc.tensor.matmul(pb[2][:, :], lhsT=B_tok[:, c, :], rhs=xdte[xs][:],
                                                               start=True, stop=True),
                     r=[("B_tok", c), ("xdte", xs)], w=[("pb", 2)])
                P.op("dve", lambda c=c: nc.vector.tensor_tensor(
                    out=v3(ht[:], 8), in0=v3(Hs[0][:], 8), in1=bc(cd[:, c, 0:8], 64), op=ALU.mult),
                    r=[("H", 0), "cd"], w=["ht"])
                P.op("dve", lambda: nc.vector.tensor_tensor(out=Hs[0][:], in0=ht[:], in1=pb[2][:, :], op=ALU.add),
                     r=["ht", ("pb", 2)], w=[("H", 0)])
                P.op("act", lambda: nc.scalar.copy(out=Hf_bf[:], in_=Hs[0][:]), r=[("H", 0)], w=["Hf_bf"])
            P.op("act", lambda: nc.scalar.copy(out=gn[:], in_=yt[0][:]), r=[("yt", 0)], w=["gn"])

            def trans():
                ins = None
                for i in range(4):
                    ins = nc.tensor.transpose(ptr[:, i * 128:(i + 1) * 128], gn[:, i * 128:(i + 1) * 128], ident[:])
                return ins
            P.op("pe", trans, r=["gn", "ident"], w=["ptr"])
            gs = c % 2
            P.op("act", lambda gs=gs: nc.scalar.copy(out=gst[gs][:].rearrange("p i t -> p (i t)"), in_=ptr[:, :]),
                 r=["ptr"], w=[("gst", gs)])
            G.x_g.write(g, c, gst[gs][:], r=[("gst", gs)], dkey=("gst", gs))

        for c in range(NCH + 1):
            if c < NCH:
                front(c)
            if c >= 1:
                back(c - 1)
    P.end_phase()


class _G:
    pass


def build_program():
    P = Prog()
    G = _G()
    ext = lambda n, shp, dt=F32: P.dram(n, shp, dt, "ExternalInput")
    G.ffn_w_in = ext("ffn_w_in", [4, 2, D, 2 * DFF])
    G.ffn_w_out = ext("ffn_w_out", [4, 2, DFF, D])
    G.ssd_in_proj = ext("ssd_in_proj", [2, D, 10368])
    G.ssd_out_proj = ext("ssd_out_proj", [2, 4096, D])
    G.attn_w_qkv = ext("attn_w_qkv", [2, D, 4096])
    G.attn_w_o = ext("attn_w_o", [2, 2048, D])
    G.rot = ext("rot", [128, 128])
    G.cst_in = ext("cst_in", [128, 4, 128])
    G.ident_in = ext("ident_in", [128, 128])
    G.sel_in = ext("sel_in", [128, 4])
    G.nwT_in = ext("nwT_in", [2, 128, 32])
    G.h0 = ext("h0", [128, KC, TT])
    G.gam_in = ext("gam_in", [128, 12, KC])
    G.cossin = ext("cossin", [128, 2, TT])
    G.qkg = ext("qkg", [128, 2, 2])
    G.cw_in = ext("cw_in", [2, 128, 2, 6, 7])
    G.cb_in = ext("cb_in", [2, 128, 2, 6])
    G.cbrow_in = ext("cbrow_in", [2, 1, 2, 640])
    G.dsk_in = ext("dsk_in", [2, 128, 2, 8])
    G.nw_in = ext("nw_in", [2, 128, 2, 512])
    G.dtb_in = ext("dtb_in", [2, 2, 128, NCH, 16])
    G.alog_in = ext("alog_in", [2, 2, 128, NCH, 16])
    G.h_out = P.dram("h_out", [128, KC, TT], F32, "ExternalOutput")
    G.h_dram = P.idram("h_dram", [128, KC, TT], F32)
    G.x_xbc = FeatX(P, "xbc", 48, order=XBC_ORDER)
    G.zpark = P.idram("zpark", [32, 128, TT], BF16)
    G.x_dt = TokX(P, "dt", 1, 128, F32)
    G.x_g = GX(P)
    G.x_qk = FeatX(P, "qk", 24)
    G.x_v = TokX(P, "v", 2, 512, BF16)
    G.x_o = OX(P)

    run_T(P, G, None, [(0, 0)], "ssd", G.h0, G.h_dram, False, (None, 0, 0))
    run_HS(P, G, 0)
    run_T(P, G, "ssd", [(0, 1), (1, 0)], "attn", G.h_dram, G.h_dram, False, (0, 0, 1))
    run_HA(P, G)
    run_T(P, G, "attn", [(1, 1), (2, 0)], "ssd", G.h_dram, G.h_dram, False, (0, 1, 2))
    run_HS(P, G, 1)
    run_T(P, G, "ssd", [(2, 1), (3, 0)], "attn", G.h_dram, G.h_dram, False, (1, 1, 3))
    run_HA(P, G)
    run_T(P, G, "attn", [(3, 1)], None, G.h_dram, G.h_out, True, (1, None, None))
    P.emit()
    return P


def rope_tables(q):
    inv = (10000.0 ** (-np.arange(0, 64, 2, dtype=np.float32) / 64.0)).astype(np.float32)
    t = q * 1024 + np.arange(1024)
    row = np.concatenate([t // 64, np.full((16,), -1)]).astype(np.float32)
    col = np.concatenate([t % 64, np.arange(16)]).astype(np.float32)
    ang = np.stack([row, col], -1)[..., None] * inv
    c, s = np.cos(ang).astype(np.float32), np.sin(ang).astype(np.float32)
    out = np.zeros((128, 2, TT), np.float32)
    for d in range(128):
        out[d, 0] = c[:, d // 64, d % 32]
        out[d, 1] = s[:, d // 64, d % 32]
    return out


def rot_matrix():
    r = np.zeros((128, 128), np.float32)
    for m in range(128):
        if m % 64 < 32:
            r[m + 32, m] = -1.0
        else:
            r[m - 32, m] = 1.0
    return r


def col_layout(v):
    return np.ascontiguousarray(np.asarray(v, np.float32).reshape(-1, 128).T)


def ssd_consts():
    j = np.arange(128)
    Uf = (j[:, None] <= j[None, :]).astype(np.float32)
    Ub = (j[:, None] >= j[None, :]).astype(np.float32)
    NEGf = np.where(j[:, None] > j[None, :], -30000.0, 0.0).astype(np.float32)
    NEGb = np.where(j[:, None] < j[None, :], -30000.0, 0.0).astype(np.float32)
    return np.ascontiguousarray(np.stack([Uf, Ub, NEGf, NEGb], axis=1))


def ssd_params(hq, conv_w, conv_b, dt_bias, a_log, d_skip, norm_w):
    nl = conv_w.shape[0]
    cw = np.zeros((nl, 128, 2, 6, 7), np.float32)
    cb = np.zeros((nl, 128, 2, 6), np.float32)
    cbrow = np.zeros((nl, 1, 2, 640), np.float32)
    dsk = np.zeros((nl, 128, 2, 8), np.float32)
    nw = np.zeros((nl, 128, 2, 512), np.float32)
    dtb = np.zeros((nl, 2, 128, NCH, 16), np.float32)
    alog = np.zeros((nl, 2, 128, NCH, 16), np.float32)
    for j in range(nl):
        for gi in range(2):
            Gg = 2 * hq + gi
            chans = [Gg * 512 + i * 128 for i in range(4)] + [4096 + Gg * 128, 5120 + Gg * 128]
            for ci, c0 in enumerate(chans):
                cw[j, :, gi, ci, :] = conv_w[j][:, c0:c0 + 128].T
                cb[j, :, gi, ci] = conv_b[j][c0:c0 + 128]
            cbrow[j, 0, gi, 0:512] = conv_b[j][Gg * 512:(Gg + 1) * 512]
            cbrow[j, 0, gi, 512:640] = conv_b[j][4096 + Gg * 128:4096 + (Gg + 1) * 128]
            dtb[j, gi] = np.concatenate([dt_bias[j, 0, Gg * 8:Gg * 8 + 8], dt_bias[j, 1, Gg * 8:Gg * 8 + 8]])[None, None, :]
            alog[j, gi] = np.concatenate([a_log[j, 0, Gg * 8:Gg * 8 + 8], a_log[j, 1, Gg * 8:Gg * 8 + 8]])[None, None, :]
            dsk[j, :, gi, :] = d_skip[j][Gg * 8:Gg * 8 + 8][None, :]
            nw[j, :, gi, :] = norm_w[j][Gg * 512:(Gg + 1) * 512][None, :]
    return {"cw_in": cw, "cb_in": cb, "cbrow_in": cbrow, "dsk_in": dsk, "nw_in": nw, "dtb_in": dtb, "alog_in": alog}


_PROG = []


def kernel(x, meta_tokens, ffn_norm, ffn_w_in, ffn_w_out, mix_norm, ssd_in_proj, ssd_conv_w, ssd_conv_b,
           ssd_dt_bias, ssd_A_log, ssd_D, ssd_norm, ssd_out_proj, attn_w_qkv, attn_q_norm, attn_k_norm, attn_w_o):
    f32 = lambda a: np.ascontiguousarray(np.asarray(a, dtype=np.float32))
    x, meta_tokens = f32(x), f32(meta_tokens)
    ffn_norm, mix_norm = f32(ffn_norm), f32(mix_norm)
    shared = {"ffn_w_in": f32(ffn_w_in), "ffn_w_out": f32(ffn_w_out), "ssd_in_proj": f32(ssd_in_proj),
              "ssd_out_proj": f32(ssd_out_proj), "attn_w_qkv": f32(attn_w_qkv), "attn_w_o": f32(attn_w_o),
              "rot": rot_matrix(), "cst_in": ssd_consts(), "ident_in": np.eye(128, dtype=np.float32)}
    ssd_conv_w, ssd_conv_b, ssd_dt_bias = f32(ssd_conv_w), f32(ssd_conv_b), f32(ssd_dt_bias)
    ssd_A_log, ssd_D, ssd_norm = f32(ssd_A_log), f32(ssd_D), f32(ssd_norm)
    attn_q_norm, attn_k_norm = f32(attn_q_norm), f32(attn_k_norm)
    gam = np.zeros((128, 12, KC), np.float32)
    for l in range(4):
        for sl in range(2):
            gam[:, 2 * l + sl, :] = col_layout(ffn_norm[l, sl])
        gam[:, 8 + l, :] = col_layout(mix_norm[l])
    qkg = np.zeros((128, 2, 2), np.float32)
    for j in range(2):
        qkg[:, j, 0] = attn_q_norm[j]
        qkg[:, j, 1] = attn_k_norm[j]
    shared["gam_in"] = gam
    shared["nwT_in"] = np.ascontiguousarray(np.stack([col_layout(ssd_norm[j]) for j in range(2)], axis=0))
    shared["qkg"] = qkg
    cs = [rope_tables(q) for q in range(4)]
    sp = [ssd_params(hq, ssd_conv_w, ssd_conv_b, ssd_dt_bias, ssd_A_log, ssd_D, ssd_norm) for hq in range(4)]
    maps = []
    for c in range(8):
        b, q = c // 4, c % 4
        H = np.concatenate([x[b, q * 1024:(q + 1) * 1024], meta_tokens], axis=0)
        m = dict(shared)
        m["h0"] = np.ascontiguousarray(H.T.reshape(KC, 128, TT).transpose(1, 0, 2))
        m["cossin"] = cs[q]
        onehot = np.zeros((128, 4), np.float32)
        onehot[:, q] = 1.0
        m["sel_in"] = onehot
        m.update(sp[q])
        maps.append(m)
    if not _PROG:
        _PROG.append(build_program())
    res = run_bass_kernel_spmd(_PROG[0].nc, maps, core_ids=list(range(8))).results
    out = np.zeros((2, SEQ, D), np.float32)
    for c in range(8):
        b, q = c // 4, c % 4
        ho = np.asarray(res[c]["h_out"])
        out[b, q * 1024:(q + 1) * 1024] = ho.transpose(1, 0, 2).reshape(D, TT).T[0:1024]
    return out
```

```python
import contextlib
import math
import numpy as np
import ml_dtypes
import concourse.bass as bass
import concourse.mybir as mybir
from concourse.bass_utils import run_bass_kernel_spmd

F32 = mybir.dt.float32
BF16 = mybir.dt.bfloat16
AF = mybir.ActivationFunctionType
ALU = mybir.AluOpType
AX = mybir.AxisListType
NPBF = ml_dtypes.bfloat16

D = 2048
KC = 16
TT = 1040
NREAL = 1024
NMETA = 16
TBS = [(0, 512), (512, 512), (1024, 16)]
TTILES = [(i * 128, 128) for i in range(8)] + [(1024, 16)]
DFF = 6144
EPS = 1e-6
SEQ = 4096
SV = SEQ + NMETA
SP_ = SEQ + 128
NCH = 33


class _Op:
    __slots__ = ("eng", "fn", "deps", "ms", "dkey", "sem", "val", "clock", "idx", "inc", "is_cc")


class Prog:
    ENGS = ("pe", "act", "dve", "pool", "sp")

    def __init__(self):
        self.nc = bass.Bass("TRN2", target_bir_lowering=False)
        self.es = contextlib.ExitStack()
        self.phase = None
        self.phase_id = 0
        nc = self.nc
        self.e = {"pe": nc.tensor, "act": nc.scalar, "dve": nc.vector, "pool": nc.gpsimd, "sp": nc.sync}
        self.ops = []
        self.nops = 0
        self.lastw = {}
        self.readers = {}
        self.out_dmas = []
        self.last_on = {}
        self.dma_since = []
        self.sem = {e: self.es.enter_context(nc.semaphore("s_" + e)) for e in ("pe", "act", "dve", "pool")}
        self.cnt = {e: 0 for e in self.sem}
        self.dsem, self.dcnt = {}, {}
        self.known = {e: {} for e in self.ENGS}
        self.bar = self.es.enter_context(nc.sbuf_tensor("bar", [128, 8], F32))

    def dram(self, name, shape, dt, kind):
        return self.nc.dram_tensor(name, list(shape), dt, kind=kind).ap()

    def idram(self, name, shape, dt):
        return self.nc.dram_tensor(name, list(shape), dt).ap()

    def _stack(self):
        return self.phase if self.phase is not None else self.es

    def sb(self, name, shape, dt):
        return self._stack().enter_context(self.nc.sbuf_tensor("%s_p%d" % (name, self.phase_id), list(shape), dt))

    def ps(self, name, shape, dt=F32):
        return self._stack().enter_context(self.nc.psum_tensor("%s_p%d" % (name, self.phase_id), list(shape), dt))

    def begin_phase(self):
        self.phase = contextlib.ExitStack()
        self.phase_id += 1

    def end_phase(self, wait_cc=False):
        self.barrier(wait_cc=wait_cc)
        self.flush()
        self.phase.close()
        self.phase = None

    def op(self, eng, fn, r=(), w=(), dkey=None, inc=16, extra=()):
        o = _Op()
        o.inc = inc
        o.is_cc = False
        o.eng, o.fn, o.dkey = eng, fn, dkey
        o.ms = dkey is not None
        o.idx = self.nops
        self.nops += 1
        o.sem = o.val = o.clock = None
        deps = {}
        for d in extra:
            deps[d.idx] = d
        for k in r:
            lw = self.lastw.get(k)
            if lw is not None:
                deps[lw.idx] = lw
        for k in w:
            lw = self.lastw.get(k)
            if lw is not None:
                deps[lw.idx] = lw
            rd = self.readers.get(k)
            if rd:
                for x in rd.values():
                    if isinstance(x, list):
                        for y in x:
                            deps[y.idx] = y
                    else:
                        deps[x.idx] = x
        dl = []
        for i in sorted(deps, reverse=True):
            d = deps[i]
            if d.eng == "pe" and eng == "pe" and d.dkey is None and dkey is None:
                continue
            d.ms = True
            dl.append(d)
        o.deps = dl
        for k in w:
            self.lastw[k] = o
            self.readers[k] = {}
        for k in r:
            rd = self.readers.setdefault(k, {})
            if dkey is not None:
                rd.setdefault("dma", []).append(o)
            else:
                rd[eng] = o
        self.ops.append(o)
        if fn is not None:
            if dkey is not None:
                self.dma_since.append(o)
            else:
                self.last_on[eng] = o
        return o

    def dma(self, q, out, in_, r=(), w=(), key=None, is_out=False):
        fn = (lambda E=self.e[q], out=out, in_=in_: E.dma_start(out=out, in_=in_))
        o = self.op(q, fn, r=r, w=w, dkey=key)
        if is_out:
            self.out_dmas.append(o)
        return o

    def allgather(self, src, dst, r, w, key):
        nc = self.nc
        fn = (lambda: nc.gpsimd.collective_compute("AllGather", ALU.bypass, replica_groups=[[0, 1, 2, 3], [4, 5, 6, 7]],
                                                   ins=[src.opt()], outs=[dst.opt()]))
        o = self.op("pool", fn, r=r, w=w, dkey=key, inc=1)
        o.is_cc = True
        return o

    def barrier(self, wait_cc=False):
        nc = self.nc
        pend_cc = [] if wait_cc else [d for d in self.dma_since if d.is_cc]
        deps = [self.last_on[e] for e in ("pe", "act", "dve", "pool") if e in self.last_on] + \
               [d for d in self.dma_since if wait_cc or not d.is_cc]
        b = self.op("dve", lambda: nc.vector.memset(self.bar[:], 0.0), extra=deps)
        b.ms = True
        for e in ("pe", "act", "pool", "sp"):
            self.op(e, None, extra=[b])
        keep = {k: o for k, o in self.lastw.items() if o.is_cc and o in pend_cc}
        self.lastw.clear()
        self.readers.clear()
        self.lastw.update(keep)
        self.dma_since = list(pend_cc)

    def _handle(self, s):
        return self.sem[s] if s in self.sem else self.dsem[s]

    def _wait(self, engname, d):
        kn = self.known[engname]
        if kn.get(d.sem, 0) >= d.val:
            return
        self.e[engname].wait_ge(self._handle(d.sem), d.val)
        kn = dict(kn)
        for ks, kv in d.clock.items():
            if kn.get(ks, 0) < kv:
                kn[ks] = kv
        self.known[engname] = kn

    def flush(self):
        nc = self.nc
        for o in self.ops:
            for d in o.deps:
                self._wait(o.eng, d)
            if o.fn is None:
                continue
            ins = o.fn()
            if o.ms:
                if o.dkey is not None:
                    k = ("d", o.dkey)
                    if k not in self.dsem:
                        self.dsem[k] = self.es.enter_context(nc.semaphore("sd%d" % len(self.dsem)))
                        self.dcnt[k] = 0
                    self.dcnt[k] += o.inc
                    ins.then_inc(self.dsem[k], o.inc)
                    o.sem, o.val = k, self.dcnt[k]
                else:
                    self.cnt[o.eng] += 1
                    ins.then_inc(self.sem[o.eng], 1)
                    o.sem, o.val = o.eng, self.cnt[o.eng]
                c = dict(self.known[o.eng])
                c[o.sem] = o.val
                o.clock = c
        self.ops = []

    def emit(self):
        self.flush()
        for o in self.out_dmas:
            self._wait("sp", o)
        return self.nc


def _xbc_order():
    o = []
    for par in range(2):
        for G_ in range(par, 8, 2):
            o += [G_ * 4 + i for i in range(4)]
        o += [32 + G_ for G_ in range(par, 8, 2)]
        o += [40 + G_ for G_ in range(par, 8, 2)]
    return o


XBC_ORDER = _xbc_order()


class FeatX:
    def __init__(self, P, name, nchunks, order=None):
        self.P, self.name = P, name
        n = nchunks // 3
        self.send = [P.idram("s_%s%d" % (name, i), [384, TT], BF16) for i in range(n)]
        self.recv = [P.idram("r_%s%d" % (name, i), [1536, TT], BF16) for i in range(n)]
        self.keys = {i: [] for i in range(n)}
        order = list(range(nchunks)) if order is None else order
        self.pos = {f: i for i, f in enumerate(order)}

    def write(self, f, src, r, dkey):
        P = self.P
        i, j = self.pos[f] // 3, self.pos[f] % 3
        sk = ("snd", self.name, f)
        self.keys[i].append(sk)
        P.dma("sp", self.send[i][j * 128:(j + 1) * 128, :], src, r=r, w=[sk], key=dkey)
        if len(self.keys[i]) == 3:
            P.allgather(self.send[i], self.recv[i], r=self.keys[i], w=[("rcv", self.name, i)], key="cc_" + self.name)
            self.keys[i] = []

    def rd(self, q, f):
        i, j = self.pos[f] // 3, self.pos[f] % 3
        return self.recv[i][q * 384 + j * 128:q * 384 + (j + 1) * 128, :], ("rcv", self.name, i)


class TokX:
    RP = (512, 528)

    def __init__(self, P, name, ncb, width, dt):
        self.P, self.name = P, name
        self.send = [[P.idram("s_%s%d_%d" % (name, cb, pt), [self.RP[pt], width], dt) for pt in range(2)] for cb in range(ncb)]
        self.recv = [[P.idram("r_%s%d_%d" % (name, cb, pt), [4 * self.RP[pt], width], dt) for pt in range(2)] for cb in range(ncb)]
        self.keys = {(cb, pt): [] for cb in range(ncb) for pt in range(2)}

    @staticmethod
    def loc(ti):
        if ti < 4:
            return 0, ti * 128
        if ti < 8:
            return 1, (ti - 4) * 128
        return 1, 512

    def write(self, ti, cb, tn, src, r, dkey):
        P = self.P
        pt, r0 = self.loc(ti)
        sk = ("snd", self.name, cb, ti)
        self.keys[(cb, pt)].append(sk)
        P.dma("sp", self.send[cb][pt][r0:r0 + tn, :], src, r=r, w=[sk], key=dkey)
        if len(self.keys[(cb, pt)]) == (4 if pt == 0 else 5):
            P.allgather(self.send[cb][pt], self.recv[cb][pt], r=self.keys[(cb, pt)], w=[("rcv", self.name, cb, pt)],
                        key="cc_" + self.name)
            self.keys[(cb, pt)] = []

    def rd(self, q, ti, cb, tn=128):
        pt, r0 = self.loc(ti)
        R = self.RP[pt]
        return self.recv[cb][pt][q * R + r0:q * R + r0 + tn, :], ("rcv", self.name, cb, pt)

    def rd4(self, q, pt, cb):
        R = self.RP[pt]
        return self.recv[cb][pt][q * R:q * R + 512, :], ("rcv", self.name, cb, pt)


class GX:
    def __init__(self, P):
        self.P = P
        w = [128] + [512] * 8
        self.send = [P.idram("s_g%d" % i, [1024, w[i]], BF16) for i in range(9)]
        self.recv = [P.idram("r_g%d" % i, [4096, w[i]], BF16) for i in range(9)]
        self.keys = {i: [] for i in range(9)}

    def write(self, g, c, src, r, dkey):
        P = self.P
        i = 0 if c == 0 else 1 + (c - 1) // 4
        c0 = 0 if c == 0 else ((c - 1) % 4) * 128
        sk = ("snd_g", g, c)
        self.keys[i].append(sk)
        P.dma("sp", self.send[i][g * 512:(g + 1) * 512, c0:c0 + 128].rearrange("(i p) t -> p i t", p=128), src,
              r=r, w=[sk], key=dkey)
        if len(self.keys[i]) == (2 if i == 0 else 8):
            P.allgather(self.send[i], self.recv[i], r=self.keys[i], w=[("rcv_g", i)], key="cc_g")
            self.keys[i] = []

    def rd(self, r_, jj, k, hf):
        i = 1 + 2 * k + hf
        return self.recv[i][r_ * 1024 + jj * 128:r_ * 1024 + (jj + 1) * 128, :], ("rcv_g", i)

    def rd_meta(self, r_, jj):
        return self.recv[0][r_ * 1024 + jj * 128:r_ * 1024 + (jj + 1) * 128, 112:128], ("rcv_g", 0)


class OX:
    def __init__(self, P):
        self.P = P
        w = [1024] * 4 + [16]
        self.send = [P.idram("s_o%d" % i, [512, w[i]], BF16) for i in range(5)]
        self.recv = [P.idram("r_o%d" % i, [2048, w[i]], BF16) for i in range(5)]
        self.keys = {i: [] for i in range(5)}

    def write(self, j, qb, qn, src, r, dkey):
        P = self.P
        i = qb // 2 if qb < 8 else 4
        c0 = (qb % 2) * 512 if qb < 8 else 0
        sk = ("snd_o", j, qb)
        self.keys[i].append(sk)
        P.dma("sp", self.send[i][j * 128:(j + 1) * 128, c0:c0 + qn], src, r=r, w=[sk], key=dkey)
        if len(self.keys[i]) == (8 if i < 4 else 4):
            P.allgather(self.send[i], self.recv[i], r=self.keys[i], w=[("rcv_o", i)], key="cc_o")
            self.keys[i] = []

    def rd(self, r_, jj, k):
        return self.recv[k][r_ * 512 + jj * 128:r_ * 512 + (jj + 1) * 128, :], ("rcv_o", k)

    def rd_meta(self, r_, jj):
        return self.recv[4][r_ * 512 + jj * 128:r_ * 512 + (jj + 1) * 128, :], ("rcv_o", 4)


class Sel:
    def __init__(self, P, G, stg, stg_keys, get_ps):
        nc = P.nc
        self.P, self.stg, self.stg_keys, self.get_ps = P, stg, stg_keys, get_ps
        self.sel = P.sb("sel_sb", [128, 4], F32)
        idf = P.sb("sel_idf", [128, 128], F32)
        self.selI = P.sb("selI", [128, 4, 128], BF16)
        P.dma("sp", self.sel[:], G.sel_in, w=["sel"], key="sel")
        P.dma("sp", idf[:], G.ident_in, w=["sel_idf"], key="sel_idf")
        for k in range(4):
            P.op("dve", lambda k=k: nc.vector.tensor_scalar(out=self.selI[:, k, :], in0=idf[:],
                                                            scalar1=self.sel[:, k:k + 1], scalar2=None, op0=ALU.mult),
                 r=["sel", "sel_idf"], w=["selI"])
        for sl in range(2):
            for k in range(4):
                P.op("dve", lambda sl=sl, k=k: nc.vector.memset(self.stg[sl][:, k, :], 0.0), w=self.stg_keys(sl, k))
        self.i = 0

    def load(self, dst, cands, ncols, r, w, view=None, prow=(0, 128)):
        P = self.P
        nc = P.nc
        slot = self.i % 2
        self.i += 1
        p0, p1 = prow
        for k in range(4):
            tgt = self.stg[slot][p0:p1, k, 0:ncols]
            if view is not None:
                tgt = view(tgt)
            P.dma("sp", tgt, cands[k], r=r, w=self.stg_keys(slot, k), key=("selstg", slot, k))
        for t0 in range(0, ncols, 512):
            tn = min(512, ncols - t0)
            ps, pkey = self.get_ps()

            def mm(ps=ps, slot=slot, t0=t0, tn=tn):
                ins = None
                for k in range(4):
                    ins = nc.tensor.matmul(ps[:, 0:tn], lhsT=self.selI[:, k, :], rhs=self.stg[slot][:, k, t0:t0 + tn],
                                           start=(k == 0), stop=(k == 3))
                return ins
            P.op("pe", mm, r=["selI"] + [kk for k in range(4) for kk in self.stg_keys(slot, k)], w=[pkey])
            P.op("act", lambda ps=ps, t0=t0, tn=tn: nc.scalar.copy(out=dst[:, t0:t0 + tn], in_=ps[:, 0:tn]),
                 r=[pkey], w=w)


class TPhase:
    def __init__(self, P, G, pre, ffns, post, h_src, h_dst, h_dst_is_out, mixj):
        self.P = P
        self.G = G
        self.pre, self.post = pre, post
        self.n_ffn = len(ffns)
        self.h_in = h_src
        self.h_out = h_dst
        self.h_dst_is_out = h_dst_is_out
        self.w_in = [G.ffn_w_in[l, sl] for (l, sl) in ffns]
        self.w_out = [G.ffn_w_out[l, sl] for (l, sl) in ffns]
        self.mixj = mixj
        self.h = P.sb("h", [128, KC, TT], F32)
        self.big = P.sb("big", [128, 32, TT], BF16)
        self.wb = [P.sb("wb%d" % i, [128, KC, 512], BF16) for i in range(3)]
        self.wslot = 0
        self.gam = P.sb("gam_sb", [128, 12, KC], F32)
        self.rstd = P.sb("rstd", [128, TT], F32)
        self.sq = [P.sb("sq%d" % i, [128, 512], BF16) for i in range(2)]
        self.sg = [P.sb("sg%d" % i, [128, 512], F32) for i in range(2)]
        self.qf = self.sg
        self.ones = P.sb("ones", [128, 128], BF16)
        self.pmm = [P.ps("pmm%d" % i, [128, 512]) for i in range(6)]
        self.pmisc = [P.ps("pmisc%d" % i, [128, 512]) for i in range(2)]
        self.pi = 0
        self.npmm = 6
        self.mi = 0
        self.si = 0
        nc = P.nc
        P.op("dve", lambda: nc.vector.memset(self.ones[:], 1.0), w=["ones"])
        self.epsb = P.sb("epsb", [128, 1], F32)
        P.op("dve", lambda: nc.vector.memset(self.epsb[:], EPS), w=["epsb"])
        P.dma("sp", self.gam[:], G.gam_in, w=["gam"], key="gam")
        P.dma("sp", self.h[:], self.h_in, w=[("h", kc, tb) for kc in range(KC) for tb in range(3)], key="h")

    def next_w(self):
        s = self.wslot
        self.wslot = (s + 1) % 3
        return s

    def next_p(self):
        i = self.pi % self.npmm
        self.pi = (i + 1) % self.npmm
        return i

    def next_m(self):
        i = self.mi
        self.mi = (i + 1) % 2
        return i

    def load_w(self, slot, W, r0, c0, ncols, col_off=0, nk=KC):
        P = self.P
        src = W[r0:r0 + nk * 128, c0:c0 + ncols].rearrange("(kc p) f -> p kc f", p=128)
        P.dma("pool", self.wb[slot][:, 0:nk, col_off:col_off + ncols], src,
              w=[("wb", slot, col_off // 256 + i) for i in range(max(1, ncols // 256))], key=("wb", slot, col_off // 256))

    def mm_group(self, ps_ap, pkey, terms, extra_r=()):
        P = self.P
        nc = P.nc
        n = len(terms)

        def fn():
            ins = None
            for i, (l, rr) in enumerate(terms):
                ins = nc.tensor.matmul(ps_ap, lhsT=l, rhs=rr, start=(i == 0), stop=(i == n - 1))
            return ins
        return P.op("pe", fn, r=list(extra_r), w=[pkey])

    def rmsnorm(self, gi, dst_kc0=0):
        P = self.P
        nc = P.nc
        for tb, (t0, tn) in enumerate(TBS):
            m = self.next_m()
            pm = self.pmisc[m]
            for kc in range(KC):
                s = self.si
                self.si = (s + 1) % 2
                P.op("act", lambda s=s, kc=kc, t0=t0, tn=tn: nc.scalar.activation(
                    out=self.sq[s][:, 0:tn], in_=self.h[:, kc, t0:t0 + tn], func=AF.Square),
                    r=[("h", kc, tb)], w=[("sq", s)])
                P.op("pe", lambda s=s, kc=kc, tn=tn, pm=pm: nc.tensor.matmul(
                    pm[:, 0:tn], lhsT=self.ones[:, :], rhs=self.sq[s][:, 0:tn], start=(kc == 0), stop=(kc == KC - 1)),
                    r=[("sq", s), "ones"], w=[("pmisc", m)])
            P.op("act", lambda t0=t0, tn=tn, pm=pm: nc.scalar.activation(
                out=self.rstd[:, t0:t0 + tn], in_=pm[:, 0:tn], func=AF.Sqrt, bias=self.epsb[:, 0:1], scale=1.0 / D),
                r=[("pmisc", m), "epsb"], w=[("rstd", tb)])
            P.op("dve", lambda t0=t0, tn=tn: nc.vector.reciprocal(
                out=self.rstd[:, t0:t0 + tn], in_=self.rstd[:, t0:t0 + tn]),
                r=[("rstd", tb)], w=[("rstd", tb)])
            for kc in range(KC):
                P.op("dve", lambda kc=kc, t0=t0, tn=tn: nc.vector.scalar_tensor_tensor(
                    out=self.big[:, dst_kc0 + kc, t0:t0 + tn], in0=self.h[:, kc, t0:t0 + tn],
                    scalar=self.gam[:, gi, kc:kc + 1], in1=self.rstd[:, t0:t0 + tn],
                    op0=ALU.mult, op1=ALU.mult),
                    r=[("h", kc, tb), ("rstd", tb), "gam"], w=[("big", dst_kc0 + kc, tb)])

    def ffn(self, fi, gi):
        P = self.P
        nc = P.nc
        W1, W2 = self.w_in[fi], self.w_out[fi]
        self.rmsnorm(gi)
        for j in range(3):
            for fg in range(8):
                slot = self.next_w()
                c0 = j * 2048 + fg * 256
                self.load_w(slot, W1, 0, c0, 256, 0)
                self.load_w(slot, W1, 0, DFF + c0, 256, 256)
                for f2 in range(2):
                    fc = fg * 2 + f2
                    for tb, (t0, tn) in enumerate(TBS):
                        pg, pu = self.next_p(), self.next_p()
                        self.mm_group(self.pmm[pg][:, 0:tn], ("pmm", pg),
                                      [(self.wb[slot][:, kc, f2 * 128:(f2 + 1) * 128], self.big[:, kc, t0:t0 + tn])
                                       for kc in range(KC)],
                                      extra_r=[("wb", slot, 0)] + [("big", kc, tb) for kc in range(KC)])
                        self.mm_group(self.pmm[pu][:, 0:tn], ("pmm", pu),
                                      [(self.wb[slot][:, kc, 256 + f2 * 128:256 + (f2 + 1) * 128],
                                        self.big[:, kc, t0:t0 + tn]) for kc in range(KC)],
                                      extra_r=[("wb", slot, 1)] + [("big", kc, tb) for kc in range(KC)])
                        s = self.si
                        self.si = (s + 1) % 2
                        P.op("act", lambda s=s, pg=pg, tn=tn: nc.scalar.activation(
                            out=self.sg[s][:, 0:tn], in_=self.pmm[pg][:, 0:tn], func=AF.Silu),
                            r=[("pmm", pg)], w=[("sg", s)])
                        P.op("dve", lambda s=s, pu=pu, fc=fc, t0=t0, tn=tn: nc.vector.tensor_tensor(
                            out=self.big[:, 16 + fc, t0:t0 + tn], in0=self.sg[s][:, 0:tn],
                            in1=self.pmm[pu][:, 0:tn], op=ALU.mult),
                            r=[("sg", s), ("pmm", pu)], w=[("big", 16 + fc, tb)])
            for dg in range(4):
                slot = self.next_w()
                self.load_w(slot, W2, j * 2048, dg * 512, 512, 0)
                for di in range(4):
                    dc = dg * 4 + di
                    for tb, (t0, tn) in enumerate(TBS):
                        p = self.next_p()
                        self.mm_group(self.pmm[p][:, 0:tn], ("pmm", p),
                                      [(self.wb[slot][:, kc, di * 128:(di + 1) * 128],
                                        self.big[:, 16 + kc, t0:t0 + tn]) for kc in range(KC)],
                                      extra_r=[("wb", slot, 0), ("wb", slot, 1)] +
                                      [("big", 16 + kc, tb) for kc in range(KC)])
                        P.op("dve", lambda p=p, dc=dc, t0=t0, tn=tn: nc.vector.scalar_tensor_tensor(
                            out=self.h[:, dc, t0:t0 + tn], in0=self.pmm[p][:, 0:tn], scalar=0.5,
                            in1=self.h[:, dc, t0:t0 + tn], op0=ALU.mult, op1=ALU.add),
                            r=[("pmm", p), ("h", dc, tb)], w=[("h", dc, tb)])

    def store_h(self):
        P = self.P
        P.dma("sp", self.h_out, self.h[:], r=[("h", kc, tb) for kc in range(KC) for tb in range(3)],
              w=["h_dram"], key="hout", is_out=self.h_dst_is_out)

    def tok_major_proj(self, W, c0, ncols, out_dram, out_c0, dt_out, stg_name):
        P = self.P
        nc = P.nc
        slot = self.next_w()
        if ncols >= 256:
            self.load_w(slot, W, 0, c0, ncols, 0)
            wres = [("wb", slot, i) for i in range(ncols // 256)]
        else:
            self.load_w(slot, W, 0, c0, ncols, 0)
            wres = [("wb", slot, 0)]
        stg = self.tstg[stg_name]
        skey = (lambda s_: ("big", 18 + s_, 0)) if stg_name == "tstg" else (lambda s_: ("dstg", s_))
        for ti, (t0, tn) in enumerate(TTILES):
            tb = min(ti // 4, 2)
            p = self.next_p()
            self.mm_group(self.pmm[p][0:tn, 0:ncols], ("pmm", p),
                          [(self.big[:, kc, t0:t0 + tn], self.wb[slot][:, kc, 0:ncols]) for kc in range(KC)],
                          extra_r=wres + [("big", kc, tb) for kc in range(KC)])
            s = self.tsi
            self.tsi = (s + 1) % 2
            P.op("act", lambda s=s, p=p, tn=tn, stg=stg: nc.scalar.copy(
                out=stg[s][0:tn, 0:ncols], in_=self.pmm[p][0:tn, 0:ncols]),
                r=[("pmm", p)], w=[skey(s)])
            out_dram.write(ti, out_c0 // 512, tn, stg[s][0:tn, 0:ncols], r=[skey(s)], dkey=(stg_name, s))

    def post_attn(self, gi):
        P = self.P
        nc = P.nc
        W = self.w_mix_in
        self.rmsnorm(gi)
        units = []
        for fg in range(6):
            for fi in range(4):
                f = fg * 4 + fi
                for tb, (t0, tn) in enumerate(TBS):
                    units.append(dict(fg=fg, fi=fi, f=f, gcol=0 if f < 16 else 1, st=f % 2, tb=tb, t0=t0, tn=tn))
        slots = {}
        qf3 = self.qf + [self.qf2]

        def stage_a(i):
            u = units[i]
            fg, fi, tb, t0, tn = u["fg"], u["fi"], u["tb"], u["t0"], u["tn"]
            for fg_ in (fg, fg + 1):
                if fg_ < 6 and fg_ not in slots:
                    slots[fg_] = self.next_w()
                    self.load_w(slots[fg_], W, 0, fg_ * 512, 512, 0)
            slot = slots[fg]
            p = self.next_p()
            self.mm_group(self.pmm[p][:, 0:tn], ("pmm", p),
                          [(self.wb[slot][:, kc, fi * 128:(fi + 1) * 128], self.big[:, kc, t0:t0 + tn])
                           for kc in range(KC)],
                          extra_r=[("wb", slot, 0), ("wb", slot, 1)] + [("big", kc, tb) for kc in range(KC)])
            s2, s3 = i % 2, i % 3
            P.op("act", lambda: nc.scalar.copy(out=qf3[s3][:, 0:tn], in_=self.pmm[p][:, 0:tn]),
                 r=[("pmm", p)], w=[("sg", s3)])
            P.op("act", lambda: nc.scalar.activation(out=self.sq[s2][:, 0:tn], in_=self.pmm[p][:, 0:tn], func=AF.Square),
                 r=[("pmm", p)], w=[("sq", s2)])

        def stage_b(i):
            u = units[i]
            tn, gcol = u["tn"], u["gcol"]
            s2, s3 = i % 2, i % 3
            m = 0
            pm = self.pmisc[m]
            P.op("pe", lambda: nc.tensor.matmul(pm[:, 0:tn], lhsT=self.ones[:, :], rhs=self.sq[s2][:, 0:tn],
                                                start=True, stop=True),
                 r=[("sq", s2), "ones"], w=[("pmisc", m)])
            P.op("act", lambda: nc.scalar.activation(out=self.qr[s2][:, 0:tn], in_=pm[:, 0:tn], func=AF.Sqrt,
                                                     bias=self.epsb[:, 0:1], scale=1.0 / 128),
                 r=[("pmisc", m), "epsb"], w=[("rstd", s2)])
            P.op("dve", lambda: nc.vector.reciprocal(out=self.qr[s2][:, 0:tn], in_=self.qr[s2][:, 0:tn]),
                 r=[("rstd", s2)], w=[("rstd", s2)])
            P.op("dve", lambda: nc.vector.scalar_tensor_tensor(
                out=self.qn[s2][:, 0:tn], in0=qf3[s3][:, 0:tn], scalar=self.qkg[:, gcol:gcol + 1],
                in1=self.qr[s2][:, 0:tn], op0=ALU.mult, op1=ALU.mult),
                r=[("sg", s3), ("rstd", s2), "qkg"], w=[("qn", s2)])

        def stage_c(i):
            u = units[i]
            f, st, tb, t0, tn = u["f"], u["st"], u["tb"], u["t0"], u["tn"]
            s2, s3 = i % 2, i % 3
            m2 = 1
            pm2 = self.pmisc[m2]
            P.op("pe", lambda: nc.tensor.matmul(pm2[:, 0:tn], lhsT=self.rot[:, :], rhs=self.qn[s2][:, 0:tn],
                                                start=True, stop=True),
                 r=[("qn", s2), "rot"], w=[("pmisc", m2)])
            P.op("dve", lambda: nc.vector.tensor_tensor(out=qf3[s3][:, 0:tn], in0=self.qn[s2][:, 0:tn],
                                                        in1=self.cos[:, t0:t0 + tn], op=ALU.mult),
                 r=[("qn", s2), "cos"], w=[("sg", s3)])
            P.op("dve", lambda: nc.vector.tensor_tensor(out=self.qn[s2][:, 0:tn], in0=pm2[:, 0:tn],
                                                        in1=self.sin[:, t0:t0 + tn], op=ALU.mult),
                 r=[("pmisc", m2), "sin"], w=[("qn", s2)])
            P.op("dve", lambda: nc.vector.tensor_tensor(out=self.fstg[st][:, t0:t0 + tn], in0=qf3[s3][:, 0:tn],
                                                        in1=self.qn[s2][:, 0:tn], op=ALU.add),
                 r=[("sg", s3), ("qn", s2)], w=[("big", 16 + st, tb)])
            if tb == 2:
                self.mix_out_f.write(f, self.fstg[st][:, :], r=[("big", 16 + st, tb_) for tb_ in range(3)],
                                     dkey=("fstg", st))

        n = len(units)
        for i in range(n + 2):
            if i < n:
                stage_a(i)
            if 1 <= i <= n:
                stage_b(i - 1)
            if i >= 2:
                stage_c(i - 2)
        for vg in range(2):
            self.tok_major_proj(W, 3072 + vg * 512, 512, self.mix_out_t, vg * 512, BF16, "tstg")

    def post_ssd(self, gi):
        P = self.P
        nc = P.nc
        W = self.w_mix_in
        self.rmsnorm(gi)
        self.tok_major_proj(W, 10240, 128, self.mix_out_dt, 0, F32, "dstg")
        segs = []
        for f in XBC_ORDER:
            if not segs or segs[-1][0] != f // 4:
                segs.append([f // 4, []])
            segs[-1][1].append(f)
        seg_slot = {}

        def want(si):
            if si < len(segs) and si not in seg_slot:
                seg_slot[si] = self.next_w()
                self.load_w(seg_slot[si], W, 0, 4096 + segs[si][0] * 512, 512, 0)
        seg_of = {}
        for si, (fg_, fl) in enumerate(segs):
            for f in fl:
                seg_of[f] = si
        want(0)
        for fpos, f in enumerate(XBC_ORDER):
            fg, fi = f // 4, f % 4
            si = seg_of[f]
            if f == segs[si][1][0]:
                want(si + 1)
            slot = seg_slot[si]
            st = fpos % 2
            for tb, (t0, tn) in enumerate(TBS):
                p = self.next_p()
                self.mm_group(self.pmm[p][:, 0:tn], ("pmm", p),
                              [(self.wb[slot][:, kc, fi * 128:(fi + 1) * 128], self.big[:, kc, t0:t0 + tn])
                               for kc in range(KC)],
                              extra_r=[("wb", slot, 0), ("wb", slot, 1)] + [("big", kc, tb) for kc in range(KC)])
                P.op("act", lambda p=p, st=st, t0=t0, tn=tn: nc.scalar.copy(
                    out=self.fstg[st][:, t0:t0 + tn], in_=self.pmm[p][:, 0:tn]),
                    r=[("pmm", p)], w=[("big", 16 + st, tb)])
            self.mix_out_f.write(f, self.fstg[st][:, :], r=[("big", 16 + st, tb) for tb in range(3)], dkey=("fstg", st))
        for fg in range(8):
            slot = self.next_w()
            self.load_w(slot, W, 0, fg * 512, 512, 0)
            for fi in range(4):
                f = fg * 4 + fi
                st = f % 2
                for tb, (t0, tn) in enumerate(TBS):
                    p = self.next_p()
                    self.mm_group(self.pmm[p][:, 0:tn], ("pmm", p),
                                  [(self.wb[slot][:, kc, fi * 128:(fi + 1) * 128], self.big[:, kc, t0:t0 + tn])
                                   for kc in range(KC)],
                                  extra_r=[("wb", slot, 0), ("wb", slot, 1)] + [("big", kc, tb) for kc in range(KC)])
                    P.op("act", lambda p=p, st=st, t0=t0, tn=tn: nc.scalar.activation(
                        out=self.fstg[st][:, t0:t0 + tn], in_=self.pmm[p][:, 0:tn], func=AF.Silu),
                        r=[("pmm", p)], w=[("big", 16 + st, tb)])
                P.dma("sp", self.G.zpark[f], self.fstg[st][:, :], r=[("big", 16 + st, tb) for tb in range(3)],
                      w=[("zpark", f)], key=("fstg", st))

    def pre_mix(self, nkc):
        P = self.P
        nc = P.nc
        G = self.G
        W = self.w_mix_out
        if self.pre == "ssd":
            self.nwT = P.sb("nwT", [128, 32], F32)
            P.dma("sp", self.nwT[:], G.nwT_in[self.mixj[0]], w=["nwT"], key="nwT")
            self.pgn = [self.pmisc[0], self.pmisc[1], self.pmm[5]]
            self.npmm = 5
        stg = [self.big[:, 16 + 4 * i:20 + 4 * i, 0:1024] for i in range(2)]
        sel = Sel(P, G, stg, lambda slot, k: [("big", 16 + 4 * slot + k, 0), ("big", 16 + 4 * slot + k, 1)],
                  lambda: (lambda p: (self.pmm[p], ("pmm", p)))(self.next_p()))
        pgk = [("pmisc", 0), ("pmisc", 1), ("pmm", 5)]
        for hh in range(nkc // KC):
            for kk in range(KC):
                kc = hh * KC + kk
                if self.pre == "attn":
                    r_, jj = kc // 4, kc % 4
                    cc = [G.x_o.rd(r_, jj, k) for k in range(4)]
                    sel.load(self.big[:, kk, 0:1024], [c_[0] for c_ in cc], 1024, r=[c_[1] for c_ in cc],
                             w=[("big", kk, 0), ("big", kk, 1)])
                    meta, mk = G.x_o.rd_meta(r_, jj)
                else:
                    r_, jj = kc // 8, kc % 8
                    for hf in range(2):
                        cc = [G.x_g.rd(r_, jj, k, hf) for k in range(4)]
                        sel.load(self.big[:, kk, hf * 512:(hf + 1) * 512], [c_[0] for c_ in cc], 512,
                                 r=[c_[1] for c_ in cc], w=[("big", kk, hf)])
                    meta, mk = G.x_g.rd_meta(r_, jj)
                P.dma("sp", self.big[:, kk, 1024:1040], meta, r=[mk], w=[("big", kk, 2)], key=("mixin_m", kk % 4))
                if self.pre == "ssd":
                    zs = kk % 2
                    zst = self.big[:, 24 + zs, :]
                    P.dma("sp", zst, G.zpark[kc], w=[("big", 24 + zs, tb) for tb in range(3)], key=("zst", zs))
                    for tb, (t0, tn) in enumerate(TBS):
                        P.op("dve", lambda kk=kk, zst=zst, t0=t0, tn=tn: nc.vector.tensor_tensor(
                            out=self.big[:, kk, t0:t0 + tn], in0=self.big[:, kk, t0:t0 + tn], in1=zst[:, t0:t0 + tn],
                            op=ALU.mult), r=[("big", kk, tb), ("big", 24 + zs, tb)], w=[("big", kk, tb)])
                        s_ = self.si
                        self.si = (s_ + 1) % 2
                        P.op("act", lambda kk=kk, s_=s_, t0=t0, tn=tn: nc.scalar.activation(
                            out=self.sq[s_][:, 0:tn], in_=self.big[:, kk, t0:t0 + tn], func=AF.Square),
                            r=[("big", kk, tb)], w=[("sq", s_)])
                        P.op("pe", lambda kk=kk, s_=s_, tb=tb, tn=tn: nc.tensor.matmul(
                            self.pgn[tb][:, 0:tn], lhsT=self.ones[:, :], rhs=self.sq[s_][:, 0:tn],
                            start=(kk % 4 == 0), stop=(kk % 4 == 3)), r=[("sq", s_), "ones"], w=[pgk[tb]])
                    if kk % 4 == 3:
                        for tb, (t0, tn) in enumerate(TBS):
                            P.op("act", lambda tb=tb, t0=t0, tn=tn: nc.scalar.activation(
                                out=self.rstd[:, t0:t0 + tn], in_=self.pgn[tb][:, 0:tn], func=AF.Sqrt,
                                bias=self.epsb[:, 0:1], scale=1.0 / 512), r=[pgk[tb], "epsb"], w=[("rstd", tb)])
                            P.op("dve", lambda t0=t0, tn=tn: nc.vector.reciprocal(
                                out=self.rstd[:, t0:t0 + tn], in_=self.rstd[:, t0:t0 + tn]),
                                r=[("rstd", tb)], w=[("rstd", tb)])
                            for k4 in range(kk - 3, kk + 1):
                                kc4 = hh * KC + k4
                                P.op("dve", lambda k4=k4, kc4=kc4, t0=t0, tn=tn: nc.vector.scalar_tensor_tensor(
                                    out=self.big[:, k4, t0:t0 + tn], in0=self.big[:, k4, t0:t0 + tn],
                                    scalar=self.nwT[:, kc4:kc4 + 1], in1=self.rstd[:, t0:t0 + tn],
                                    op0=ALU.mult, op1=ALU.mult),
                                    r=[("big", k4, tb), ("rstd", tb), "nwT"], w=[("big", k4, tb)])
            for dg in range(4):
                slot = self.next_w()
                self.load_w(slot, W, hh * 2048, dg * 512, 512, 0)
                for di in range(4):
                    dc = dg * 4 + di
                    for tb, (t0, tn) in enumerate(TBS):
                        p = self.next_p()
                        self.mm_group(self.pmm[p][:, 0:tn], ("pmm", p),
                                      [(self.wb[slot][:, kc, di * 128:(di + 1) * 128],
                                        self.big[:, kc, t0:t0 + tn]) for kc in range(KC)],
                                      extra_r=[("wb", slot, 0), ("wb", slot, 1)] +
                                      [("big", kc, tb) for kc in range(KC)])
                        P.op("dve", lambda p=p, dc=dc, t0=t0, tn=tn: nc.vector.tensor_tensor(
                            out=self.h[:, dc, t0:t0 + tn], in0=self.pmm[p][:, 0:tn],
                            in1=self.h[:, dc, t0:t0 + tn], op=ALU.add),
                            r=[("pmm", p), ("h", dc, tb)], w=[("h", dc, tb)])
        self.npmm = 6

    def setup_mix(self):
        P = self.P
        nc = P.nc
        G = self.G
        j = self.mixj
        self.tsi = 0
        self.sent = []
        if self.pre == "attn":
            self.w_mix_out = G.attn_w_o[j[0]]
        elif self.pre == "ssd":
            self.w_mix_out = G.ssd_out_proj[j[0]]
        if self.post == "attn":
            self.w_mix_in = G.attn_w_qkv[j[1]]
            self.mix_out_f = G.x_qk
            self.mix_out_t = G.x_v
            self.cs = P.sb("cs", [128, 2, TT], F32)
            self.cos = self.cs[:, 0, :]
            self.sin = self.cs[:, 1, :]
            self.rot = P.sb("rot_sb", [128, 128], F32)
            self.qkg = P.sb("qkg_sb", [128, 2], F32)
            self.qn = [P.sb("qn%d" % i, [128, 512], F32) for i in range(2)]
            self.qf2 = P.sb("qf2", [128, 512], F32)
            self.qr = [self.rstd[:, i * 512:(i + 1) * 512] for i in range(2)]
            P.dma("sp", self.cs[:], G.cossin, w=["cos", "sin"], key="cs")
            P.dma("sp", self.rot[:], G.rot, w=["rot"], key="rot")
            P.dma("sp", self.qkg[:], G.qkg[:, j[1], :], w=["qkg"], key="qkg")
        elif self.post == "ssd":
            self.w_mix_in = G.ssd_in_proj[j[1]]
            self.mix_out_f = G.x_xbc
            self.mix_out_dt = G.x_dt
        if self.post is not None:
            self.fstg = [self.big[:, 16 + i, :] for i in range(2)]
            self.tstg = {"tstg": [self.big[:, 18 + i, 0:512] for i in range(2)],
                         "dstg": [P.sb("dstg%d" % i, [128, 128], F32) for i in range(2)]}

    def exchange(self):
        pass


def run_T(P, G, pre, ffns, post, h_src, h_dst, h_dst_is_out, mixj):
    P.begin_phase()
    T = TPhase(P, G, pre, ffns, post, h_src, h_dst, h_dst_is_out, mixj)
    T.setup_mix()
    if pre == "attn":
        T.pre_mix(16)
    elif pre == "ssd":
        T.pre_mix(32)
    for i, (l, sl) in enumerate(ffns):
        T.ffn(i, 2 * l + sl)
    if post == "attn":
        T.post_attn(8 + mixj[2])
    elif post == "ssd":
        T.post_ssd(8 + mixj[2])
    T.store_h()
    T.exchange()
    P.end_phase()


QBS = [(i * 512, 512) for i in range(8)] + [(4096, 16)]
KCS = [(i * 128, 128) for i in range(32)] + [(4096, 16)]


def run_HA(P, G):
    P.begin_phase()
    nc = P.nc
    qT = P.sb("qT", [128, 4, SV], BF16)
    kT = P.sb("kT", [128, 2, SV], BF16)
    v = P.sb("v", [128, 33, 256], BF16)
    ones = P.sb("ones", [128, 128], BF16)
    pt = [P.sb("pt%d" % i, [128, 512], BF16) for i in range(3)]
    rl = [P.sb("rl%d" % i, [128, 512], F32) for i in range(2)]
    ost = [P.sb("ost%d" % i, [128, 512], BF16) for i in range(2)]
    pss = [P.ps("pss%d" % i, [128, 512]) for i in range(3)]
    po = [P.ps("po%d" % i, [128, 512]) for i in range(2)]
    pl = [P.ps("pl%d" % i, [128, 512]) for i in range(2)]
    P.op("dve", lambda: nc.vector.memset(ones[:], 1.0), w=["ones"])
    P.op("dve", lambda: nc.vector.memset(v[:, 32, :], 0.0), w=["v"])
    selstg = [P.sb("selstg%d" % i, [128, 4, 1024], BF16) for i in range(2)]
    psel = P.ps("psel", [128, 512])
    sel = Sel(P, G, selstg, lambda slot, k: [("selstg", slot, k)], lambda: (psel, "psel"))
    def ld_feat(dst3, idx, fsel, kname):
        for q in range(4):
            cc = [G.x_qk.rd(q, fsel(k)) for k in range(4)]
            sel.load(dst3[:, idx, q * 1024:(q + 1) * 1024], [c_[0][:, 0:1024] for c_ in cc], 1024,
                     r=[c_[1] for c_ in cc], w=[(kname, idx, q)])
        cc = [G.x_qk.rd(0, fsel(k)) for k in range(4)]
        sel.load(dst3[:, idx, 4096:4112], [c_[0][:, 1024:1040] for c_ in cc], 16, r=[c_[1] for c_ in cc],
                 w=[(kname, idx, 4)])
    for j in range(4):
        ld_feat(qT, j, lambda k, j=j: 4 * k + j, "q")
    for g in range(2):
        ld_feat(kT, g, lambda k, g=g: 16 + 2 * k + g, "k")
    v3d = lambda ap: ap.rearrange("p (c d) -> p c d", d=256)
    for q in range(4):
        for hf in range(2):
            c0 = 8 * q + 4 * hf
            cc = [G.x_v.rd4(q, hf, k // 2) for k in range(4)]
            sel.load(v[:, c0:c0 + 4, :].rearrange("p c d -> p (c d)"),
                     [cc[k][0][:, (k % 2) * 256:(k % 2 + 1) * 256].rearrange("(c p) d -> p c d", p=128) for k in range(4)],
                     1024, r=[c_[1] for c_ in cc] + ["v"], w=[("v", q)], view=v3d)
    cc = [G.x_v.rd(0, 8, k // 2, 16) for k in range(4)]
    sel.load(v[:, 32, :], [cc[k][0][:, (k % 2) * 256:(k % 2 + 1) * 256] for k in range(4)], 256,
             r=[c_[1] for c_ in cc] + ["v"], w=[("v", 4)], prow=(0, 16))
    QK = [("q", j, q) for j in range(4) for q in range(5)]
    sent = []
    scale = 128 ** -0.5
    its = []
    ob = 0
    for qb, (q0, qn) in enumerate(QBS):
        for g in range(2):
            for jj in range(2):
                j = 2 * g + jj
                a = ob % 2
                ob += 1
                for kc, (k0, kn) in enumerate(KCS):
                    its.append((qb, q0, qn, g, j, a, kc, k0, kn))

    def emit_s(i):
        qb, q0, qn, g, j, a, kc, k0, kn = its[i]
        s3 = i % 3
        P.op("pe", lambda: nc.tensor.matmul(
            pss[s3][0:kn, 0:qn], lhsT=kT[:, g, k0:k0 + kn], rhs=qT[:, j, q0:q0 + qn], start=True, stop=True),
            r=[("k", g, x_) for x_ in range(5)] + [("q", j, x_) for x_ in range(5)], w=[("pss", s3)])
        P.op("act", lambda: nc.scalar.activation(
            out=pt[s3][0:kn, 0:qn], in_=pss[s3][0:kn, 0:qn], func=AF.Exp, scale=scale),
            r=[("pss", s3)], w=[("pt", s3)])

    def emit_pv(i):
        qb, q0, qn, g, j, a, kc, k0, kn = its[i]
        s3 = i % 3
        P.op("pe", lambda: nc.tensor.matmul(
            po[a][:, 0:qn], lhsT=v[0:kn, kc, g * 128:(g + 1) * 128], rhs=pt[s3][0:kn, 0:qn],
            start=(kc == 0), stop=(kc == 32)), r=[("pt", s3), ("v", min(kc // 8, 4))], w=[("po", a)])
        P.op("pe", lambda: nc.tensor.matmul(
            pl[a][:, 0:qn], lhsT=ones[0:kn, :], rhs=pt[s3][0:kn, 0:qn],
            start=(kc == 0), stop=(kc == 32)), r=[("pt", s3), "ones"], w=[("pl", a)])
        if kc == 32:
            P.op("dve", lambda: nc.vector.reciprocal(out=rl[a][:, 0:qn], in_=pl[a][:, 0:qn]),
                 r=[("pl", a)], w=[("rl", a)])
            P.op("dve", lambda: nc.vector.tensor_tensor(
                out=ost[a][:, 0:qn], in0=po[a][:, 0:qn], in1=rl[a][:, 0:qn], op=ALU.mult),
                r=[("po", a), ("rl", a)], w=[("ost", a)])
            G.x_o.write(j, qb, qn, ost[a][:, 0:qn], r=[("ost", a)], dkey=("ost", a))

    for i in range(len(its) + 2):
        if i < len(its):
            emit_s(i)
        if i >= 2:
            emit_pv(i - 2)
    P.end_phase()


SW = SP_ + 6


def run_HS(P, G, jl):
    P.begin_phase()
    nc = P.nc
    cw_in, cb_in, cbrow_in = G.cw_in[jl], G.cb_in[jl], G.cbrow_in[jl]
    dsk_in, nw_in, cst_in = G.dsk_in[jl], G.nw_in[jl], G.cst_in
    dtb_in, alog_in = G.dtb_in[jl], G.alog_in[jl]
    sent = []
    selstg = [P.sb("selstg%d" % i, [128, 4, 512], BF16) for i in range(2)]
    psel = P.ps("psel", [128, 512])
    sel = Sel(P, G, selstg, lambda slot, k: [("selstg", slot, k)], lambda: (psel, "psel"))
    stg32 = [P.sb("stg32_%d" % i, [128, NCH, 16], F32) for i in range(1)]
    for i_ in range(1):
        P.op("dve", lambda i_=i_: nc.vector.memset(stg32[i_][:], 0.0), w=[("stg32", i_)])

    cst = P.sb("cst", [128, 4, 128], F32)
    Uf, Ub = cst[:, 0, :], cst[:, 1, :]
    neg4 = P.sb("neg4", [128, 2, 4, 128], BF16)
    ident = P.sb("ident", [128, 128], BF16)
    identf = P.sb("identf", [128, 128], F32)
    ones32 = P.sb("ones32", [128, 128], F32)
    onesrow = P.sb("onesrow", [1, 128], BF16)
    cw = P.sb("cw", [128, 2, 6, 7], F32)
    cb = P.sb("cb", [128, 2, 6], F32)
    cbrow32 = P.sb("cbrow32", [1, 2, 640], F32)
    cbrow = P.sb("cbrow", [1, 2, 640], BF16)
    dsk = P.sb("dsk", [128, 2, 8], F32)
    nw = P.sb("nw", [128, 2, 512], F32)
    diag = P.sb("diag", [128, 6, 7, 128], BF16)
    xbc_sb = P.sb("xbc_sb", [128, 6, SW], BF16)
    x_tok = P.sb("x_tok", [128, NCH, 512], BF16)
    B_tok = P.sb("B_tok", [128, NCH, 128], BF16)
    BT = P.sb("BT", [128, SP_], BF16)
    CT = P.sb("CT", [128, SP_], BF16)
    NS = NCH * 16
    dtt = P.sb("dtt", [128, NCH, 16], F32)
    av = P.sb("av", [128, NCH, 16], F32)
    aneg = P.sb("aneg", [128, NCH, 16], F32)
    ldt = aneg
    acs = P.sb("acs", [128, NCH, 16], F32)
    tot = P.sb("tot", [128, NCH, 16], F32)
    biasD = aneg
    dte = P.sb("dte", [128, NCH, 16], F32)
    eacs = P.sb("eacs", [128, NCH, 16], F32)
    cd = P.sb("cd", [128, NCH, 16], F32)
    tmps = stg32[0]
    prevb = xbc_sb[:].rearrange("p c t -> p (c t)")[:, 0:NCH * 512].rearrange("p (c f) -> p c f", c=NCH)
    Hs = [P.sb("H%d" % i, [128, 512], F32) for i in range(2)]
    Hf_bf = P.sb("Hf_bf", [128, 512], BF16)
    aU = [P.sb("aU%d" % i, [128, 8, 128], F32) for i in range(2)]
    Dm = [[P.sb("Dm%d_%d" % (pr_, i), [128, 8, 128], BF16) for i in range(2)] for pr_ in range(2)]
    Msum = P.sb("Msum", [128, 8, 128], BF16)
    Mm = [P.sb("Mm%d" % i, [128, 8, 128], BF16) for i in range(2)]
    CBT = [P.sb("CBT%d" % i, [128, 128], BF16) for i in range(2)]
    xdte = [P.sb("xdte%d" % i, [128, 512], BF16) for i in range(2)]
    yt = [P.sb("yt%d" % i, [128, 512], F32) for i in range(3)]
    ht = P.sb("ht", [128, 512], F32)
    sz = ht
    junk = yt[2]
    ss = P.sb("ss", [128, 1], F32)
    gn = P.sb("gn", [128, 512], BF16)
    gst = [P.sb("gst%d" % i, [128, 4, 128], BF16) for i in range(2)]
    pb = [P.ps("pb%d" % i, [128, 512]) for i in range(4)]
    pacs = P.ps("pacs", [128, 1024])
    pb.append(pacs[:, 0:512])
    pb.append(pacs[:, 512:1024])
    ptr = P.ps("ptr", [128, 512], BF16)

    P.dma("sp", cst[:], cst_in, w=["cst"], key="cst")
    P.dma("sp", cw[:], cw_in, w=["cw"], key="cw")
    P.dma("sp", cb[:], cb_in, w=["cb"], key="cb")
    P.dma("sp", cbrow32[:], cbrow_in, w=["cbrow32"], key="cbrow")
    P.dma("sp", dsk[:], dsk_in, w=["dsk"], key="dsk")
    P.dma("sp", nw[:], nw_in, w=["nw"], key="nw")
    P.op("dve", lambda: nc.vector.memset(ones32[:], 1.0), w=["ones32"])
    P.op("dve", lambda: nc.vector.memset(onesrow[:], 1.0), w=["onesrow"])
    P.op("dve", lambda: nc.vector.tensor_tensor(out=identf[:], in0=Uf, in1=Ub, op=ALU.mult), r=["cst"], w=["identf"])
    P.op("dve", lambda: nc.vector.tensor_copy(out=ident[:], in_=identf[:]), r=["identf"], w=["ident"])
    for d_ in range(2):
        for hh in range(4):
            P.op("dve", lambda d_=d_, hh=hh: nc.vector.tensor_copy(out=neg4[:, d_, hh, :], in_=cst[:, 2 + d_, :]),
                 r=["cst"], w=["neg4"])
    P.op("dve", lambda: nc.vector.tensor_copy(out=cbrow[:], in_=cbrow32[:]), r=["cbrow32"], w=["cbrow"])

    def bc(ap2, n):
        return ap2.unsqueeze(2).to_broadcast([128, ap2.shape[1], n])

    def v3(ap2, k):
        return ap2.rearrange("p (k n) -> p k n", k=k)

    for g in range(2):
        if g == 1:
            P.barrier()
        for ch in range(6):
            for k in range(7):
                P.op("dve", lambda ch=ch, k=k, g=g: nc.vector.tensor_scalar(
                    out=diag[:, ch, k, :], in0=ident[:], scalar1=cw[:, g, ch, k:k + 1], scalar2=None, op0=ALU.mult),
                    r=["ident", "cw"], w=[("diag", ch)])
        P.op("dve", lambda: nc.vector.memset(xbc_sb[:, :, 0:115], 0.0), w=[("xbc", ci, 0) for ci in range(6)])
        P.op("dve", lambda: nc.vector.memset(xbc_sb[:, :, SW - 3:SW], 0.0), w=[("xbc", ci, 5) for ci in range(6)])
        for ci in range(6):
            if ci < 4:
                fsel = lambda k, ci=ci: (2 * k + g) * 4 + ci
            elif ci == 4:
                fsel = lambda k: 32 + 2 * k + g
            else:
                fsel = lambda k: 40 + 2 * k + g
            for q in range(4):
                cc = [G.x_xbc.rd(q, fsel(k)) for k in range(4)]
                for hf in range(2):
                    sel.load(xbc_sb[:, ci, 131 + q * 1024 + hf * 512:131 + q * 1024 + (hf + 1) * 512],
                             [c_[0][:, hf * 512:(hf + 1) * 512] for c_ in cc], 512, r=[c_[1] for c_ in cc],
                             w=[("xbc", ci, 1 + q)])
            cc = [G.x_xbc.rd(0, fsel(k)) for k in range(4)]
            sel.load(xbc_sb[:, ci, 115:131], [c_[0][:, 1024:1040] for c_ in cc], 16,
                     r=[c_[1] for c_ in cc] + [("xbc", ci, 0)], w=[("xbc", ci, 0)])
        XBC = [("xbc", ci, x_) for ci in range(6) for x_ in range(6)]
        P.op("dve", lambda: nc.vector.memset(dtt[:], 0.0), w=["dtt"])
        for k in range(4):
            sl32 = 0
            for d_ in range(2):
                c0 = k * 16 + d_ * 64 + g * 8
                for q in range(4):
                    for pt in range(2):
                        src, rk = G.x_dt.rd4(q, pt, 0)
                        cb_ = 1 + 8 * q + 4 * pt
                        P.dma("sp", stg32[sl32][:, cb_:cb_ + 4, d_ * 8:d_ * 8 + 8],
                              src[:, c0:c0 + 8].rearrange("(c p) k -> p c k", p=128),
                              r=[rk], w=[("stg32", sl32)], key=("stg32", sl32, d_, q, pt))
                src, rk = G.x_dt.rd(0, 8, 0, 16)
                P.dma("sp", stg32[sl32][112:128, 0, d_ * 8:d_ * 8 + 8], src[:, c0:c0 + 8],
                      r=[rk], w=[("stg32", sl32)], key=("stg32", sl32, d_, 4, 0))
            P.op("dve", lambda k=k, sl32=sl32: nc.vector.scalar_tensor_tensor(
                out=dtt[:], in0=stg32[sl32][:], scalar=sel.sel[:, k:k + 1], in1=dtt[:], op0=ALU.mult, op1=ALU.add),
                r=[("stg32", sl32), "sel", "dtt"], w=["dtt"])
        P.dma("sp", tmps[:], dtb_in[g], w=[("stg32", 0)], key=("stg32", 0))
        P.dma("sp", aneg[:], alog_in[g], w=["aneg"], key="aneg")
        P.op("dve", lambda: nc.vector.tensor_tensor(out=dtt[:], in0=dtt[:], in1=tmps[:], op=ALU.add),
             r=["dtt", ("stg32", 0)], w=["dtt"])
        P.op("act", lambda: nc.scalar.activation(out=dtt[:], in_=dtt[:], func=AF.Exp), r=["dtt"], w=["dtt"])
        P.op("act", lambda: nc.scalar.activation(out=dtt[:], in_=dtt[:], func=AF.Ln, bias=1.0), r=["dtt"], w=["dtt"])
        P.op("dve", lambda: nc.vector.memset(dtt[0:112, 0, :], 0.0), r=["dtt"], w=["dtt"])
        P.op("act", lambda: nc.scalar.activation(out=aneg[:], in_=aneg[:], func=AF.Exp), r=["aneg"], w=["aneg"])
        P.op("dve", lambda: nc.vector.scalar_tensor_tensor(out=av[:], in0=aneg[:], scalar=-1.0, in1=dtt[:],
                                                           op0=ALU.mult, op1=ALU.mult),
             r=["aneg", "dtt"], w=["av"])
        P.op("dve", lambda: nc.vector.tensor_scalar_max(out=ldt[:], in0=dtt[:], scalar1=1e-30), r=["dtt"], w=["aneg"])
        P.op("act", lambda: nc.scalar.activation(out=ldt[:], in_=ldt[:], func=AF.Ln), r=["aneg"], w=["aneg"])
        for c in range(NCH):
            P.op("pe", lambda c=c: nc.tensor.matmul(pacs[:, c * 16:c * 16 + 8], lhsT=Uf, rhs=av[:, c, 0:8],
                                                   start=True, stop=True), r=["av", "cst"], w=[("pb", 4), ("pb", 5)])
            P.op("pe", lambda c=c: nc.tensor.matmul(pacs[:, c * 16 + 8:c * 16 + 16], lhsT=Ub, rhs=av[:, c, 8:16],
                                                   start=True, stop=True), r=["av", "cst"], w=[("pb", 4), ("pb", 5)])
        P.op("act", lambda: nc.scalar.copy(out=acs[:].rearrange("p c k -> p (c k)"), in_=pacs[:, 0:NS]),
             r=[("pb", 4), ("pb", 5)], w=["acs"])
        for c in range(NCH):
            P.op("pe", lambda c=c: nc.tensor.matmul(pacs[:, c * 16:c * 16 + 16], lhsT=ones32[:], rhs=av[:, c, :],
                                                   start=True, stop=True), r=["av", "ones32", "acs"], w=[("pb", 4), ("pb", 5)])
        P.op("act", lambda: nc.scalar.copy(out=tot[:].rearrange("p c k -> p (c k)"), in_=pacs[:, 0:NS]),
             r=[("pb", 4), ("pb", 5)], w=["tot"])
        P.op("dve", lambda: nc.vector.tensor_tensor(out=biasD[:], in0=ldt[:], in1=acs[:], op=ALU.subtract),
             r=["aneg", "acs"], w=["aneg"])
        P.op("act", lambda: nc.scalar.activation(out=eacs[:], in_=acs[:], func=AF.Exp), r=["acs"], w=["eacs"])
        P.op("act", lambda: nc.scalar.activation(out=cd[:], in_=tot[:], func=AF.Exp), r=["tot"], w=["cd"])
        P.op("dve", lambda: nc.vector.tensor_tensor(out=dte[:], in0=tot[:], in1=acs[:], op=ALU.subtract),
             r=["tot", "acs"], w=["dte"])
        P.op("act", lambda: nc.scalar.activation(out=dte[:], in_=dte[:], func=AF.Exp), r=["dte"], w=["dte"])
        P.op("dve", lambda: nc.vector.tensor_tensor(out=dte[:], in0=dte[:], in1=dtt[:], op=ALU.mult),
             r=["dte", "dtt"], w=["dte"])
        for c in range(NCH):
            w_ = 0
            cb0 = c * 128
            p0 = c % 2

            def conv_tok(c=c, cb0=cb0, p0=p0, g=g):
                ins = None
                for xc in range(4):
                    for k in range(7):
                        nc.tensor.matmul(pb[p0][:, xc * 128:(xc + 1) * 128], lhsT=xbc_sb[:, xc, cb0 + k:cb0 + k + 128],
                                         rhs=diag[:, xc, k, :], start=(k == 0), stop=False)
                    ins = nc.tensor.matmul(pb[p0][:, xc * 128:(xc + 1) * 128], lhsT=onesrow[0:1, :],
                                           rhs=cbrow[0:1, g, xc * 128:(xc + 1) * 128], start=False, stop=True)
                return ins
            P.op("pe", conv_tok, r=XBC + ["onesrow", "cbrow"] + [("diag", ch) for ch in range(4)], w=[("pb", p0)])
            P.op("act", lambda c=c, p0=p0: nc.scalar.activation(out=x_tok[:, c, :], in_=pb[p0][:, :], func=AF.Silu),
                 r=[("pb", p0)], w=[("x_tok", c)])
            p2 = 2 + c % 2

            def conv_b(c=c, cb0=cb0, p2=p2, g=g):
                for k in range(7):
                    nc.tensor.matmul(pb[p2][:, 0:128], lhsT=xbc_sb[:, 4, cb0 + k:cb0 + k + 128], rhs=diag[:, 4, k, :],
                                     start=(k == 0), stop=False)
                nc.tensor.matmul(pb[p2][:, 0:128], lhsT=onesrow[0:1, :], rhs=cbrow[0:1, g, 512:640],
                                 start=False, stop=True)
                for k in range(7):
                    nc.tensor.matmul(pb[p2][:, 128:256], lhsT=diag[:, 4, k, :], rhs=xbc_sb[:, 4, cb0 + k:cb0 + k + 128],
                                     start=(k == 0), stop=(k == 6))
                ins = None
                for k in range(7):
                    ins = nc.tensor.matmul(pb[p2][:, 256:384], lhsT=diag[:, 5, k, :], rhs=xbc_sb[:, 5, cb0 + k:cb0 + k + 128],
                                           start=(k == 0), stop=(k == 6))
                return ins
            P.op("pe", conv_b, r=XBC + ["onesrow", "cbrow", ("diag", 4), ("diag", 5)], w=[("pb", p2)])
            P.op("act", lambda c=c, p2=p2: nc.scalar.activation(out=B_tok[:, c, :], in_=pb[p2][:, 0:128], func=AF.Silu),
                 r=[("pb", p2)], w=[("B_tok", c)])
            P.op("act", lambda c=c, p2=p2, g=g: nc.scalar.activation(
                out=BT[:, c * 128:(c + 1) * 128], in_=pb[p2][:, 128:256], func=AF.Silu, bias=cb[:, g, 4:5]),
                r=[("pb", p2), "cb"], w=[("BT", c)])
            P.op("act", lambda c=c, p2=p2, g=g: nc.scalar.activation(
                out=CT[:, c * 128:(c + 1) * 128], in_=pb[p2][:, 256:384], func=AF.Silu, bias=cb[:, g, 5:6]),
                r=[("pb", p2), "cb"], w=[("CT", c)])
            if c == 0:
                P.op("dve", lambda: nc.vector.memset(x_tok[0:112, 0, :], 0.0), r=[("x_tok", 0)], w=[("x_tok", 0)])
                P.op("dve", lambda: nc.vector.memset(B_tok[0:112, 0, :], 0.0), r=[("B_tok", 0)], w=[("B_tok", 0)])
                P.op("dve", lambda: nc.vector.memset(BT[:, 0:112], 0.0), r=[("BT", 0)], w=[("BT", 0)])
                P.op("dve", lambda: nc.vector.memset(CT[:, 0:112], 0.0), r=[("CT", 0)], w=[("CT", 0)])
        P.op("dve", lambda: nc.vector.memset(Hs[1][:], 0.0), w=[("H", 1)])
        P.op("dve", lambda: nc.vector.memset(Hs[0][:], 0.0), w=[("H", 0)])
        P.op("dve", lambda: nc.vector.memset(Hf_bf[:], 0.0), w=["Hf_bf"])
        for c in range(NCH - 1, -1, -1):
            xs = c % 2
            P.op("act", lambda c=c: nc.scalar.copy(out=prevb[:, c, :], in_=Hs[1][:]), r=[("H", 1)],
                 w=[("prevb", c)] + (XBC if c == NCH - 1 else []))
            if c == 0:
                break
            P.op("dve", lambda c=c, xs=xs: nc.vector.tensor_tensor(
                out=v3(xdte[xs][:], 8), in0=v3(x_tok[:, c, :], 8), in1=bc(dte[:, c, 8:16], 64), op=ALU.mult),
                r=[("x_tok", c), "dte"], w=[("xdte", xs)])
            pS = 4 + c % 2
            P.op("pe", lambda c=c, xs=xs, pS=pS: nc.tensor.matmul(pb[pS][:, :], lhsT=B_tok[:, c, :], rhs=xdte[xs][:],
                                                                  start=True, stop=True),
                 r=[("B_tok", c), ("xdte", xs)], w=[("pb", pS)])
            P.op("dve", lambda c=c: nc.vector.tensor_tensor(
                out=v3(ht[:], 8), in0=v3(Hs[1][:], 8), in1=bc(cd[:, c, 8:16], 64), op=ALU.mult),
                r=[("H", 1), "cd"], w=["ht"])
            P.op("dve", lambda pS=pS: nc.vector.tensor_tensor(out=Hs[1][:], in0=ht[:], in1=pb[pS][:, :], op=ALU.add),
                 r=["ht", ("pb", pS)], w=[("H", 1)])
        def front(c, g=g):
            cs_ = slice(c * 128, (c + 1) * 128)
            pr = c % 2
            for d_ in range(2):
                U = Uf if d_ == 0 else Ub
                P.op("dve", lambda c=c, d_=d_, U=U: nc.vector.tensor_tensor(
                    out=aU[d_][:], in0=bc(av[:, c, d_ * 8:d_ * 8 + 8], 128),
                    in1=U.unsqueeze(1).to_broadcast([128, 8, 128]), op=ALU.mult),
                    r=["av", "cst"], w=[("aU", d_)])
                for hb in range(2):
                    pD = hb

                    def segmm(c=c, d_=d_, hb=hb, pD=pD):
                        nc.tensor.matmul(pb[pD][:, :], lhsT=ones32[:], rhs=aU[d_][:, hb * 4:hb * 4 + 4, :],
                                         start=True, stop=False)
                        return nc.tensor.matmul(pb[pD][:, :], lhsT=ident[:], rhs=neg4[:, d_, :, :],
                                                start=False, stop=True)
                    P.op("pe", segmm, r=[("aU", d_), "ones32", "ident", "neg4"], w=[("pb", pD)])
                    for h4 in range(4):
                        hh = hb * 4 + h4
                        P.op("act", lambda c=c, d_=d_, pD=pD, h4=h4, hh=hh, pr=pr: nc.scalar.activation(
                            out=Dm[pr][d_][:, hh, :], in_=pb[pD][:, h4 * 128:(h4 + 1) * 128], func=AF.Exp,
                            bias=biasD[:, c, d_ * 8 + hh:d_ * 8 + hh + 1]),
                            r=[("pb", pD), "aneg"], w=[("Dm", pr, d_, hh)])
            P.op("pe", lambda cs_=cs_: nc.tensor.matmul(pb[2][:, 0:128], lhsT=BT[:, cs_], rhs=CT[:, cs_],
                                                        start=True, stop=True),
                 r=[("BT", c), ("CT", c)], w=[("pb", 2)])
            P.op("act", lambda pr=pr: nc.scalar.copy(out=CBT[pr][:], in_=pb[2][:, 0:128]), r=[("pb", 2)], w=[("CBT", pr)])
            P.op("dve", lambda pr=pr: nc.vector.tensor_tensor(out=Msum[:], in0=Dm[pr][0][:], in1=Dm[pr][1][:], op=ALU.add),
                 r=[("Dm", pr, d_, hh) for d_ in range(2) for hh in range(8)], w=["Msum"])
            P.op("dve", lambda pr=pr: nc.vector.tensor_tensor(
                out=Mm[pr][:], in0=Msum[:], in1=CBT[pr][:].unsqueeze(1).to_broadcast([128, 8, 128]), op=ALU.mult),
                r=["Msum", ("CBT", pr)], w=[("Mm", pr)])

        def back(c, g=g):
            cs_ = slice(c * 128, (c + 1) * 128)
            pr = c % 2

            def ymm(c=c, pr=pr):
                ins = None
                for hh in range(8):
                    ins = nc.tensor.matmul(pb[3][:, hh * 64:(hh + 1) * 64], lhsT=Mm[pr][:, hh, :],
                                           rhs=x_tok[:, c, hh * 64:(hh + 1) * 64], start=True, stop=True)
                return ins
            P.op("pe", ymm, r=[("Mm", pr), ("x_tok", c)], w=[("pb", 3)])
            P.op("dve", lambda c=c, g=g: nc.vector.tensor_tensor(
                out=v3(yt[0][:], 8), in0=v3(x_tok[:, c, :], 8), in1=bc(dsk[:, g, :], 64), op=ALU.mult),
                r=[("x_tok", c), "dsk"], w=[("yt", 0)])
            P.op("dve", lambda: nc.vector.tensor_tensor(out=yt[0][:], in0=yt[0][:], in1=pb[3][:, :], op=ALU.add),
                 r=[("yt", 0), ("pb", 3)], w=[("yt", 0)])
            P.op("pe", lambda cs_=cs_: nc.tensor.matmul(pb[4][:, :], lhsT=CT[:, cs_], rhs=Hf_bf[:], start=True, stop=True),
                 r=[("CT", c), "Hf_bf"], w=[("pb", 4)])
            P.op("dve", lambda c=c: nc.vector.tensor_tensor(
                out=v3(yt[1][:], 8), in0=v3(pb[4][:, :], 8), in1=bc(eacs[:, c, 0:8], 64), op=ALU.mult),
                r=[("pb", 4), "eacs"], w=[("yt", 1)])
            P.op("pe", lambda c=c, cs_=cs_: nc.tensor.matmul(pb[5][:, :], lhsT=CT[:, cs_], rhs=prevb[:, c, :],
                                                             start=True, stop=True),
                 r=[("CT", c), ("prevb", c)], w=[("pb", 5)])
            P.op("dve", lambda c=c: nc.vector.tensor_tensor(
                out=v3(yt[2][:], 8), in0=v3(pb[5][:, :], 8), in1=bc(eacs[:, c, 8:16], 64), op=ALU.mult),
                r=[("pb", 5), "eacs"], w=[("yt", 2)])
            P.op("dve", lambda: nc.vector.tensor_tensor(out=yt[0][:], in0=yt[0][:], in1=yt[1][:], op=ALU.add),
                 r=[("yt", 0), ("yt", 1)], w=[("yt", 0)])
            P.op("dve", lambda: nc.vector.tensor_tensor(out=yt[0][:], in0=yt[0][:], in1=yt[2][:], op=ALU.add),
                 r=[("yt", 0), ("yt", 2)], w=[("yt", 0)])
            if c < NCH - 1:
                xs = c % 2
                P.op("dve", lambda c=c, xs=xs: nc.vector.tensor_tensor(
                    out=v3(xdte[xs][:], 8), in0=v3(x_tok[:, c, :], 8), in1=bc(dte[:, c, 0:8], 64), op=ALU.mult),
                    r=[("x_tok", c), "dte"], w=[("xdte", xs)])
                P.op("pe", lambda c=c, xs=xs: nc.tensor.matmul(pb[2][:, :], lhsT=B_tok[:, c, :], rhs=xdte[xs][:],
                                                               start=True, stop=True),
                     r=[("B_tok", c), ("xdte", xs)], w=[("pb", 2)])
                P.op("dve", lambda c=c: nc.vector.tensor_tensor(
                    out=v3(ht[:], 8), in0=v3(Hs[0][:], 8), in1=bc(cd[:, c, 0:8], 64), op=ALU.mult),
                    r=[("H", 0), "cd"], w=["ht"])
                P.op("dve", lambda: nc.vector.tensor_tensor(out=Hs[0][:], in0=ht[:], in1=pb[2][:, :], op=ALU.add),
                     r=["ht", ("pb", 2)], w=[("H", 0)])
                P.op("act", lambda: nc.scalar.copy(out=Hf_bf[:], in_=Hs[0][:]), r=[("H", 0)], w=["Hf_bf"])
            P.op("act", lambda: nc.scalar.copy(out=gn[:], in_=yt[0][:]), r=[("yt", 0)], w=["gn"])

            def trans():
                ins = None
                for i in range(4):
                    ins = nc.tensor.transpose(ptr[:, i * 128:(i + 1) * 128], gn[:, i * 128:(i + 1) * 128], ident[:])
                return ins
            P.op("pe", trans, r=["gn", "ident"], w=["ptr"])
            gs = c % 2
            P.op("act", lambda gs=gs: nc.scalar.copy(out=gst[gs][:].rearrange("p i t -> p (i t)"), in_=ptr[:, :]),
                 r=["ptr"], w=[("gst", gs)])
            G.x_g.write(g, c, gst[gs][:], r=[("gst", gs)], dkey=("gst", gs))

        for c in range(NCH + 1):
            if c < NCH:
                front(c)
            if c >= 1:
                back(c - 1)
    P.end_phase()


class _G:
    pass


def build_program():
    P = Prog()
    G = _G()
    ext = lambda n, shp, dt=F32: P.dram(n, shp, dt, "ExternalInput")
    G.ffn_w_in = ext("ffn_w_in", [4, 2, D, 2 * DFF])
    G.ffn_w_out = ext("ffn_w_out", [4, 2, DFF, D])
    G.ssd_in_proj = ext("ssd_in_proj", [2, D, 10368])
    G.ssd_out_proj = ext("ssd_out_proj", [2, 4096, D])
    G.attn_w_qkv = ext("attn_w_qkv", [2, D, 4096])
    G.attn_w_o = ext("attn_w_o", [2, 2048, D])
    G.rot = ext("rot", [128, 128])
    G.cst_in = ext("cst_in", [128, 4, 128])
    G.ident_in = ext("ident_in", [128, 128])
    G.sel_in = ext("sel_in", [128, 4])
    G.nwT_in = ext("nwT_in", [2, 128, 32])
    G.h0 = ext("h0", [128, KC, TT])
    G.gam_in = ext("gam_in", [128, 12, KC])
    G.cossin = ext("cossin", [128, 2, TT])
    G.qkg = ext("qkg", [128, 2, 2])
    G.cw_in = ext("cw_in", [2, 128, 2, 6, 7])
    G.cb_in = ext("cb_in", [2, 128, 2, 6])
    G.cbrow_in = ext("cbrow_in", [2, 1, 2, 640])
    G.dsk_in = ext("dsk_in", [2, 128, 2, 8])
    G.nw_in = ext("nw_in", [2, 128, 2, 512])
    G.dtb_in = ext("dtb_in", [2, 2, 128, NCH, 16])
    G.alog_in = ext("alog_in", [2, 2, 128, NCH, 16])
    G.h_out = P.dram("h_out", [128, KC, TT], F32, "ExternalOutput")
    G.h_dram = P.idram("h_dram", [128, KC, TT], F32)
    G.x_xbc = FeatX(P, "xbc", 48, order=XBC_ORDER)
    G.zpark = P.idram("zpark", [32, 128, TT], BF16)
    G.x_dt = TokX(P, "dt", 1, 128, F32)
    G.x_g = GX(P)
    G.x_qk = FeatX(P, "qk", 24)
    G.x_v = TokX(P, "v", 2, 512, BF16)
    G.x_o = OX(P)

    run_T(P, G, None, [(0, 0)], "ssd", G.h0, G.h_dram, False, (None, 0, 0))
    run_HS(P, G, 0)
    run_T(P, G, "ssd", [(0, 1), (1, 0)], "attn", G.h_dram, G.h_dram, False, (0, 0, 1))
    run_HA(P, G)
    run_T(P, G, "attn", [(1, 1), (2, 0)], "ssd", G.h_dram, G.h_dram, False, (0, 1, 2))
    run_HS(P, G, 1)
    run_T(P, G, "ssd", [(2, 1), (3, 0)], "attn", G.h_dram, G.h_dram, False, (1, 1, 3))
    run_HA(P, G)
    run_T(P, G, "attn", [(3, 1)], None, G.h_dram, G.h_out, True, (1, None, None))
    P.emit()
    return P


def rope_tables(q):
    inv = (10000.0 ** (-np.arange(0, 64, 2, dtype=np.float32) / 64.0)).astype(np.float32)
    t = q * 1024 + np.arange(1024)
    row = np.concatenate([t // 64, np.full((16,), -1)]).astype(np.float32)
    col = np.concatenate([t % 64, np.arange(16)]).astype(np.float32)
    ang = np.stack([row, col], -1)[..., None] * inv
    c, s = np.cos(ang).astype(np.float32), np.sin(ang).astype(np.float32)
    out = np.zeros((128, 2, TT), np.float32)
    for d in range(128):
        out[d, 0] = c[:, d // 64, d % 32]
        out[d, 1] = s[:, d // 64, d % 32]
    return out


def rot_matrix():
    r = np.zeros((128, 128), np.float32)
    for m in range(128):
        if m % 64 < 32:
            r[m + 32, m] = -1.0
        else:
            r[m - 32, m] = 1.0
    return r


def col_layout(v):
    return np.ascontiguousarray(np.asarray(v, np.float32).reshape(-1, 128).T)


def ssd_consts():
    j = np.arange(128)
    Uf = (j[:, None] <= j[None, :]).astype(np.float32)
    Ub = (j[:, None] >= j[None, :]).astype(np.float32)
    NEGf = np.where(j[:, None] > j[None, :], -30000.0, 0.0).astype(np.float32)
    NEGb = np.where(j[:, None] < j[None, :], -30000.0, 0.0).astype(np.float32)
    return np.ascontiguousarray(np.stack([Uf, Ub, NEGf, NEGb], axis=1))


def ssd_params(hq, conv_w, conv_b, dt_bias, a_log, d_skip, norm_w):
    nl = conv_w.shape[0]
    cw = np.zeros((nl, 128, 2, 6, 7), np.float32)
    cb = np.zeros((nl, 128, 2, 6), np.float32)
    cbrow = np.zeros((nl, 1, 2, 640), np.float32)
    dsk = np.zeros((nl, 128, 2, 8), np.float32)
    nw = np.zeros((nl, 128, 2, 512), np.float32)
    dtb = np.zeros((nl, 2, 128, NCH, 16), np.float32)
    alog = np.zeros((nl, 2, 128, NCH, 16), np.float32)
    for j in range(nl):
        for gi in range(2):
            Gg = 2 * hq + gi
            chans = [Gg * 512 + i * 128 for i in range(4)] + [4096 + Gg * 128, 5120 + Gg * 128]
            for ci, c0 in enumerate(chans):
                cw[j, :, gi, ci, :] = conv_w[j][:, c0:c0 + 128].T
                cb[j, :, gi, ci] = conv_b[j][c0:c0 + 128]
            cbrow[j, 0, gi, 0:512] = conv_b[j][Gg * 512:(Gg + 1) * 512]
            cbrow[j, 0, gi, 512:640] = conv_b[j][4096 + Gg * 128:4096 + (Gg + 1) * 128]
            dtb[j, gi] = np.concatenate([dt_bias[j, 0, Gg * 8:Gg * 8 + 8], dt_bias[j, 1, Gg * 8:Gg * 8 + 8]])[None, None, :]
            alog[j, gi] = np.concatenate([a_log[j, 0, Gg * 8:Gg * 8 + 8], a_log[j, 1, Gg * 8:Gg * 8 + 8]])[None, None, :]
            dsk[j, :, gi, :] = d_skip[j][Gg * 8:Gg * 8 + 8][None, :]
            nw[j, :, gi, :] = norm_w[j][Gg * 512:(Gg + 1) * 512][None, :]
    return {"cw_in": cw, "cb_in": cb, "cbrow_in": cbrow, "dsk_in": dsk, "nw_in": nw, "dtb_in": dtb, "alog_in": alog}


_PROG = []


def kernel(x, meta_tokens, ffn_norm, ffn_w_in, ffn_w_out, mix_norm, ssd_in_proj, ssd_conv_w, ssd_conv_b,
           ssd_dt_bias, ssd_A_log, ssd_D, ssd_norm, ssd_out_proj, attn_w_qkv, attn_q_norm, attn_k_norm, attn_w_o):
    f32 = lambda a: np.ascontiguousarray(np.asarray(a, dtype=np.float32))
    x, meta_tokens = f32(x), f32(meta_tokens)
    ffn_norm, mix_norm = f32(ffn_norm), f32(mix_norm)
    shared = {"ffn_w_in": f32(ffn_w_in), "ffn_w_out": f32(ffn_w_out), "ssd_in_proj": f32(ssd_in_proj),
              "ssd_out_proj": f32(ssd_out_proj), "attn_w_qkv": f32(attn_w_qkv), "attn_w_o": f32(attn_w_o),
              "rot": rot_matrix(), "cst_in": ssd_consts(), "ident_in": np.eye(128, dtype=np.float32)}
    ssd_conv_w, ssd_conv_b, ssd_dt_bias = f32(ssd_conv_w), f32(ssd_conv_b), f32(ssd_dt_bias)
    ssd_A_log, ssd_D, ssd_norm = f32(ssd_A_log), f32(ssd_D), f32(ssd_norm)
    attn_q_norm, attn_k_norm = f32(attn_q_norm), f32(attn_k_norm)
    gam = np.zeros((128, 12, KC), np.float32)
    for l in range(4):
        for sl in range(2):
            gam[:, 2 * l + sl, :] = col_layout(ffn_norm[l, sl])
        gam[:, 8 + l, :] = col_layout(mix_norm[l])
    qkg = np.zeros((128, 2, 2), np.float32)
    for j in range(2):
        qkg[:, j, 0] = attn_q_norm[j]
        qkg[:, j, 1] = attn_k_norm[j]
    shared["gam_in"] = gam
    shared["nwT_in"] = np.ascontiguousarray(np.stack([col_layout(ssd_norm[j]) for j in range(2)], axis=0))
    shared["qkg"] = qkg
    cs = [rope_tables(q) for q in range(4)]
    sp = [ssd_params(hq, ssd_conv_w, ssd_conv_b, ssd_dt_bias, ssd_A_log, ssd_D, ssd_norm) for hq in range(4)]
    maps = []
    for c in range(8):
        b, q = c // 4, c % 4
        H = np.concatenate([x[b, q * 1024:(q + 1) * 1024], meta_tokens], axis=0)
        m = dict(shared)
        m["h0"] = np.ascontiguousarray(H.T.reshape(KC, 128, TT).transpose(1, 0, 2))
        m["cossin"] = cs[q]
        onehot = np.zeros((128, 4), np.float32)
        onehot[:, q] = 1.0
        m["sel_in"] = onehot
        m.update(sp[q])
        maps.append(m)
    if not _PROG:
        _PROG.append(build_program())
    res = run_bass_kernel_spmd(_PROG[0].nc, maps, core_ids=list(range(8))).results
    out = np.zeros((2, SEQ, D), np.float32)
    for c in range(8):
        b, q = c // 4, c % 4
        ho = np.asarray(res[c]["h_out"])
        out[b, q * 1024:(q + 1) * 1024] = ho.transpose(1, 0, 2).reshape(D, TT).T[0:1024]
    return out
```

```python
import contextlib
import math
import numpy as np
import ml_dtypes
import concourse.bass as bass
import concourse.mybir as mybir
from concourse.bass_utils import run_bass_kernel_spmd

F32 = mybir.dt.float32
BF16 = mybir.dt.bfloat16
AF = mybir.ActivationFunctionType
ALU = mybir.AluOpType
AX = mybir.AxisListType
NPBF = ml_dtypes.bfloat16

D = 2048
KC = 16
TT = 1040
NREAL = 1024
NMETA = 16
TBS = [(0, 512), (512, 512), (1024, 16)]
TTILES = [(i * 128, 128) for i in range(8)] + [(1024, 16)]
DFF = 6144
EPS = 1e-6
SEQ = 4096
SV = SEQ + NMETA
SP_ = SEQ + 128
NCH = 33


class _Op:
    __slots__ = ("eng", "fn", "deps", "ms", "dkey", "sem", "val", "clock", "idx", "inc", "is_cc")


class Prog:
    ENGS = ("pe", "act", "dve", "pool", "sp")

    def __init__(self):
        self.nc = bass.Bass("TRN2", target_bir_lowering=False)
        self.es = contextlib.ExitStack()
        self.phase = None
        self.phase_id = 0
        nc = self.nc
        self.e = {"pe": nc.tensor, "act": nc.scalar, "dve": nc.vector, "pool": nc.gpsimd, "sp": nc.sync}
        self.ops = []
        self.nops = 0
        self.lastw = {}
        self.readers = {}
        self.out_dmas = []
        self.last_on = {}
        self.dma_since = []
        self.sem = {e: self.es.enter_context(nc.semaphore("s_" + e)) for e in ("pe", "act", "dve", "pool")}
        self.cnt = {e: 0 for e in self.sem}
        self.dsem, self.dcnt = {}, {}
        self.known = {e: {} for e in self.ENGS}
        self.bar = self.es.enter_context(nc.sbuf_tensor("bar", [128, 8], F32))

    def dram(self, name, shape, dt, kind):
        return self.nc.dram_tensor(name, list(shape), dt, kind=kind).ap()

    def idram(self, name, shape, dt):
        return self.nc.dram_tensor(name, list(shape), dt).ap()

    def _stack(self):
        return self.phase if self.phase is not None else self.es

    def sb(self, name, shape, dt):
        return self._stack().enter_context(self.nc.sbuf_tensor("%s_p%d" % (name, self.phase_id), list(shape), dt))

    def ps(self, name, shape, dt=F32):
        return self._stack().enter_context(self.nc.psum_tensor("%s_p%d" % (name, self.phase_id), list(shape), dt))

    def begin_phase(self):
        self.phase = contextlib.ExitStack()
        self.phase_id += 1

    def end_phase(self, wait_cc=False):
        self.barrier(wait_cc=wait_cc)
        self.flush()
        self.phase.close()
        self.phase = None

    def op(self, eng, fn, r=(), w=(), dkey=None, inc=16, extra=()):
        o = _Op()
        o.inc = inc
        o.is_cc = False
        o.eng, o.fn, o.dkey = eng, fn, dkey
        o.ms = dkey is not None
        o.idx = self.nops
        self.nops += 1
        o.sem = o.val = o.clock = None
        deps = {}
        for d in extra:
            deps[d.idx] = d
        for k in r:
            lw = self.lastw.get(k)
            if lw is not None:
                deps[lw.idx] = lw
        for k in w:
            lw = self.lastw.get(k)
            if lw is not None:
                deps[lw.idx] = lw
            rd = self.readers.get(k)
            if rd:
                for x in rd.values():
                    if isinstance(x, list):
                        for y in x:
                            deps[y.idx] = y
                    else:
                        deps[x.idx] = x
        dl = []
        for i in sorted(deps, reverse=True):
            d = deps[i]
            if d.eng == "pe" and eng == "pe" and d.dkey is None and dkey is None:
                continue
            d.ms = True
            dl.append(d)
        o.deps = dl
        for k in w:
            self.lastw[k] = o
            self.readers[k] = {}
        for k in r:
            rd = self.readers.setdefault(k, {})
            if dkey is not None:
                rd.setdefault("dma", []).append(o)
            else:
                rd[eng] = o
        self.ops.append(o)
        if fn is not None:
            if dkey is not None:
                self.dma_since.append(o)
            else:
                self.last_on[eng] = o
        return o

    def dma(self, q, out, in_, r=(), w=(), key=None, is_out=False):
        fn = (lambda E=self.e[q], out=out, in_=in_: E.dma_start(out=out, in_=in_))
        o = self.op(q, fn, r=r, w=w, dkey=key)
        if is_out:
            self.out_dmas.append(o)
        return o

    def allgather(self, src, dst, r, w, key):
        nc = self.nc
        fn = (lambda: nc.gpsimd.collective_compute("AllGather", ALU.bypass, replica_groups=[[0, 1, 2, 3], [4, 5, 6, 7]],
                                                   ins=[src.opt()], outs=[dst.opt()]))
        o = self.op("pool", fn, r=r, w=w, dkey=key, inc=1)
        o.is_cc = True
        return o

    def barrier(self, wait_cc=False):
        nc = self.nc
        pend_cc = [] if wait_cc else [d for d in self.dma_since if d.is_cc]
        deps = [self.last_on[e] for e in ("pe", "act", "dve", "pool") if e in self.last_on] + \
               [d for d in self.dma_since if wait_cc or not d.is_cc]
        b = self.op("dve", lambda: nc.vector.memset(self.bar[:], 0.0), extra=deps)
        b.ms = True
        for e in ("pe", "act", "pool", "sp"):
            self.op(e, None, extra=[b])
        keep = {k: o for k, o in self.lastw.items() if o.is_cc and o in pend_cc}
        self.lastw.clear()
        self.readers.clear()
        self.lastw.update(keep)
        self.dma_since = list(pend_cc)

    def _handle(self, s):
        return self.sem[s] if s in self.sem else self.dsem[s]

    def _wait(self, engname, d):
        kn = self.known[engname]
        if kn.get(d.sem, 0) >= d.val:
            return
        self.e[engname].wait_ge(self._handle(d.sem), d.val)
        kn = dict(kn)
        for ks, kv in d.clock.items():
            if kn.get(ks, 0) < kv:
                kn[ks] = kv
        self.known[engname] = kn

    def flush(self):
        nc = self.nc
        for o in self.ops:
            for d in o.deps:
                self._wait(o.eng, d)
            if o.fn is None:
                continue
            ins = o.fn()
            if o.ms:
                if o.dkey is not None:
                    k = ("d", o.dkey)
                    if k not in self.dsem:
                        self.dsem[k] = self.es.enter_context(nc.semaphore("sd%d" % len(self.dsem)))
                        self.dcnt[k] = 0
                    self.dcnt[k] += o.inc
                    ins.then_inc(self.dsem[k], o.inc)
                    o.sem, o.val = k, self.dcnt[k]
                else:
                    self.cnt[o.eng] += 1
                    ins.then_inc(self.sem[o.eng], 1)
                    o.sem, o.val = o.eng, self.cnt[o.eng]
                c = dict(self.known[o.eng])
                c[o.sem] = o.val
                o.clock = c
        self.ops = []

    def emit(self):
        self.flush()
        for o in self.out_dmas:
            self._wait("sp", o)
        return self.nc


def _xbc_order():
    o = []
    for par in range(2):
        for G_ in range(par, 8, 2):
            o += [G_ * 4 + i for i in range(4)]
        o += [32 + G_ for G_ in range(par, 8, 2)]
        o += [40 + G_ for G_ in range(par, 8, 2)]
    return o


XBC_ORDER = _xbc_order()


class FeatX:
    def __init__(self, P, name, nchunks, order=None):
        self.P, self.name = P, name
        n = nchunks // 3
        self.send = [P.idram("s_%s%d" % (name, i), [384, TT], BF16) for i in range(n)]
        self.recv = [P.idram("r_%s%d" % (name, i), [1536, TT], BF16) for i in range(n)]
        self.keys = {i: [] for i in range(n)}
        order = list(range(nchunks)) if order is None else order
        self.pos = {f: i for i, f in enumerate(order)}

    def write(self, f, src, r, dkey):
        P = self.P
        i, j = self.pos[f] // 3, self.pos[f] % 3
        sk = ("snd", self.name, f)
        self.keys[i].append(sk)
        P.dma("sp", self.send[i][j * 128:(j + 1) * 128, :], src, r=r, w=[sk], key=dkey)
        if len(self.keys[i]) == 3:
            P.allgather(self.send[i], self.recv[i], r=self.keys[i], w=[("rcv", self.name, i)], key="cc_" + self.name)
            self.keys[i] = []

    def rd(self, q, f):
        i, j = self.pos[f] // 3, self.pos[f] % 3
        return self.recv[i][q * 384 + j * 128:q * 384 + (j + 1) * 128, :], ("rcv", self.name, i)


class TokX:
    RP = (512, 528)

    def __init__(self, P, name, ncb, width, dt):
        self.P, self.name = P, name
        self.send = [[P.idram("s_%s%d_%d" % (name, cb, pt), [self.RP[pt], width], dt) for pt in range(2)] for cb in range(ncb)]
        self.recv = [[P.idram("r_%s%d_%d" % (name, cb, pt), [4 * self.RP[pt], width], dt) for pt in range(2)] for cb in range(ncb)]
        self.keys = {(cb, pt): [] for cb in range(ncb) for pt in range(2)}

    @staticmethod
    def loc(ti):
        if ti < 4:
            return 0, ti * 128
        if ti < 8:
            return 1, (ti - 4) * 128
        return 1, 512

    def write(self, ti, cb, tn, src, r, dkey):
        P = self.P
        pt, r0 = self.loc(ti)
        sk = ("snd", self.name, cb, ti)
        self.keys[(cb, pt)].append(sk)
        P.dma("sp", self.send[cb][pt][r0:r0 + tn, :], src, r=r, w=[sk], key=dkey)
        if len(self.keys[(cb, pt)]) == (4 if pt == 0 else 5):
            P.allgather(self.send[cb][pt], self.recv[cb][pt], r=self.keys[(cb, pt)], w=[("rcv", self.name, cb, pt)],
                        key="cc_" + self.name)
            self.keys[(cb, pt)] = []

    def rd(self, q, ti, cb, tn=128):
        pt, r0 = self.loc(ti)
        R = self.RP[pt]
        return self.recv[cb][pt][q * R + r0:q * R + r0 + tn, :], ("rcv", self.name, cb, pt)

    def rd4(self, q, pt, cb):
        R = self.RP[pt]
        return self.recv[cb][pt][q * R:q * R + 512, :], ("rcv", self.name, cb, pt)


class GX:
    def __init__(self, P):
        self.P = P
        w = [128] + [512] * 8
        self.send = [P.idram("s_g%d" % i, [1024, w[i]], BF16) for i in range(9)]
        self.recv = [P.idram("r_g%d" % i, [4096, w[i]], BF16) for i in range(9)]
        self.keys = {i: [] for i in range(9)}

    def write(self, g, c, src, r, dkey):
        P = self.P
        i = 0 if c == 0 else 1 + (c - 1) // 4
        c0 = 0 if c == 0 else ((c - 1) % 4) * 128
        sk = ("snd_g", g, c)
        self.keys[i].append(sk)
        P.dma("sp", self.send[i][g * 512:(g + 1) * 512, c0:c0 + 128].rearrange("(i p) t -> p i t", p=128), src,
              r=r, w=[sk], key=dkey)
        if len(self.keys[i]) == (2 if i == 0 else 8):
            P.allgather(self.send[i], self.recv[i], r=self.keys[i], w=[("rcv_g", i)], key="cc_g")
            self.keys[i] = []

    def rd(self, r_, jj, k, hf):
        i = 1 + 2 * k + hf
        return self.recv[i][r_ * 1024 + jj * 128:r_ * 1024 + (jj + 1) * 128, :], ("rcv_g", i)

    def rd_meta(self, r_, jj):
        return self.recv[0][r_ * 1024 + jj * 128:r_ * 1024 + (jj + 1) * 128, 112:128], ("rcv_g", 0)


class OX:
    def __init__(self, P):
        self.P = P
        w = [1024] * 4 + [16]
        self.send = [P.idram("s_o%d" % i, [512, w[i]], BF16) for i in range(5)]
        self.recv = [P.idram("r_o%d" % i, [2048, w[i]], BF16) for i in range(5)]
        self.keys = {i: [] for i in range(5)}

    def write(self, j, qb, qn, src, r, dkey):
        P = self.P
        i = qb // 2 if qb < 8 else 4
        c0 = (qb % 2) * 512 if qb < 8 else 0
        sk = ("snd_o", j, qb)
        self.keys[i].append(sk)
        P.dma("sp", self.send[i][j * 128:(j + 1) * 128, c0:c0 + qn], src, r=r, w=[sk], key=dkey)
        if len(self.keys[i]) == (8 if i < 4 else 4):
            P.allgather(self.send[i], self.recv[i], r=self.keys[i], w=[("rcv_o", i)], key="cc_o")
            self.keys[i] = []

    def rd(self, r_, jj, k):
        return self.recv[k][r_ * 512 + jj * 128:r_ * 512 + (jj + 1) * 128, :], ("rcv_o", k)

    def rd_meta(self, r_, jj):
        return self.recv[4][r_ * 512 + jj * 128:r_ * 512 + (jj + 1) * 128, :], ("rcv_o", 4)


class Sel:
    def __init__(self, P, G, stg, stg_keys, get_ps):
        nc = P.nc
        self.P, self.stg, self.stg_keys, self.get_ps = P, stg, stg_keys, get_ps
        self.sel = P.sb("sel_sb", [128, 4], F32)
        idf = P.sb("sel_idf", [128, 128], F32)
        self.selI = P.sb("selI", [128, 4, 128], BF16)
        P.dma("sp", self.sel[:], G.sel_in, w=["sel"], key="sel")
        P.dma("sp", idf[:], G.ident_in, w=["sel_idf"], key="sel_idf")
        for k in range(4):
            P.op("dve", lambda k=k: nc.vector.tensor_scalar(out=self.selI[:, k, :], in0=idf[:],
                                                            scalar1=self.sel[:, k:k + 1], scalar2=None, op0=ALU.mult),
                 r=["sel", "sel_idf"], w=["selI"])
        for sl in range(len(self.stg)):
            for k in range(4):
                P.op("dve", lambda sl=sl, k=k: nc.vector.memset(self.stg[sl][:, k, :], 0.0), w=self.stg_keys(sl, k))
        self.i = 0

    def load(self, dst, cands, ncols, r, w, view=None, prow=(0, 128)):
        P = self.P
        nc = P.nc
        slot = self.i % len(self.stg)
        self.i += 1
        p0, p1 = prow
        for k in range(4):
            tgt = self.stg[slot][p0:p1, k, 0:ncols]
            if view is not None:
                tgt = view(tgt)
            P.dma("sp", tgt, cands[k], r=r, w=self.stg_keys(slot, k), key=("selstg", slot, k))
        for t0 in range(0, ncols, 512):
            tn = min(512, ncols - t0)
            ps, pkey = self.get_ps()

            def mm(ps=ps, slot=slot, t0=t0, tn=tn):
                ins = None
                for k in range(4):
                    ins = nc.tensor.matmul(ps[:, 0:tn], lhsT=self.selI[:, k, :], rhs=self.stg[slot][:, k, t0:t0 + tn],
                                           start=(k == 0), stop=(k == 3))
                return ins
            P.op("pe", mm, r=["selI"] + [kk for k in range(4) for kk in self.stg_keys(slot, k)], w=[pkey])
            P.op("act", lambda ps=ps, t0=t0, tn=tn: nc.scalar.copy(out=dst[:, t0:t0 + tn], in_=ps[:, 0:tn]),
                 r=[pkey], w=w)


class TPhase:
    def __init__(self, P, G, pre, ffns, post, h_src, h_dst, h_dst_is_out, mixj):
        self.P = P
        self.G = G
        self.pre, self.post = pre, post
        self.n_ffn = len(ffns)
        self.h_in = h_src
        self.h_out = h_dst
        self.h_dst_is_out = h_dst_is_out
        self.w_in = [G.ffn_w_in[l, sl] for (l, sl) in ffns]
        self.w_out = [G.ffn_w_out[l, sl] for (l, sl) in ffns]
        self.mixj = mixj
        self.h = P.sb("h", [128, KC, TT], F32)
        self.big = P.sb("big", [128, 32, TT], BF16)
        self.wb = [P.sb("wb%d" % i, [128, KC, 512], BF16) for i in range(3)]
        self.wslot = 0
        self.gam = P.sb("gam_sb", [128, 12, KC], F32)
        self.rstd = P.sb("rstd", [128, TT], F32)
        self.sq = [P.sb("sq%d" % i, [128, 512], BF16) for i in range(2)]
        self.sg = [P.sb("sg%d" % i, [128, 512], F32) for i in range(2)]
        self.qf = self.sg
        self.ones = P.sb("ones", [128, 128], BF16)
        self.pmm = [P.ps("pmm%d" % i, [128, 512]) for i in range(6)]
        self.pmisc = [P.ps("pmisc%d" % i, [128, 512]) for i in range(2)]
        self.pi = 0
        self.npmm = 6
        self.mi = 0
        self.si = 0
        nc = P.nc
        P.op("dve", lambda: nc.vector.memset(self.ones[:], 1.0), w=["ones"])
        self.epsb = P.sb("epsb", [128, 1], F32)
        P.op("dve", lambda: nc.vector.memset(self.epsb[:], EPS), w=["epsb"])
        P.dma("sp", self.gam[:], G.gam_in, w=["gam"], key="gam")
        P.dma("sp", self.h[:], self.h_in, w=[("h", kc, tb) for kc in range(KC) for tb in range(3)], key="h")

    def next_w(self):
        s = self.wslot
        self.wslot = (s + 1) % 3
        return s

    def next_p(self):
        i = self.pi % self.npmm
        self.pi = (i + 1) % self.npmm
        return i

    def next_m(self):
        i = self.mi
        self.mi = (i + 1) % 2
        return i

    def load_w(self, slot, W, r0, c0, ncols, col_off=0, nk=KC):
        P = self.P
        src = W[r0:r0 + nk * 128, c0:c0 + ncols].rearrange("(kc p) f -> p kc f", p=128)
        P.dma("pool", self.wb[slot][:, 0:nk, col_off:col_off + ncols], src,
              w=[("wb", slot, col_off // 256 + i) for i in range(max(1, ncols // 256))], key=("wb", slot, col_off // 256))

    def mm_group(self, ps_ap, pkey, terms, extra_r=()):
        P = self.P
        nc = P.nc
        n = len(terms)

        def fn():
            ins = None
            for i, (l, rr) in enumerate(terms):
                ins = nc.tensor.matmul(ps_ap, lhsT=l, rhs=rr, start=(i == 0), stop=(i == n - 1))
            return ins
        return P.op("pe", fn, r=list(extra_r), w=[pkey])

    def rmsnorm(self, gi, dst_kc0=0):
        P = self.P
        nc = P.nc
        for tb, (t0, tn) in enumerate(TBS):
            m = self.next_m()
            pm = self.pmisc[m]
            for kc in range(KC):
                s = self.si
                self.si = (s + 1) % 2
                P.op("act", lambda s=s, kc=kc, t0=t0, tn=tn: nc.scalar.activation(
                    out=self.sq[s][:, 0:tn], in_=self.h[:, kc, t0:t0 + tn], func=AF.Square),
                    r=[("h", kc, tb)], w=[("sq", s)])
                P.op("pe", lambda s=s, kc=kc, tn=tn, pm=pm: nc.tensor.matmul(
                    pm[:, 0:tn], lhsT=self.ones[:, :], rhs=self.sq[s][:, 0:tn], start=(kc == 0), stop=(kc == KC - 1)),
                    r=[("sq", s), "ones"], w=[("pmisc", m)])
            P.op("act", lambda t0=t0, tn=tn, pm=pm: nc.scalar.activation(
                out=self.rstd[:, t0:t0 + tn], in_=pm[:, 0:tn], func=AF.Sqrt, bias=self.epsb[:, 0:1], scale=1.0 / D),
                r=[("pmisc", m), "epsb"], w=[("rstd", tb)])
            P.op("dve", lambda t0=t0, tn=tn: nc.vector.reciprocal(
                out=self.rstd[:, t0:t0 + tn], in_=self.rstd[:, t0:t0 + tn]),
                r=[("rstd", tb)], w=[("rstd", tb)])
            for kc in range(KC):
                P.op("dve", lambda kc=kc, t0=t0, tn=tn: nc.vector.scalar_tensor_tensor(
                    out=self.big[:, dst_kc0 + kc, t0:t0 + tn], in0=self.h[:, kc, t0:t0 + tn],
                    scalar=self.gam[:, gi, kc:kc + 1], in1=self.rstd[:, t0:t0 + tn],
                    op0=ALU.mult, op1=ALU.mult),
                    r=[("h", kc, tb), ("rstd", tb), "gam"], w=[("big", dst_kc0 + kc, tb)])

    def ffn(self, fi, gi):
        P = self.P
        nc = P.nc
        W1, W2 = self.w_in[fi], self.w_out[fi]
        self.rmsnorm(gi)
        for j in range(3):
            for fg in range(8):
                slot = self.next_w()
                c0 = j * 2048 + fg * 256
                self.load_w(slot, W1, 0, c0, 256, 0)
                self.load_w(slot, W1, 0, DFF + c0, 256, 256)
                for f2 in range(2):
                    fc = fg * 2 + f2
                    for tb, (t0, tn) in enumerate(TBS):
                        pg, pu = self.next_p(), self.next_p()
                        self.mm_group(self.pmm[pg][:, 0:tn], ("pmm", pg),
                                      [(self.wb[slot][:, kc, f2 * 128:(f2 + 1) * 128], self.big[:, kc, t0:t0 + tn])
                                       for kc in range(KC)],
                                      extra_r=[("wb", slot, 0)] + [("big", kc, tb) for kc in range(KC)])
                        self.mm_group(self.pmm[pu][:, 0:tn], ("pmm", pu),
                                      [(self.wb[slot][:, kc, 256 + f2 * 128:256 + (f2 + 1) * 128],
                                        self.big[:, kc, t0:t0 + tn]) for kc in range(KC)],
                                      extra_r=[("wb", slot, 1)] + [("big", kc, tb) for kc in range(KC)])
                        s = self.si
                        self.si = (s + 1) % 2
                        P.op("act", lambda s=s, pg=pg, tn=tn: nc.scalar.activation(
                            out=self.sg[s][:, 0:tn], in_=self.pmm[pg][:, 0:tn], func=AF.Silu),
                            r=[("pmm", pg)], w=[("sg", s)])
                        P.op("dve", lambda s=s, pu=pu, fc=fc, t0=t0, tn=tn: nc.vector.tensor_tensor(
                            out=self.big[:, 16 + fc, t0:t0 + tn], in0=self.sg[s][:, 0:tn],
                            in1=self.pmm[pu][:, 0:tn], op=ALU.mult),
                            r=[("sg", s), ("pmm", pu)], w=[("big", 16 + fc, tb)])
            for dg in range(4):
                slot = self.next_w()
                self.load_w(slot, W2, j * 2048, dg * 512, 512, 0)
                for di in range(4):
                    dc = dg * 4 + di
                    for tb, (t0, tn) in enumerate(TBS):
                        p = self.next_p()
                        self.mm_group(self.pmm[p][:, 0:tn], ("pmm", p),
                                      [(self.wb[slot][:, kc, di * 128:(di + 1) * 128],
                                        self.big[:, 16 + kc, t0:t0 + tn]) for kc in range(KC)],
                                      extra_r=[("wb", slot, 0), ("wb", slot, 1)] +
                                      [("big", 16 + kc, tb) for kc in range(KC)])
                        P.op("dve", lambda p=p, dc=dc, t0=t0, tn=tn: nc.vector.scalar_tensor_tensor(
                            out=self.h[:, dc, t0:t0 + tn], in0=self.pmm[p][:, 0:tn], scalar=0.5,
                            in1=self.h[:, dc, t0:t0 + tn], op0=ALU.mult, op1=ALU.add),
                            r=[("pmm", p), ("h", dc, tb)], w=[("h", dc, tb)])

    def store_h(self):
        P = self.P
        P.dma("sp", self.h_out, self.h[:], r=[("h", kc, tb) for kc in range(KC) for tb in range(3)],
              w=["h_dram"], key="hout", is_out=self.h_dst_is_out)

    def tok_major_proj(self, W, c0, ncols, out_dram, out_c0, dt_out, stg_name):
        P = self.P
        nc = P.nc
        slot = self.next_w()
        if ncols >= 256:
            self.load_w(slot, W, 0, c0, ncols, 0)
            wres = [("wb", slot, i) for i in range(ncols // 256)]
        else:
            self.load_w(slot, W, 0, c0, ncols, 0)
            wres = [("wb", slot, 0)]
        stg = self.tstg[stg_name]
        skey = (lambda s_: ("big", 18 + s_, 0)) if stg_name == "tstg" else (lambda s_: ("dstg", s_))
        for ti, (t0, tn) in enumerate(TTILES):
            tb = min(ti // 4, 2)
            p = self.next_p()
            self.mm_group(self.pmm[p][0:tn, 0:ncols], ("pmm", p),
                          [(self.big[:, kc, t0:t0 + tn], self.wb[slot][:, kc, 0:ncols]) for kc in range(KC)],
                          extra_r=wres + [("big", kc, tb) for kc in range(KC)])
            s = self.tsi
            self.tsi = (s + 1) % 2
            P.op("act", lambda s=s, p=p, tn=tn, stg=stg: nc.scalar.copy(
                out=stg[s][0:tn, 0:ncols], in_=self.pmm[p][0:tn, 0:ncols]),
                r=[("pmm", p)], w=[skey(s)])
            out_dram.write(ti, out_c0 // 512, tn, stg[s][0:tn, 0:ncols], r=[skey(s)], dkey=(stg_name, s))

    def post_attn(self, gi):
        P = self.P
        nc = P.nc
        W = self.w_mix_in
        self.rmsnorm(gi)
        units = []
        for fg in range(6):
            for fi in range(4):
                f = fg * 4 + fi
                for tb, (t0, tn) in enumerate(TBS):
                    units.append(dict(fg=fg, fi=fi, f=f, gcol=0 if f < 16 else 1, st=f % 2, tb=tb, t0=t0, tn=tn))
        slots = {}
        qf3 = self.qf + [self.qf2]

        def stage_a(i):
            u = units[i]
            fg, fi, tb, t0, tn = u["fg"], u["fi"], u["tb"], u["t0"], u["tn"]
            for fg_ in (fg, fg + 1):
                if fg_ < 6 and fg_ not in slots:
                    slots[fg_] = self.next_w()
                    self.load_w(slots[fg_], W, 0, fg_ * 512, 512, 0)
            slot = slots[fg]
            p = self.next_p()
            self.mm_group(self.pmm[p][:, 0:tn], ("pmm", p),
                          [(self.wb[slot][:, kc, fi * 128:(fi + 1) * 128], self.big[:, kc, t0:t0 + tn])
                           for kc in range(KC)],
                          extra_r=[("wb", slot, 0), ("wb", slot, 1)] + [("big", kc, tb) for kc in range(KC)])
            s2, s3 = i % 2, i % 3
            P.op("act", lambda: nc.scalar.copy(out=qf3[s3][:, 0:tn], in_=self.pmm[p][:, 0:tn]),
                 r=[("pmm", p)], w=[("sg", s3)])
            P.op("act", lambda: nc.scalar.activation(out=self.sq[s2][:, 0:tn], in_=self.pmm[p][:, 0:tn], func=AF.Square),
                 r=[("pmm", p)], w=[("sq", s2)])

        def stage_b(i):
            u = units[i]
            tn, gcol = u["tn"], u["gcol"]
            s2, s3 = i % 2, i % 3
            m = 0
            pm = self.pmisc[m]
            P.op("pe", lambda: nc.tensor.matmul(pm[:, 0:tn], lhsT=self.ones[:, :], rhs=self.sq[s2][:, 0:tn],
                                                start=True, stop=True),
                 r=[("sq", s2), "ones"], w=[("pmisc", m)])
            P.op("act", lambda: nc.scalar.activation(out=self.qr[s2][:, 0:tn], in_=pm[:, 0:tn], func=AF.Sqrt,
                                                     bias=self.epsb[:, 0:1], scale=1.0 / 128),
                 r=[("pmisc", m), "epsb"], w=[("rstd", s2)])
            P.op("dve", lambda: nc.vector.reciprocal(out=self.qr[s2][:, 0:tn], in_=self.qr[s2][:, 0:tn]),
                 r=[("rstd", s2)], w=[("rstd", s2)])
            P.op("dve", lambda: nc.vector.scalar_tensor_tensor(
                out=self.qn[s2][:, 0:tn], in0=qf3[s3][:, 0:tn], scalar=self.qkg[:, gcol:gcol + 1],
                in1=self.qr[s2][:, 0:tn], op0=ALU.mult, op1=ALU.mult),
                r=[("sg", s3), ("rstd", s2), "qkg"], w=[("qn", s2)])

        def stage_c(i):
            u = units[i]
            f, st, tb, t0, tn = u["f"], u["st"], u["tb"], u["t0"], u["tn"]
            s2, s3 = i % 2, i % 3
            m2 = 1
            pm2 = self.pmisc[m2]
            P.op("pe", lambda: nc.tensor.matmul(pm2[:, 0:tn], lhsT=self.rot[:, :], rhs=self.qn[s2][:, 0:tn],
                                                start=True, stop=True),
                 r=[("qn", s2), "rot"], w=[("pmisc", m2)])
            P.op("dve", lambda: nc.vector.tensor_tensor(out=qf3[s3][:, 0:tn], in0=self.qn[s2][:, 0:tn],
                                                        in1=self.cos[:, t0:t0 + tn], op=ALU.mult),
                 r=[("qn", s2), "cos"], w=[("sg", s3)])
            P.op("dve", lambda: nc.vector.tensor_tensor(out=self.qn[s2][:, 0:tn], in0=pm2[:, 0:tn],
                                                        in1=self.sin[:, t0:t0 + tn], op=ALU.mult),
                 r=[("pmisc", m2), "sin"], w=[("qn", s2)])
            P.op("dve", lambda: nc.vector.tensor_tensor(out=self.fstg[st][:, t0:t0 + tn], in0=qf3[s3][:, 0:tn],
                                                        in1=self.qn[s2][:, 0:tn], op=ALU.add),
                 r=[("sg", s3), ("qn", s2)], w=[("big", 16 + st, tb)])
            if tb == 2:
                self.mix_out_f.write(f, self.fstg[st][:, :], r=[("big", 16 + st, tb_) for tb_ in range(3)],
                                     dkey=("fstg", st))

        n = len(units)
        for i in range(n + 2):
            if i < n:
                stage_a(i)
            if 1 <= i <= n:
                stage_b(i - 1)
            if i >= 2:
                stage_c(i - 2)
        for vg in range(2):
            self.tok_major_proj(W, 3072 + vg * 512, 512, self.mix_out_t, vg * 512, BF16, "tstg")

    def post_ssd(self, gi):
        P = self.P
        nc = P.nc
        W = self.w_mix_in
        self.rmsnorm(gi)
        self.tok_major_proj(W, 10240, 128, self.mix_out_dt, 0, F32, "dstg")
        segs = []
        for f in XBC_ORDER:
            if not segs or segs[-1][0] != f // 4:
                segs.append([f // 4, []])
            segs[-1][1].append(f)
        seg_slot = {}

        def want(si):
            if si < len(segs) and si not in seg_slot:
                seg_slot[si] = self.next_w()
                self.load_w(seg_slot[si], W, 0, 4096 + segs[si][0] * 512, 512, 0)
        seg_of = {}
        for si, (fg_, fl) in enumerate(segs):
            for f in fl:
                seg_of[f] = si
        want(0)
        for fpos, f in enumerate(XBC_ORDER):
            fg, fi = f // 4, f % 4
            si = seg_of[f]
            if f == segs[si][1][0]:
                want(si + 1)
            slot = seg_slot[si]
            st = fpos % 2
            for tb, (t0, tn) in enumerate(TBS):
                p = self.next_p()
                self.mm_group(self.pmm[p][:, 0:tn], ("pmm", p),
                              [(self.wb[slot][:, kc, fi * 128:(fi + 1) * 128], self.big[:, kc, t0:t0 + tn])
                               for kc in range(KC)],
                              extra_r=[("wb", slot, 0), ("wb", slot, 1)] + [("big", kc, tb) for kc in range(KC)])
                P.op("act", lambda p=p, st=st, t0=t0, tn=tn: nc.scalar.copy(
                    out=self.fstg[st][:, t0:t0 + tn], in_=self.pmm[p][:, 0:tn]),
                    r=[("pmm", p)], w=[("big", 16 + st, tb)])
            self.mix_out_f.write(f, self.fstg[st][:, :], r=[("big", 16 + st, tb) for tb in range(3)], dkey=("fstg", st))
        for fg in range(8):
            slot = self.next_w()
            self.load_w(slot, W, 0, fg * 512, 512, 0)
            for fi in range(4):
                f = fg * 4 + fi
                st = f % 2
                for tb, (t0, tn) in enumerate(TBS):
                    p = self.next_p()
                    self.mm_group(self.pmm[p][:, 0:tn], ("pmm", p),
                                  [(self.wb[slot][:, kc, fi * 128:(fi + 1) * 128], self.big[:, kc, t0:t0 + tn])
                                   for kc in range(KC)],
                                  extra_r=[("wb", slot, 0), ("wb", slot, 1)] + [("big", kc, tb) for kc in range(KC)])
                    P.op("act", lambda p=p, st=st, t0=t0, tn=tn: nc.scalar.activation(
                        out=self.fstg[st][:, t0:t0 + tn], in_=self.pmm[p][:, 0:tn], func=AF.Silu),
                        r=[("pmm", p)], w=[("big", 16 + st, tb)])
                P.dma("sp", self.G.zpark[f], self.fstg[st][:, :], r=[("big", 16 + st, tb) for tb in range(3)],
                      w=[("zpark", f)], key=("fstg", st))

    def pre_mix(self, nkc):
        P = self.P
        nc = P.nc
        G = self.G
        W = self.w_mix_out
        if self.pre == "ssd":
            self.nwT = P.sb("nwT", [128, 32], F32)
            P.dma("sp", self.nwT[:], G.nwT_in[self.mixj[0]], w=["nwT"], key="nwT")
            self.pgn = [self.pmisc[0], self.pmisc[1], self.pmm[5]]
            self.npmm = 5
        stg = [self.big[:, 16 + 4 * i:20 + 4 * i, 0:1024] for i in range(2)]
        sel = Sel(P, G, stg, lambda slot, k: [("big", 16 + 4 * slot + k, 0), ("big", 16 + 4 * slot + k, 1)],
                  lambda: (lambda p: (self.pmm[p], ("pmm", p)))(self.next_p()))
        pgk = [("pmisc", 0), ("pmisc", 1), ("pmm", 5)]
        for hh in range(nkc // KC):
            for kk in range(KC):
                kc = hh * KC + kk
                if self.pre == "attn":
                    r_, jj = kc // 4, kc % 4
                    cc = [G.x_o.rd(r_, jj, k) for k in range(4)]
                    sel.load(self.big[:, kk, 0:1024], [c_[0] for c_ in cc], 1024, r=[c_[1] for c_ in cc],
                             w=[("big", kk, 0), ("big", kk, 1)])
                    meta, mk = G.x_o.rd_meta(r_, jj)
                else:
                    r_, jj = kc // 8, kc % 8
                    for hf in range(2):
                        cc = [G.x_g.rd(r_, jj, k, hf) for k in range(4)]
                        sel.load(self.big[:, kk, hf * 512:(hf + 1) * 512], [c_[0] for c_ in cc], 512,
                                 r=[c_[1] for c_ in cc], w=[("big", kk, hf)])
                    meta, mk = G.x_g.rd_meta(r_, jj)
                P.dma("sp", self.big[:, kk, 1024:1040], meta, r=[mk], w=[("big", kk, 2)], key=("mixin_m", kk % 4))
                if self.pre == "ssd":
                    zs = kk % 2
                    zst = self.big[:, 24 + zs, :]
                    P.dma("sp", zst, G.zpark[kc], w=[("big", 24 + zs, tb) for tb in range(3)], key=("zst", zs))
                    for tb, (t0, tn) in enumerate(TBS):
                        P.op("dve", lambda kk=kk, zst=zst, t0=t0, tn=tn: nc.vector.tensor_tensor(
                            out=self.big[:, kk, t0:t0 + tn], in0=self.big[:, kk, t0:t0 + tn], in1=zst[:, t0:t0 + tn],
                            op=ALU.mult), r=[("big", kk, tb), ("big", 24 + zs, tb)], w=[("big", kk, tb)])
                        s_ = self.si
                        self.si = (s_ + 1) % 2
                        P.op("act", lambda kk=kk, s_=s_, t0=t0, tn=tn: nc.scalar.activation(
                            out=self.sq[s_][:, 0:tn], in_=self.big[:, kk, t0:t0 + tn], func=AF.Square),
                            r=[("big", kk, tb)], w=[("sq", s_)])
                        P.op("pe", lambda kk=kk, s_=s_, tb=tb, tn=tn: nc.tensor.matmul(
                            self.pgn[tb][:, 0:tn], lhsT=self.ones[:, :], rhs=self.sq[s_][:, 0:tn],
                            start=(kk % 4 == 0), stop=(kk % 4 == 3)), r=[("sq", s_), "ones"], w=[pgk[tb]])
                    if kk % 4 == 3:
                        for tb, (t0, tn) in enumerate(TBS):
                            P.op("act", lambda tb=tb, t0=t0, tn=tn: nc.scalar.activation(
                                out=self.rstd[:, t0:t0 + tn], in_=self.pgn[tb][:, 0:tn], func=AF.Sqrt,
                                bias=self.epsb[:, 0:1], scale=1.0 / 512), r=[pgk[tb], "epsb"], w=[("rstd", tb)])
                            P.op("dve", lambda t0=t0, tn=tn: nc.vector.reciprocal(
                                out=self.rstd[:, t0:t0 + tn], in_=self.rstd[:, t0:t0 + tn]),
                                r=[("rstd", tb)], w=[("rstd", tb)])
                            for k4 in range(kk - 3, kk + 1):
                                kc4 = hh * KC + k4
                                P.op("dve", lambda k4=k4, kc4=kc4, t0=t0, tn=tn: nc.vector.scalar_tensor_tensor(
                                    out=self.big[:, k4, t0:t0 + tn], in0=self.big[:, k4, t0:t0 + tn],
                                    scalar=self.nwT[:, kc4:kc4 + 1], in1=self.rstd[:, t0:t0 + tn],
                                    op0=ALU.mult, op1=ALU.mult),
                                    r=[("big", k4, tb), ("rstd", tb), "nwT"], w=[("big", k4, tb)])
            for dg in range(4):
                slot = self.next_w()
                self.load_w(slot, W, hh * 2048, dg * 512, 512, 0)
                for di in range(4):
                    dc = dg * 4 + di
                    for tb, (t0, tn) in enumerate(TBS):
                        p = self.next_p()
                        self.mm_group(self.pmm[p][:, 0:tn], ("pmm", p),
                                      [(self.wb[slot][:, kc, di * 128:(di + 1) * 128],
                                        self.big[:, kc, t0:t0 + tn]) for kc in range(KC)],
                                      extra_r=[("wb", slot, 0), ("wb", slot, 1)] +
                                      [("big", kc, tb) for kc in range(KC)])
                        P.op("dve", lambda p=p, dc=dc, t0=t0, tn=tn: nc.vector.tensor_tensor(
                            out=self.h[:, dc, t0:t0 + tn], in0=self.pmm[p][:, 0:tn],
                            in1=self.h[:, dc, t0:t0 + tn], op=ALU.add),
                            r=[("pmm", p), ("h", dc, tb)], w=[("h", dc, tb)])
        self.npmm = 6

    def setup_mix(self):
        P = self.P
        nc = P.nc
        G = self.G
        j = self.mixj
        self.tsi = 0
        self.sent = []
        if self.pre == "attn":
            self.w_mix_out = G.attn_w_o[j[0]]
        elif self.pre == "ssd":
            self.w_mix_out = G.ssd_out_proj[j[0]]
        if self.post == "attn":
            self.w_mix_in = G.attn_w_qkv[j[1]]
            self.mix_out_f = G.x_qk
            self.mix_out_t = G.x_v
            self.cs = P.sb("cs", [128, 2, TT], F32)
            self.cos = self.cs[:, 0, :]
            self.sin = self.cs[:, 1, :]
            self.rot = P.sb("rot_sb", [128, 128], F32)
            self.qkg = P.sb("qkg_sb", [128, 2], F32)
            self.qn = [P.sb("qn%d" % i, [128, 512], F32) for i in range(2)]
            self.qf2 = P.sb("qf2", [128, 512], F32)
            self.qr = [self.rstd[:, i * 512:(i + 1) * 512] for i in range(2)]
            P.dma("sp", self.cs[:], G.cossin, w=["cos", "sin"], key="cs")
            P.dma("sp", self.rot[:], G.rot, w=["rot"], key="rot")
            P.dma("sp", self.qkg[:], G.qkg[:, j[1], :], w=["qkg"], key="qkg")
        elif self.post == "ssd":
            self.w_mix_in = G.ssd_in_proj[j[1]]
            self.mix_out_f = G.x_xbc
            self.mix_out_dt = G.x_dt
        if self.post is not None:
            self.fstg = [self.big[:, 16 + i, :] for i in range(2)]
            self.tstg = {"tstg": [self.big[:, 18 + i, 0:512] for i in range(2)],
                         "dstg": [P.sb("dstg%d" % i, [128, 128], F32) for i in range(2)]}

    def exchange(self):
        pass


def run_T(P, G, pre, ffns, post, h_src, h_dst, h_dst_is_out, mixj):
    P.begin_phase()
    T = TPhase(P, G, pre, ffns, post, h_src, h_dst, h_dst_is_out, mixj)
    T.setup_mix()
    if pre == "attn":
        T.pre_mix(16)
    elif pre == "ssd":
        T.pre_mix(32)
    for i, (l, sl) in enumerate(ffns):
        T.ffn(i, 2 * l + sl)
    if post == "attn":
        T.post_attn(8 + mixj[2])
    elif post == "ssd":
        T.post_ssd(8 + mixj[2])
    T.store_h()
    T.exchange()
    P.end_phase()


QBS = [(i * 512, 512) for i in range(8)] + [(4096, 16)]
KCS = [(i * 128, 128) for i in range(32)] + [(4096, 16)]


def run_HA(P, G):
    P.begin_phase()
    nc = P.nc
    qT = P.sb("qT", [128, 4, SV], BF16)
    kT = P.sb("kT", [128, 2, SV], BF16)
    v = P.sb("v", [128, 33, 256], BF16)
    ones = P.sb("ones", [128, 128], BF16)
    pt = [P.sb("pt%d" % i, [128, 512], BF16) for i in range(3)]
    rl = [P.sb("rl%d" % i, [128, 512], F32) for i in range(2)]
    ost = [P.sb("ost%d" % i, [128, 512], BF16) for i in range(2)]
    pss = [P.ps("pss%d" % i, [128, 512]) for i in range(3)]
    po = [P.ps("po%d" % i, [128, 512]) for i in range(2)]
    pl = [P.ps("pl%d" % i, [128, 512]) for i in range(2)]
    P.op("dve", lambda: nc.vector.memset(ones[:], 1.0), w=["ones"])
    P.op("dve", lambda: nc.vector.memset(v[:, 32, :], 0.0), w=["v"])
    selstg = [P.sb("selstg%d" % i, [128, 4, 1024], BF16) for i in range(4)]
    psel = P.ps("psel", [128, 512])
    sel = Sel(P, G, selstg, lambda slot, k: [("selstg", slot, k)], lambda: (psel, "psel"))
    def ld_feat(dst3, idx, fsel, kname):
        for q in range(4):
            cc = [G.x_qk.rd(q, fsel(k)) for k in range(4)]
            sel.load(dst3[:, idx, q * 1024:(q + 1) * 1024], [c_[0][:, 0:1024] for c_ in cc], 1024,
                     r=[c_[1] for c_ in cc], w=[(kname, idx, q)])
        cc = [G.x_qk.rd(0, fsel(k)) for k in range(4)]
        sel.load(dst3[:, idx, 4096:4112], [c_[0][:, 1024:1040] for c_ in cc], 16, r=[c_[1] for c_ in cc],
                 w=[(kname, idx, 4)])
    for j in range(4):
        ld_feat(qT, j, lambda k, j=j: 4 * k + j, "q")
    for g in range(2):
        ld_feat(kT, g, lambda k, g=g: 16 + 2 * k + g, "k")
    v3d = lambda ap: ap.rearrange("p (c d) -> p c d", d=256)
    for q in range(4):
        for hf in range(2):
            c0 = 8 * q + 4 * hf
            cc = [G.x_v.rd4(q, hf, k // 2) for k in range(4)]
            sel.load(v[:, c0:c0 + 4, :].rearrange("p c d -> p (c d)"),
                     [cc[k][0][:, (k % 2) * 256:(k % 2 + 1) * 256].rearrange("(c p) d -> p c d", p=128) for k in range(4)],
                     1024, r=[c_[1] for c_ in cc] + ["v"], w=[("v", q)], view=v3d)
    cc = [G.x_v.rd(0, 8, k // 2, 16) for k in range(4)]
    sel.load(v[:, 32, :], [cc[k][0][:, (k % 2) * 256:(k % 2 + 1) * 256] for k in range(4)], 256,
             r=[c_[1] for c_ in cc] + ["v"], w=[("v", 4)], prow=(0, 16))
    QK = [("q", j, q) for j in range(4) for q in range(5)]
    sent = []
    scale = 128 ** -0.5
    its = []
    ob = 0
    for qb, (q0, qn) in enumerate(QBS):
        for g in range(2):
            for jj in range(2):
                j = 2 * g + jj
                a = ob % 2
                ob += 1
                for kc, (k0, kn) in enumerate(KCS):
                    its.append((qb, q0, qn, g, j, a, kc, k0, kn))

    def emit_s(i):
        qb, q0, qn, g, j, a, kc, k0, kn = its[i]
        s3 = i % 3
        P.op("pe", lambda: nc.tensor.matmul(
            pss[s3][0:kn, 0:qn], lhsT=kT[:, g, k0:k0 + kn], rhs=qT[:, j, q0:q0 + qn], start=True, stop=True),
            r=[("k", g, x_) for x_ in range(5)] + [("q", j, x_) for x_ in range(5)], w=[("pss", s3)])
        P.op("act", lambda: nc.scalar.activation(
            out=pt[s3][0:kn, 0:qn], in_=pss[s3][0:kn, 0:qn], func=AF.Exp, scale=scale),
            r=[("pss", s3)], w=[("pt", s3)])

    def emit_pv(i):
        qb, q0, qn, g, j, a, kc, k0, kn = its[i]
        s3 = i % 3
        P.op("pe", lambda: nc.tensor.matmul(
            po[a][:, 0:qn], lhsT=v[0:kn, kc, g * 128:(g + 1) * 128], rhs=pt[s3][0:kn, 0:qn],
            start=(kc == 0), stop=(kc == 32)), r=[("pt", s3), ("v", min(kc // 8, 4))], w=[("po", a)])
        P.op("pe", lambda: nc.tensor.matmul(
            pl[a][:, 0:qn], lhsT=ones[0:kn, :], rhs=pt[s3][0:kn, 0:qn],
            start=(kc == 0), stop=(kc == 32)), r=[("pt", s3), "ones"], w=[("pl", a)])
        if kc == 32:
            P.op("dve", lambda: nc.vector.reciprocal(out=rl[a][:, 0:qn], in_=pl[a][:, 0:qn]),
                 r=[("pl", a)], w=[("rl", a)])
            P.op("dve", lambda: nc.vector.tensor_tensor(
                out=ost[a][:, 0:qn], in0=po[a][:, 0:qn], in1=rl[a][:, 0:qn], op=ALU.mult),
                r=[("po", a), ("rl", a)], w=[("ost", a)])
            G.x_o.write(j, qb, qn, ost[a][:, 0:qn], r=[("ost", a)], dkey=("ost", a))

    for i in range(len(its) + 2):
        if i < len(its):
            emit_s(i)
        if i >= 2:
            emit_pv(i - 2)
    P.end_phase()


SW = SP_ + 6


def run_HS(P, G, jl):
    P.begin_phase()
    nc = P.nc
    cw_in, cb_in, cbrow_in = G.cw_in[jl], G.cb_in[jl], G.cbrow_in[jl]
    dsk_in, nw_in, cst_in = G.dsk_in[jl], G.nw_in[jl], G.cst_in
    dtb_in, alog_in = G.dtb_in[jl], G.alog_in[jl]
    sent = []
    selstg = [P.sb("selstg%d" % i, [128, 4, 512], BF16) for i in range(3)]
    psel = P.ps("psel", [128, 512])
    sel = Sel(P, G, selstg, lambda slot, k: [("selstg", slot, k)], lambda: (psel, "psel"))
    stg32 = [P.sb("stg32_%d" % i, [128, NCH, 16], F32) for i in range(1)]
    for i_ in range(1):
        P.op("dve", lambda i_=i_: nc.vector.memset(stg32[i_][:], 0.0), w=[("stg32", i_)])

    cst = P.sb("cst", [128, 4, 128], F32)
    Uf, Ub = cst[:, 0, :], cst[:, 1, :]
    neg4 = P.sb("neg4", [128, 2, 4, 128], BF16)
    ident = P.sb("ident", [128, 128], BF16)
    identf = P.sb("identf", [128, 128], F32)
    ones32 = P.sb("ones32", [128, 128], F32)
    onesrow = P.sb("onesrow", [1, 128], BF16)
    cw = P.sb("cw", [128, 2, 6, 7], F32)
    cb = P.sb("cb", [128, 2, 6], F32)
    cbrow32 = P.sb("cbrow32", [1, 2, 640], F32)
    cbrow = P.sb("cbrow", [1, 2, 640], BF16)
    dsk = P.sb("dsk", [128, 2, 8], F32)
    diag = P.sb("diag", [128, 6, 7, 128], BF16)
    xbc_sb = P.sb("xbc_sb", [128, 6, SW], BF16)
    x_tok = P.sb("x_tok", [128, NCH, 512], BF16)
    B_tok = P.sb("B_tok", [128, NCH, 128], BF16)
    BT = P.sb("BT", [128, SP_], BF16)
    CT = P.sb("CT", [128, SP_], BF16)
    NS = NCH * 16
    dtt = P.sb("dtt", [128, NCH, 16], F32)
    av = P.sb("av", [128, NCH, 16], F32)
    aneg = P.sb("aneg", [128, NCH, 16], F32)
    ldt = aneg
    acs = P.sb("acs", [128, NCH, 16], F32)
    tot = P.sb("tot", [128, NCH, 16], F32)
    biasD = aneg
    dte = P.sb("dte", [128, NCH, 16], F32)
    eacs = P.sb("eacs", [128, NCH, 16], F32)
    cd = P.sb("cd", [128, NCH, 16], F32)
    tmps = stg32[0]
    prevb = xbc_sb[:].rearrange("p c t -> p (c t)")[:, 0:NCH * 512].rearrange("p (c f) -> p c f", c=NCH)
    Hs = [P.sb("H%d" % i, [128, 512], F32) for i in range(2)]
    Hf_bf = P.sb("Hf_bf", [128, 512], BF16)
    aU = [P.sb("aU%d" % i, [128, 8, 128], F32) for i in range(2)]
    Dm = [[P.sb("Dm%d_%d" % (pr_, i), [128, 8, 128], BF16) for i in range(2)] for pr_ in range(2)]
    Msum = P.sb("Msum", [128, 8, 128], BF16)
    Mm = [P.sb("Mm%d" % i, [128, 8, 128], BF16) for i in range(2)]
    CBT = [P.sb("CBT%d" % i, [128, 128], BF16) for i in range(2)]
    xdte = [P.sb("xdte%d" % i, [128, 512], BF16) for i in range(2)]
    yt = [P.sb("yt%d" % i, [128, 512], F32) for i in range(3)]
    ht = P.sb("ht", [128, 512], F32)
    sz = ht
    junk = yt[2]
    ss = P.sb("ss", [128, 1], F32)
    gn = P.sb("gn", [128, 512], BF16)
    gst = [P.sb("gst%d" % i, [128, 4, 128], BF16) for i in range(2)]
    pb = [P.ps("pb%d" % i, [128, 512]) for i in range(4)]
    pacs = P.ps("pacs", [128, 1024])
    pb.append(pacs[:, 0:512])
    pb.append(pacs[:, 512:1024])
    ptr = P.ps("ptr", [128, 512], BF16)

    P.dma("sp", cst[:], cst_in, w=["cst"], key="cst")
    P.dma("sp", cw[:], cw_in, w=["cw"], key="cw")
    P.dma("sp", cb[:], cb_in, w=["cb"], key="cb")
    P.dma("sp", cbrow32[:], cbrow_in, w=["cbrow32"], key="cbrow")
    P.dma("sp", dsk[:], dsk_in, w=["dsk"], key="dsk")
    P.op("dve", lambda: nc.vector.memset(ones32[:], 1.0), w=["ones32"])
    P.op("dve", lambda: nc.vector.memset(onesrow[:], 1.0), w=["onesrow"])
    P.op("dve", lambda: nc.vector.tensor_tensor(out=identf[:], in0=Uf, in1=Ub, op=ALU.mult), r=["cst"], w=["identf"])
    P.op("dve", lambda: nc.vector.tensor_copy(out=ident[:], in_=identf[:]), r=["identf"], w=["ident"])
    for d_ in range(2):
        for hh in range(4):
            P.op("dve", lambda d_=d_, hh=hh: nc.vector.tensor_copy(out=neg4[:, d_, hh, :], in_=cst[:, 2 + d_, :]),
                 r=["cst"], w=["neg4"])
    P.op("dve", lambda: nc.vector.tensor_copy(out=cbrow[:], in_=cbrow32[:]), r=["cbrow32"], w=["cbrow"])

    def bc(ap2, n):
        return ap2.unsqueeze(2).to_broadcast([128, ap2.shape[1], n])

    def v3(ap2, k):
        return ap2.rearrange("p (k n) -> p k n", k=k)

    for g in range(2):
        if g == 1:
            P.barrier()
        for ch in range(6):
            for k in range(7):
                P.op("dve", lambda ch=ch, k=k, g=g: nc.vector.tensor_scalar(
                    out=diag[:, ch, k, :], in0=ident[:], scalar1=cw[:, g, ch, k:k + 1], scalar2=None, op0=ALU.mult),
                    r=["ident", "cw"], w=[("diag", ch)])
        P.op("dve", lambda: nc.vector.memset(xbc_sb[:, :, 0:115], 0.0), w=[("xbc", ci, 0) for ci in range(6)])
        P.op("dve", lambda: nc.vector.memset(xbc_sb[:, :, SW - 3:SW], 0.0), w=[("xbc", ci, 5) for ci in range(6)])
        for ci in range(6):
            if ci < 4:
                fsel = lambda k, ci=ci: (2 * k + g) * 4 + ci
            elif ci == 4:
                fsel = lambda k: 32 + 2 * k + g
            else:
                fsel = lambda k: 40 + 2 * k + g
            for q in range(4):
                cc = [G.x_xbc.rd(q, fsel(k)) for k in range(4)]
                for hf in range(2):
                    sel.load(xbc_sb[:, ci, 131 + q * 1024 + hf * 512:131 + q * 1024 + (hf + 1) * 512],
                             [c_[0][:, hf * 512:(hf + 1) * 512] for c_ in cc], 512, r=[c_[1] for c_ in cc],
                             w=[("xbc", ci, 1 + q)])
            cc = [G.x_xbc.rd(0, fsel(k)) for k in range(4)]
            sel.load(xbc_sb[:, ci, 115:131], [c_[0][:, 1024:1040] for c_ in cc], 16,
                     r=[c_[1] for c_ in cc] + [("xbc", ci, 0)], w=[("xbc", ci, 0)])
        XBC = [("xbc", ci, x_) for ci in range(6) for x_ in range(6)]
        P.op("dve", lambda: nc.vector.memset(dtt[:], 0.0), w=["dtt"])
        for k in range(4):
            sl32 = 0
            for d_ in range(2):
                c0 = k * 16 + d_ * 64 + g * 8
                for q in range(4):
                    for pt in range(2):
                        src, rk = G.x_dt.rd4(q, pt, 0)
                        cb_ = 1 + 8 * q + 4 * pt
                        P.dma("sp", stg32[sl32][:, cb_:cb_ + 4, d_ * 8:d_ * 8 + 8],
                              src[:, c0:c0 + 8].rearrange("(c p) k -> p c k", p=128),
                              r=[rk], w=[("stg32", sl32)], key=("stg32", sl32, d_, q, pt))
                src, rk = G.x_dt.rd(0, 8, 0, 16)
                P.dma("sp", stg32[sl32][112:128, 0, d_ * 8:d_ * 8 + 8], src[:, c0:c0 + 8],
                      r=[rk], w=[("stg32", sl32)], key=("stg32", sl32, d_, 4, 0))
            P.op("dve", lambda k=k, sl32=sl32: nc.vector.scalar_tensor_tensor(
                out=dtt[:], in0=stg32[sl32][:], scalar=sel.sel[:, k:k + 1], in1=dtt[:], op0=ALU.mult, op1=ALU.add),
                r=[("stg32", sl32), "sel", "dtt"], w=["dtt"])
        P.dma("sp", tmps[:], dtb_in[g], w=[("stg32", 0)], key=("stg32", 0))
        P.dma("sp", aneg[:], alog_in[g], w=["aneg"], key="aneg")
        P.op("dve", lambda: nc.vector.tensor_tensor(out=dtt[:], in0=dtt[:], in1=tmps[:], op=ALU.add),
             r=["dtt", ("stg32", 0)], w=["dtt"])
        P.op("act", lambda: nc.scalar.activation(out=dtt[:], in_=dtt[:], func=AF.Exp), r=["dtt"], w=["dtt"])
        P.op("act", lambda: nc.scalar.activation(out=dtt[:], in_=dtt[:], func=AF.Ln, bias=1.0), r=["dtt"], w=["dtt"])
        P.op("dve", lambda: nc.vector.memset(dtt[0:112, 0, :], 0.0), r=["dtt"], w=["dtt"])
        P.op("act", lambda: nc.scalar.activation(out=aneg[:], in_=aneg[:], func=AF.Exp), r=["aneg"], w=["aneg"])
        P.op("dve", lambda: nc.vector.scalar_tensor_tensor(out=av[:], in0=aneg[:], scalar=-1.0, in1=dtt[:],
                                                           op0=ALU.mult, op1=ALU.mult),
             r=["aneg", "dtt"], w=["av"])
        P.op("dve", lambda: nc.vector.tensor_scalar_max(out=ldt[:], in0=dtt[:], scalar1=1e-30), r=["dtt"], w=["aneg"])
        P.op("act", lambda: nc.scalar.activation(out=ldt[:], in_=ldt[:], func=AF.Ln), r=["aneg"], w=["aneg"])
        for c in range(NCH):
            P.op("pe", lambda c=c: nc.tensor.matmul(pacs[:, c * 16:c * 16 + 8], lhsT=Uf, rhs=av[:, c, 0:8],
                                                   start=True, stop=True), r=["av", "cst"], w=[("pb", 4), ("pb", 5)])
            P.op("pe", lambda c=c: nc.tensor.matmul(pacs[:, c * 16 + 8:c * 16 + 16], lhsT=Ub, rhs=av[:, c, 8:16],
                                                   start=True, stop=True), r=["av", "cst"], w=[("pb", 4), ("pb", 5)])
        P.op("act", lambda: nc.scalar.copy(out=acs[:].rearrange("p c k -> p (c k)"), in_=pacs[:, 0:NS]),
             r=[("pb", 4), ("pb", 5)], w=["acs"])
        for c in range(NCH):
            P.op("pe", lambda c=c: nc.tensor.matmul(pacs[:, c * 16:c * 16 + 16], lhsT=ones32[:], rhs=av[:, c, :],
                                                   start=True, stop=True), r=["av", "ones32", "acs"], w=[("pb", 4), ("pb", 5)])
        P.op("act", lambda: nc.scalar.copy(out=tot[:].rearrange("p c k -> p (c k)"), in_=pacs[:, 0:NS]),
             r=[("pb", 4), ("pb", 5)], w=["tot"])
        P.op("dve", lambda: nc.vector.tensor_tensor(out=biasD[:], in0=ldt[:], in1=acs[:], op=ALU.subtract),
             r=["aneg", "acs"], w=["aneg"])
        P.op("act", lambda: nc.scalar.activation(out=eacs[:], in_=acs[:], func=AF.Exp), r=["acs"], w=["eacs"])
        P.op("act", lambda: nc.scalar.activation(out=cd[:], in_=tot[:], func=AF.Exp), r=["tot"], w=["cd"])
        P.op("dve", lambda: nc.vector.tensor_tensor(out=dte[:], in0=tot[:], in1=acs[:], op=ALU.subtract),
             r=["tot", "acs"], w=["dte"])
        P.op("act", lambda: nc.scalar.activation(out=dte[:], in_=dte[:], func=AF.Exp), r=["dte"], w=["dte"])
        P.op("dve", lambda: nc.vector.tensor_tensor(out=dte[:], in0=dte[:], in1=dtt[:], op=ALU.mult),
             r=["dte", "dtt"], w=["dte"])
        for c in range(NCH):
            w_ = 0
            cb0 = c * 128
            p0 = c % 2

            def conv_tok(c=c, cb0=cb0, p0=p0, g=g):
                ins = None
                for xc in range(4):
                    for k in range(7):
                        nc.tensor.matmul(pb[p0][:, xc * 128:(xc + 1) * 128], lhsT=xbc_sb[:, xc, cb0 + k:cb0 + k + 128],
                                         rhs=diag[:, xc, k, :], start=(k == 0), stop=False)
                    ins = nc.tensor.matmul(pb[p0][:, xc * 128:(xc + 1) * 128], lhsT=onesrow[0:1, :],
                                           rhs=cbrow[0:1, g, xc * 128:(xc + 1) * 128], start=False, stop=True)
                return ins
            P.op("pe", conv_tok, r=XBC + ["onesrow", "cbrow"] + [("diag", ch) for ch in range(4)], w=[("pb", p0)])
            P.op("act", lambda c=c, p0=p0: nc.scalar.activation(out=x_tok[:, c, :], in_=pb[p0][:, :], func=AF.Silu),
                 r=[("pb", p0)], w=[("x_tok", c)])
            p2 = 2 + c % 2

            def conv_b(c=c, cb0=cb0, p2=p2, g=g):
                for k in range(7):
                    nc.tensor.matmul(pb[p2][:, 0:128], lhsT=xbc_sb[:, 4, cb0 + k:cb0 + k + 128], rhs=diag[:, 4, k, :],
                                     start=(k == 0), stop=False)
                nc.tensor.matmul(pb[p2][:, 0:128], lhsT=onesrow[0:1, :], rhs=cbrow[0:1, g, 512:640],
                                 start=False, stop=True)
                for k in range(7):
                    nc.tensor.matmul(pb[p2][:, 128:256], lhsT=diag[:, 4, k, :], rhs=xbc_sb[:, 4, cb0 + k:cb0 + k + 128],
                                     start=(k == 0), stop=(k == 6))
                ins = None
                for k in range(7):
                    ins = nc.tensor.matmul(pb[p2][:, 256:384], lhsT=diag[:, 5, k, :], rhs=xbc_sb[:, 5, cb0 + k:cb0 + k + 128],
                                           start=(k == 0), stop=(k == 6))
                return ins
            P.op("pe", conv_b, r=XBC + ["onesrow", "cbrow", ("diag", 4), ("diag", 5)], w=[("pb", p2)])
            P.op("act", lambda c=c, p2=p2: nc.scalar.activation(out=B_tok[:, c, :], in_=pb[p2][:, 0:128], func=AF.Silu),
                 r=[("pb", p2)], w=[("B_tok", c)])
            P.op("act", lambda c=c, p2=p2, g=g: nc.scalar.activation(
                out=BT[:, c * 128:(c + 1) * 128], in_=pb[p2][:, 128:256], func=AF.Silu, bias=cb[:, g, 4:5]),
                r=[("pb", p2), "cb"], w=[("BT", c)])
            P.op("act", lambda c=c, p2=p2, g=g: nc.scalar.activation(
                out=CT[:, c * 128:(c + 1) * 128], in_=pb[p2][:, 256:384], func=AF.Silu, bias=cb[:, g, 5:6]),
                r=[("pb", p2), "cb"], w=[("CT", c)])
            if c == 0:
                P.op("dve", lambda: nc.vector.memset(x_tok[0:112, 0, :], 0.0), r=[("x_tok", 0)], w=[("x_tok", 0)])
                P.op("dve", lambda: nc.vector.memset(B_tok[0:112, 0, :], 0.0), r=[("B_tok", 0)], w=[("B_tok", 0)])
                P.op("dve", lambda: nc.vector.memset(BT[:, 0:112], 0.0), r=[("BT", 0)], w=[("BT", 0)])
                P.op("dve", lambda: nc.vector.memset(CT[:, 0:112], 0.0), r=[("CT", 0)], w=[("CT", 0)])
        P.op("dve", lambda: nc.vector.memset(Hs[1][:], 0.0), w=[("H", 1)])
        P.op("dve", lambda: nc.vector.memset(Hs[0][:], 0.0), w=[("H", 0)])
        P.op("dve", lambda: nc.vector.memset(Hf_bf[:], 0.0), w=["Hf_bf"])
        for c in range(NCH - 1, -1, -1):
            xs = c % 2
            P.op("act", lambda c=c: nc.scalar.copy(out=prevb[:, c, :], in_=Hs[1][:]), r=[("H", 1)],
                 w=[("prevb", c)] + (XBC if c == NCH - 1 else []))
            if c == 0:
                break
            P.op("dve", lambda c=c, xs=xs: nc.vector.tensor_tensor(
                out=v3(xdte[xs][:], 8), in0=v3(x_tok[:, c, :], 8), in1=bc(dte[:, c, 8:16], 64), op=ALU.mult),
                r=[("x_tok", c), "dte"], w=[("xdte", xs)])
            pS = 4 + c % 2
            P.op("pe", lambda c=c, xs=xs, pS=pS: nc.tensor.matmul(pb[pS][:, :], lhsT=B_tok[:, c, :], rhs=xdte[xs][:],
                                                                  start=True, stop=True),
                 r=[("B_tok", c), ("xdte", xs)], w=[("pb", pS)])
            P.op("dve", lambda c=c: nc.vector.tensor_tensor(
                out=v3(ht[:], 8), in0=v3(Hs[1][:], 8), in1=bc(cd[:, c, 8:16], 64), op=ALU.mult),
                r=[("H", 1), "cd"], w=["ht"])
            P.op("dve", lambda pS=pS: nc.vector.tensor_tensor(out=Hs[1][:], in0=ht[:], in1=pb[pS][:, :], op=ALU.add),
                 r=["ht", ("pb", pS)], w=[("H", 1)])
        def front(c, g=g):
            cs_ = slice(c * 128, (c + 1) * 128)
            pr = c % 2
            for d_ in range(2):
                U = Uf if d_ == 0 else Ub
                P.op("dve", lambda c=c, d_=d_, U=U: nc.vector.tensor_tensor(
                    out=aU[d_][:], in0=bc(av[:, c, d_ * 8:d_ * 8 + 8], 128),
                    in1=U.unsqueeze(1).to_broadcast([128, 8, 128]), op=ALU.mult),
                    r=["av", "cst"], w=[("aU", d_)])
                for hb in range(2):
                    pD = hb

                    def segmm(c=c, d_=d_, hb=hb, pD=pD):
                        nc.tensor.matmul(pb[pD][:, :], lhsT=ones32[:], rhs=aU[d_][:, hb * 4:hb * 4 + 4, :],
                                         start=True, stop=False)
                        return nc.tensor.matmul(pb[pD][:, :], lhsT=ident[:], rhs=neg4[:, d_, :, :],
                                                start=False, stop=True)
                    P.op("pe", segmm, r=[("aU", d_), "ones32", "ident", "neg4"], w=[("pb", pD)])
                    for h4 in range(4):
                        hh = hb * 4 + h4
                        P.op("act", lambda c=c, d_=d_, pD=pD, h4=h4, hh=hh, pr=pr: nc.scalar.activation(
                            out=Dm[pr][d_][:, hh, :], in_=pb[pD][:, h4 * 128:(h4 + 1) * 128], func=AF.Exp,
                            bias=biasD[:, c, d_ * 8 + hh:d_ * 8 + hh + 1]),
                            r=[("pb", pD), "aneg"], w=[("Dm", pr, d_, hh)])
            P.op("pe", lambda cs_=cs_: nc.tensor.matmul(pb[2][:, 0:128], lhsT=BT[:, cs_], rhs=CT[:, cs_],
                                                        start=True, stop=True),
                 r=[("BT", c), ("CT", c)], w=[("pb", 2)])
            P.op("act", lambda pr=pr: nc.scalar.copy(out=CBT[pr][:], in_=pb[2][:, 0:128]), r=[("pb", 2)], w=[("CBT", pr)])
            P.op("dve", lambda pr=pr: nc.vector.tensor_tensor(out=Msum[:], in0=Dm[pr][0][:], in1=Dm[pr][1][:], op=ALU.add),
                 r=[("Dm", pr, d_, hh) for d_ in range(2) for hh in range(8)], w=["Msum"])
            P.op("dve", lambda pr=pr: nc.vector.tensor_tensor(
                out=Mm[pr][:], in0=Msum[:], in1=CBT[pr][:].unsqueeze(1).to_broadcast([128, 8, 128]), op=ALU.mult),
                r=["Msum", ("CBT", pr)], w=[("Mm", pr)])

        def back(c, g=g):
            cs_ = slice(c * 128, (c + 1) * 128)
            pr = c % 2

            def ymm(c=c, pr=pr):
                ins = None
                for hh in range(8):
                    ins = nc.tensor.matmul(pb[3][:, hh * 64:(hh + 1) * 64], lhsT=Mm[pr][:, hh, :],
                                           rhs=x_tok[:, c, hh * 64:(hh + 1) * 64], start=True, stop=True)
                return ins
            P.op("pe", ymm, r=[("Mm", pr), ("x_tok", c)], w=[("pb", 3)])
            P.op("dve", lambda c=c, g=g: nc.vector.tensor_tensor(
                out=v3(yt[0][:], 8), in0=v3(x_tok[:, c, :], 8), in1=bc(dsk[:, g, :], 64), op=ALU.mult),
                r=[("x_tok", c), "dsk"], w=[("yt", 0)])
            P.op("dve", lambda: nc.vector.tensor_tensor(out=yt[0][:], in0=yt[0][:], in1=pb[3][:, :], op=ALU.add),
                 r=[("yt", 0), ("pb", 3)], w=[("yt", 0)])
            P.op("pe", lambda cs_=cs_: nc.tensor.matmul(pb[4][:, :], lhsT=CT[:, cs_], rhs=Hf_bf[:], start=True, stop=True),
                 r=[("CT", c), "Hf_bf"], w=[("pb", 4)])
            P.op("dve", lambda c=c: nc.vector.tensor_tensor(
                out=v3(yt[1][:], 8), in0=v3(pb[4][:, :], 8), in1=bc(eacs[:, c, 0:8], 64), op=ALU.mult),
                r=[("pb", 4), "eacs"], w=[("yt", 1)])
            P.op("pe", lambda c=c, cs_=cs_: nc.tensor.matmul(pb[5][:, :], lhsT=CT[:, cs_], rhs=prevb[:, c, :],
                                                             start=True, stop=True),
                 r=[("CT", c), ("prevb", c)], w=[("pb", 5)])
            P.op("dve", lambda c=c: nc.vector.tensor_tensor(
                out=v3(yt[2][:], 8), in0=v3(pb[5][:, :], 8), in1=bc(eacs[:, c, 8:16], 64), op=ALU.mult),
                r=[("pb", 5), "eacs"], w=[("yt", 2)])
            P.op("dve", lambda: nc.vector.tensor_tensor(out=yt[0][:], in0=yt[0][:], in1=yt[1][:], op=ALU.add),
                 r=[("yt", 0), ("yt", 1)], w=[("yt", 0)])
            P.op("dve", lambda: nc.vector.tensor_tensor(out=yt[0][:], in0=yt[0][:], in1=yt[2][:], op=ALU.add),
                 r=[("yt", 0), ("yt", 2)], w=[("yt", 0)])
            if c < NCH - 1:
                xs = c % 2
                P.op("dve", lambda c=c, xs=xs: nc.vector.tensor_tensor(
                    out=v3(xdte[xs][:], 8), in0=v3(x_tok[:, c, :], 8), in1=bc(dte[:, c, 0:8], 64), op=ALU.mult),
                    r=[("x_tok", c), "dte"], w=[("xdte", xs)])
                P.op("pe", lambda c=c, xs=xs: nc.tensor.matmul(pb[2][:, :], lhsT=B_tok[:, c, :], rhs=xdte[xs][:],
                                                               start=True, stop=True),
                     r=[("B_tok", c), ("xdte", xs)], w=[("pb", 2)])
                P.op("dve", lambda c=c: nc.vector.tensor_tensor(
                    out=v3(ht[:], 8), in0=v3(Hs[0][:], 8), in1=bc(cd[:, c, 0:8], 64), op=ALU.mult),
                    r=[("H", 0), "cd"], w=["ht"])
                P.op("dve", lambda: nc.vector.tensor_tensor(out=Hs[0][:], in0=ht[:], in1=pb[2][:, :], op=ALU.add),
                     r=["ht", ("pb", 2)], w=[("H", 0)])
                P.op("act", lambda: nc.scalar.copy(out=Hf_bf[:], in_=Hs[0][:]), r=[("H", 0)], w=["Hf_bf"])
            P.op("act", lambda: nc.scalar.copy(out=gn[:], in_=yt[0][:]), r=[("yt", 0)], w=["gn"])

            def trans():
                ins = None
                for i in range(4):
                    ins = nc.tensor.transpose(ptr[:, i * 128:(i + 1) * 128], gn[:, i * 128:(i + 1) * 128], ident[:])
                return ins
            P.op("pe", trans, r=["gn", "ident"], w=["ptr"])
            gs = c % 2
            P.op("act", lambda gs=gs: nc.scalar.copy(out=gst[gs][:].rearrange("p i t -> p (i t)"), in_=ptr[:, :]),
                 r=["ptr"], w=[("gst", gs)])
            G.x_g.write(g, c, gst[gs][:], r=[("gst", gs)], dkey=("gst", gs))

        for c in range(NCH + 1):
            if c < NCH:
                front(c)
            if c >= 1:
                back(c - 1)
    P.end_phase()


class _G:
    pass


def build_program():
    P = Prog()
    G = _G()
    ext = lambda n, shp, dt=F32: P.dram(n, shp, dt, "ExternalInput")
    G.ffn_w_in = ext("ffn_w_in", [4, 2, D, 2 * DFF])
    G.ffn_w_out = ext("ffn_w_out", [4, 2, DFF, D])
    G.ssd_in_proj = ext("ssd_in_proj", [2, D, 10368])
    G.ssd_out_proj = ext("ssd_out_proj", [2, 4096, D])
    G.attn_w_qkv = ext("attn_w_qkv", [2, D, 4096])
    G.attn_w_o = ext("attn_w_o", [2, 2048, D])
    G.rot = ext("rot", [128, 128])
    G.cst_in = ext("cst_in", [128, 4, 128])
    G.ident_in = ext("ident_in", [128, 128])
    G.sel_in = ext("sel_in", [128, 4])
    G.nwT_in = ext("nwT_in", [2, 128, 32])
    G.h0 = ext("h0", [128, KC, TT])
    G.gam_in = ext("gam_in", [128, 12, KC])
    G.cossin = ext("cossin", [128, 2, TT])
    G.qkg = ext("qkg", [128, 2, 2])
    G.cw_in = ext("cw_in", [2, 128, 2, 6, 7])
    G.cb_in = ext("cb_in", [2, 128, 2, 6])
    G.cbrow_in = ext("cbrow_in", [2, 1, 2, 640])
    G.dsk_in = ext("dsk_in", [2, 128, 2, 8])
    G.nw_in = ext("nw_in", [2, 128, 2, 512])
    G.dtb_in = ext("dtb_in", [2, 2, 128, NCH, 16])
    G.alog_in = ext("alog_in", [2, 2, 128, NCH, 16])
    G.h_out = P.dram("h_out", [128, KC, TT], F32, "ExternalOutput")
    G.h_dram = P.idram("h_dram", [128, KC, TT], F32)
    G.x_xbc = FeatX(P, "xbc", 48, order=XBC_ORDER)
    G.zpark = P.idram("zpark", [32, 128, TT], BF16)
    G.x_dt = TokX(P, "dt", 1, 128, F32)
    G.x_g = GX(P)
    G.x_qk = FeatX(P, "qk", 24)
    G.x_v = TokX(P, "v", 2, 512, BF16)
    G.x_o = OX(P)

    run_T(P, G, None, [(0, 0)], "ssd", G.h0, G.h_dram, False, (None, 0, 0))
    run_HS(P, G, 0)
    run_T(P, G, "ssd", [(0, 1), (1, 0)], "attn", G.h_dram, G.h_dram, False, (0, 0, 1))
    run_HA(P, G)
    run_T(P, G, "attn", [(1, 1), (2, 0)], "ssd", G.h_dram, G.h_dram, False, (0, 1, 2))
    run_HS(P, G, 1)
    run_T(P, G, "ssd", [(2, 1), (3, 0)], "attn", G.h_dram, G.h_dram, False, (1, 1, 3))
    run_HA(P, G)
    run_T(P, G, "attn", [(3, 1)], None, G.h_dram, G.h_out, True, (1, None, None))
    P.emit()
    return P


def rope_tables(q):
    inv = (10000.0 ** (-np.arange(0, 64, 2, dtype=np.float32) / 64.0)).astype(np.float32)
    t = q * 1024 + np.arange(1024)
    row = np.concatenate([t // 64, np.full((16,), -1)]).astype(np.float32)
    col = np.concatenate([t % 64, np.arange(16)]).astype(np.float32)
    ang = np.stack([row, col], -1)[..., None] * inv
    c, s = np.cos(ang).astype(np.float32), np.sin(ang).astype(np.float32)
    out = np.zeros((128, 2, TT), np.float32)
    for d in range(128):
        out[d, 0] = c[:, d // 64, d % 32]
        out[d, 1] = s[:, d // 64, d % 32]
    return out


def rot_matrix():
    r = np.zeros((128, 128), np.float32)
    for m in range(128):
        if m % 64 < 32:
            r[m + 32, m] = -1.0
        else:
            r[m - 32, m] = 1.0
    return r


def col_layout(v):
    return np.ascontiguousarray(np.asarray(v, np.float32).reshape(-1, 128).T)


def ssd_consts():
    j = np.arange(128)
    Uf = (j[:, None] <= j[None, :]).astype(np.float32)
    Ub = (j[:, None] >= j[None, :]).astype(np.float32)
    NEGf = np.where(j[:, None] > j[None, :], -30000.0, 0.0).astype(np.float32)
    NEGb = np.where(j[:, None] < j[None, :], -30000.0, 0.0).astype(np.float32)
    return np.ascontiguousarray(np.stack([Uf, Ub, NEGf, NEGb], axis=1))


def ssd_params(hq, conv_w, conv_b, dt_bias, a_log, d_skip, norm_w):
    nl = conv_w.shape[0]
    cw = np.zeros((nl, 128, 2, 6, 7), np.float32)
    cb = np.zeros((nl, 128, 2, 6), np.float32)
    cbrow = np.zeros((nl, 1, 2, 640), np.float32)
    dsk = np.zeros((nl, 128, 2, 8), np.float32)
    nw = np.zeros((nl, 128, 2, 512), np.float32)
    dtb = np.zeros((nl, 2, 128, NCH, 16), np.float32)
    alog = np.zeros((nl, 2, 128, NCH, 16), np.float32)
    for j in range(nl):
        for gi in range(2):
            Gg = 2 * hq + gi
            chans = [Gg * 512 + i * 128 for i in range(4)] + [4096 + Gg * 128, 5120 + Gg * 128]
            for ci, c0 in enumerate(chans):
                cw[j, :, gi, ci, :] = conv_w[j][:, c0:c0 + 128].T
                cb[j, :, gi, ci] = conv_b[j][c0:c0 + 128]
            cbrow[j, 0, gi, 0:512] = conv_b[j][Gg * 512:(Gg + 1) * 512]
            cbrow[j, 0, gi, 512:640] = conv_b[j][4096 + Gg * 128:4096 + (Gg + 1) * 128]
            dtb[j, gi] = np.concatenate([dt_bias[j, 0, Gg * 8:Gg * 8 + 8], dt_bias[j, 1, Gg * 8:Gg * 8 + 8]])[None, None, :]
            alog[j, gi] = np.concatenate([a_log[j, 0, Gg * 8:Gg * 8 + 8], a_log[j, 1, Gg * 8:Gg * 8 + 8]])[None, None, :]
            dsk[j, :, gi, :] = d_skip[j][Gg * 8:Gg * 8 + 8][None, :]
            nw[j, :, gi, :] = norm_w[j][Gg * 512:(Gg + 1) * 512][None, :]
    return {"cw_in": cw, "cb_in": cb, "cbrow_in": cbrow, "dsk_in": dsk, "nw_in": nw, "dtb_in": dtb, "alog_in": alog}


_PROG = []


def kernel(x, meta_tokens, ffn_norm, ffn_w_in, ffn_w_out, mix_norm, ssd_in_proj, ssd_conv_w, ssd_conv_b,
           ssd_dt_bias, ssd_A_log, ssd_D, ssd_norm, ssd_out_proj, attn_w_qkv, attn_q_norm, attn_k_norm, attn_w_o):
    f32 = lambda a: np.ascontiguousarray(np.asarray(a, dtype=np.float32))
    x, meta_tokens = f32(x), f32(meta_tokens)
    ffn_norm, mix_norm = f32(ffn_norm), f32(mix_norm)
    shared = {"ffn_w_in": f32(ffn_w_in), "ffn_w_out": f32(ffn_w_out), "ssd_in_proj": f32(ssd_in_proj),
              "ssd_out_proj": f32(ssd_out_proj), "attn_w_qkv": f32(attn_w_qkv), "attn_w_o": f32(attn_w_o),
              "rot": rot_matrix(), "cst_in": ssd_consts(), "ident_in": np.eye(128, dtype=np.float32)}
    ssd_conv_w, ssd_conv_b, ssd_dt_bias = f32(ssd_conv_w), f32(ssd_conv_b), f32(ssd_dt_bias)
    ssd_A_log, ssd_D, ssd_norm = f32(ssd_A_log), f32(ssd_D), f32(ssd_norm)
    attn_q_norm, attn_k_norm = f32(attn_q_norm), f32(attn_k_norm)
    gam = np.zeros((128, 12, KC), np.float32)
    for l in range(4):
        for sl in range(2):
            gam[:, 2 * l + sl, :] = col_layout(ffn_norm[l, sl])
        gam[:, 8 + l, :] = col_layout(mix_norm[l])
    qkg = np.zeros((128, 2, 2), np.float32)
    for j in range(2):
        qkg[:, j, 0] = attn_q_norm[j]
        qkg[:, j, 1] = attn_k_norm[j]
    shared["gam_in"] = gam
    shared["nwT_in"] = np.ascontiguousarray(np.stack([col_layout(ssd_norm[j]) for j in range(2)], axis=0))
    shared["qkg"] = qkg
    cs = [rope_tables(q) for q in range(4)]
    sp = [ssd_params(hq, ssd_conv_w, ssd_conv_b, ssd_dt_bias, ssd_A_log, ssd_D, ssd_norm) for hq in range(4)]
    maps = []
    for c in range(8):
        b, q = c // 4, c % 4
        H = np.concatenate([x[b, q * 1024:(q + 1) * 1024], meta_tokens], axis=0)
        m = dict(shared)
        m["h0"] = np.ascontiguousarray(H.T.reshape(KC, 128, TT).transpose(1, 0, 2))
        m["cossin"] = cs[q]
        onehot = np.zeros((128, 4), np.float32)
        onehot[:, q] = 1.0
        m["sel_in"] = onehot
        m.update(sp[q])
        maps.append(m)
    if not _PROG:
        _PROG.append(build_program())
    res = run_bass_kernel_spmd(_PROG[0].nc, maps, core_ids=list(range(8))).results
    out = np.zeros((2, SEQ, D), np.float32)
    for c in range(8):
        b, q = c // 4, c % 4
        ho = np.asarray(res[c]["h_out"])
        out[b, q * 1024:(q + 1) * 1024] = ho.transpose(1, 0, 2).reshape(D, TT).T[0:1024]
    return out
```
